# Optimizing a Trainium2 kernel written in Bass

```python
import jax
import jax.numpy as jnp
from jax import lax
import numpy as np


D_MODEL = 1024
BATCH = 16
SEQ = 256
DEPTH = 2
DEC_BATCH = 8
DEC_SEQ = 1024
PAST_LEN = 256

GRID_W = 64
N_MIXERS = 2
N_RWKV_LAYERS = (DEPTH + 1) // 2
N_CONV_LAYERS = DEPTH // 2
RWKV_HEAD = 64
RWKV_WIDTH = D_MODEL
RWKV_HEADS = RWKV_WIDTH // RWKV_HEAD
DECAY_RANK = 64
ICLR_RANK = 64
N_SHIFT_MIX = 6
CONV_WIDTH = D_MODEL
CONV_K = 31
RMS_EPS = 1e-6
GN_EPS = 64e-5
LN_EPS = 1e-5

kernel_name = "hybrid_rwkv7_conformer_diffusion_step"


def rms_norm(x, g):
    xf = x.astype(jnp.float32)
    y = xf * lax.rsqrt(jnp.mean(xf * xf, axis=-1, keepdims=True) + RMS_EPS)
    return (y * g.astype(jnp.float32)).astype(x.dtype)


def modulation(cond, ada_w, ada_b):
    m = jax.nn.silu(cond) @ ada_w + ada_b
    return jnp.split(m, 3, axis=-1)


def centred_shift_delta(h):
    prev = jnp.pad(h[:, :-1], ((0, 0), (1, 0), (0, 0)))
    nxt = jnp.pad(h[:, 1:], ((0, 0), (0, 1), (0, 0)))
    return 0.5 * (prev + nxt) - h


def _orient(t):
    return jnp.stack([t[0], jnp.flip(t[1], axis=1)])


def _both(t):
    return jnp.stack([t, jnp.flip(t, axis=1)])


def _wkv7_step(S, inp):
    w, r, kk, k, b, v = inp
    sa = jnp.einsum('dbhij,dbhj->dbhi', S, -kk)
    S = S * w[..., None, :] + sa[..., :, None] * b[..., None, :] + v[..., :, None] * k[..., None, :]
    o = jnp.einsum('dbhij,dbhj->dbhi', S, r)
    return S, o


def rwkv7_bidir(h, s0, mu, w_in, w0, w1, w2, a0, a1, a2, k_k, k_a, r_k, lnx_g, lnx_b, w_out):
    f32 = jnp.float32
    B, T, _ = h.shape
    H, N = RWKV_HEADS, RWKV_HEAD
    xs = h[None] + centred_shift_delta(h)[None] * mu[:, None, None, :]
    r, k, v, g = jnp.einsum('nbtc,nce->nbte', xs[:4], w_in)
    x_w, x_a = xs[4], xs[5]
    w_lora = jnp.einsum('dbtr,dre->dbte', jnp.tanh(jnp.einsum('btc,dcr->dbtr', x_w, w1)), w2)
    w_log = -jax.nn.softplus(-(w0[:, None, None, :] + w_lora).astype(f32)) - 0.5
    decay = jnp.exp(-jnp.exp(w_log))
    a_lora = jnp.einsum('dbtr,dre->dbte', jnp.einsum('btc,dcr->dbtr', x_a, a1), a2)
    a = jax.nn.sigmoid((a0[:, None, None, :] + a_lora).astype(f32))
    rf, kf, vf = r.astype(f32), k.astype(f32), v.astype(f32)
    kk = (kf * k_k.astype(f32)).reshape(B, T, H, N)
    kk = kk / jnp.maximum(jnp.sqrt(jnp.sum(kk * kk, axis=-1, keepdims=True)), 1e-12)
    kk = kk.reshape(B, T, H * N)
    k_dir = kf[None] * (1.0 + (a - 1.0) * k_a.astype(f32))
    b_dir = kk[None] * a
    seqs = (_orient(decay), _both(rf), _both(kk), _orient(k_dir), _orient(b_dir), _both(vf))
    seqs = tuple(jnp.moveaxis(t.reshape(2, B, T, H, N), 2, 0) for t in seqs)
    s_fin, o = lax.scan(_wkv7_step, s0.astype(f32), seqs)
    o = _orient(jnp.moveaxis(o, 0, 2)).sum(axis=0)
    mean = jnp.mean(o, axis=-1, keepdims=True)
    var = jnp.mean(jnp.square(o - mean), axis=-1, keepdims=True)
    o = ((o - mean) * lax.rsqrt(var + GN_EPS)).reshape(B, T, H * N)
    o = o * lnx_g.astype(f32) + lnx_b.astype(f32)
    k_bonus = 0.5 * (k_dir[0] + k_dir[1])
    bonus = jnp.sum((rf * k_bonus).reshape(B, T, H, N) * r_k.astype(f32), axis=-1, keepdims=True) * vf.reshape(B, T, H, N)
    out = (o + bonus.reshape(B, T, H * N)) * jax.nn.silu(g.astype(f32))
    return out.astype(h.dtype) @ w_out, s_fin


def conformer_conv(h, rows, w_in, dw_w, dw_b, ln_g, ln_b, w_out):
    f32 = jnp.float32
    B, T, _ = h.shape
    C = CONV_WIDTH
    val, glu_gate, s_gate = jnp.split(h @ w_in, 3, axis=-1)
    z = (val * jax.nn.sigmoid(glu_gate)).reshape(B * rows, T // rows, C)
    z = lax.conv_general_dilated(z, dw_w[:, None, :].astype(z.dtype), window_strides=(1,),
                                 padding=[(CONV_K // 2, CONV_K // 2)],
                                 dimension_numbers=('NWC', 'WIO', 'NWC'), feature_group_count=C)
    z = z.reshape(B, T, C).astype(f32) + dw_b.astype(f32)
    mean = jnp.mean(z, axis=-1, keepdims=True)
    var = jnp.mean(jnp.square(z - mean), axis=-1, keepdims=True)
    z = (z - mean) * lax.rsqrt(var + LN_EPS) * ln_g.astype(f32) + ln_b.astype(f32)
    z = jax.nn.silu(z) * jax.nn.silu(s_gate.astype(f32))
    return z.astype(h.dtype) @ w_out


def setup_inputs(seed: int = 0) -> dict:
    key = jax.random.key(seed)
    keys = jax.random.split(key, 40)
    ks = [keys[i] for i in range(40)]

    def nrm(shape, scale):
        return jax.random.normal(ks.pop(), shape, jnp.float32) * scale

    def unif(shape, lo, hi):
        return jax.random.uniform(ks.pop(), shape, jnp.float32, lo, hi)

    D, DI, C, H, N = D_MODEL, RWKV_WIDTH, CONV_WIDTH, RWKV_HEADS, RWKV_HEAD
    R, Q = N_RWKV_LAYERS, N_CONV_LAYERS
    return {
        'x_prompt': nrm((BATCH, SEQ, D), 1.0),
        'x_sample': nrm((DEC_BATCH, DEC_SEQ, D), 1.0),
        'state_rwkv': nrm((DEC_BATCH, R, 2, H, N, N), 0.3),
        'c': nrm((DEC_BATCH, D), 1.0),
        'c_ctx': nrm((D,), 1.0),
        'norm_pre': 1.0 + nrm((DEPTH, D), 0.05),
        'norm_post': 1.0 + nrm((DEPTH, D), 0.05),
        'ada_w': nrm((DEPTH, D, 3 * D), 0.5 * D ** -0.5),
        'ada_b': nrm((DEPTH, 3 * D), 0.02),
        'rw_mu': unif((R, N_SHIFT_MIX, D), 0.0, 1.0),
        'rw_w_in': nrm((R, 4, D, DI), D ** -0.5),
        'rw_w0': unif((R, 2, DI), -4.0, 0.5),
        'rw_w1': nrm((R, 2, D, DECAY_RANK), D ** -0.5),
        'rw_w2': nrm((R, 2, DECAY_RANK, DI), 0.5 * DECAY_RANK ** -0.5),
        'rw_a0': nrm((R, 2, DI), 0.5),
        'rw_a1': nrm((R, 2, D, ICLR_RANK), D ** -0.5),
        'rw_a2': nrm((R, 2, ICLR_RANK, DI), 0.5 * ICLR_RANK ** -0.5),
        'rw_k_k': 0.85 + nrm((R, DI), 0.05),
        'rw_k_a': 1.0 + nrm((R, DI), 0.05),
        'rw_r_k': nrm((R, H, N), 0.1),
        'rw_lnx_g': 1.0 + nrm((R, DI), 0.05),
        'rw_lnx_b': nrm((R, DI), 0.02),
        'rw_w_out': nrm((R, DI, D), DI ** -0.5),
        'cv_w_in': nrm((Q, D, 3 * C), D ** -0.5),
        'cv_dw_w': nrm((Q, CONV_K, C), CONV_K ** -0.5),
        'cv_dw_b': nrm((Q, C), 0.02),
        'cv_ln_g': 1.0 + nrm((Q, C), 0.05),
        'cv_ln_b': nrm((Q, C), 0.02),
        'cv_w_out': nrm((Q, C, D), C ** -0.5),
    }


def reference(x_prompt, x_sample, state_rwkv, c, c_ctx, norm_pre, norm_post, ada_w, ada_b,
              rw_mu, rw_w_in, rw_w0, rw_w1, rw_w2, rw_a0, rw_a1, rw_a2, rw_k_k, rw_k_a, rw_r_k,
              rw_lnx_g, rw_lnx_b, rw_w_out,
              cv_w_in, cv_dw_w, cv_dw_b, cv_ln_g, cv_ln_b, cv_w_out):
    b_ctx = x_prompt.shape[0]
    rows = x_sample.shape[1] // GRID_W
    xp, xs = x_prompt, x_sample
    new_states = []
    for i in range(DEPTH):
        j = i // N_MIXERS
        sh_p, sc_p, g_p = modulation(c_ctx, ada_w[i], ada_b[i])
        sh_s, sc_s, g_s = modulation(c, ada_w[i], ada_b[i])
        hp = rms_norm(xp, norm_pre[i]) * (1.0 + sc_p) + sh_p
        hs = rms_norm(xs, norm_pre[i]) * (1.0 + sc_s[:, None, :]) + sh_s[:, None, :]
        if i % N_MIXERS == 0:
            params = (rw_mu[j], rw_w_in[j], rw_w0[j], rw_w1[j], rw_w2[j], rw_a0[j], rw_a1[j],
                      rw_a2[j], rw_k_k[j], rw_k_a[j], rw_r_k[j], rw_lnx_g[j], rw_lnx_b[j], rw_w_out[j])
            s_zero = jnp.zeros((2, b_ctx, RWKV_HEADS, RWKV_HEAD, RWKV_HEAD), jnp.float32)
            yp, s_ctx = rwkv7_bidir(hp, s_zero, *params)
            ys, _ = rwkv7_bidir(hs, jnp.swapaxes(state_rwkv[:, j], 0, 1), *params)
            new_states.append(jnp.swapaxes(s_ctx, 0, 1))
        else:
            params = (cv_w_in[j], cv_dw_w[j], cv_dw_b[j], cv_ln_g[j], cv_ln_b[j], cv_w_out[j])
            yp = conformer_conv(hp, 1, *params)
            ys = conformer_conv(hs, rows, *params)
        xp = xp + g_p * rms_norm(yp, norm_post[i])
        xs = xs + g_s[:, None, :] * rms_norm(ys, norm_post[i])
    new_state_rwkv = jnp.stack(new_states, axis=1).astype(x_prompt.dtype)
    return (xp, xs, new_state_rwkv)
```

```python
import numpy as np
from contextlib import ExitStack
import concourse.bass as bass
import concourse.mybir as mybir
from concourse.bass_utils import run_bass_kernel_spmd

F32 = mybir.dt.float32
F32R = mybir.dt.float32r
AF = mybir.ActivationFunctionType
ALU = mybir.AluOpType
AX = mybir.AxisListType

ENGS = ("pe", "act", "dve", "pool", "sp")
NCORES = 8
D = 1024
KC = 8
NH = 16
HD = 64
CK = 31
RMS_EPS = 1e-6
GN_EPS = 64e-5
LN_EPS = 1e-5
DEC_C = float(np.exp(-0.5))
import os as _os
WDT = F32 if _os.environ.get('WKV_F32') == '1' else F32R
YDT = F32R if _os.environ.get('WKV_YR') == '1' else F32


class Tok:
    __slots__ = ("name", "lw", "rd", "excl")

    def __init__(self, name="", excl=False):
        self.name = name
        self.lw = None
        self.rd = []
        self.excl = excl


class Op:
    __slots__ = ("eng", "fn", "deps", "signal", "semkey", "semval", "idx", "isdma", "waits")

    def __init__(self, eng, fn, isdma=False, semkey=None):
        self.eng = eng
        self.fn = fn
        self.deps = []
        self.signal = False
        self.semkey = semkey
        self.semval = None
        self.isdma = isdma
        self.waits = []


class Tl:
    def __init__(self, t, name, nreg=1, excl=False):
        self.t = t
        self.toks = [Tok(f"{name}.{i}", excl) for i in range(nreg)]

    def __getitem__(self, k):
        return self.t[k]

    def all(self):
        return list(self.toks)

    def r(self, i):
        return [self.toks[i]]


class Vw(Tl):
    def __init__(self, ap, tk):
        self.t = ap
        self.toks = list(tk)


class Prog:
    def __init__(self, nc):
        self.nc = nc
        self.ops = []
        self.stack = ExitStack()
        self.nt = 0
        self.sb_bytes = 0
        self.zcol = None

    def sb(self, shape, dtype=F32, name=None, nreg=1, zero=True):
        self.nt += 1
        name = (name or "t") + f"_{self.nt}"
        t = self.stack.enter_context(self.nc.sbuf_tensor(name, list(shape), dtype))
        self.sb_bytes += int(np.prod(shape[1:])) * 4
        tl = Tl(t, name, nreg)
        if not zero:
            return tl
        if self.zcol is None:
            self.zcol = tl
            self.op("dve", lambda e: e.memset(t[:], 0.0), writes=tl.toks)
        else:
            eng = ("dve", "pool")[self.nt % 2]
            if dtype == F32:
                self.op(eng, lambda e: e.memset(t[:], 0.0), writes=tl.toks)
            else:
                z = self.zcol.t[0:shape[0], 0:1]
                for _ in range(len(shape) - 2):
                    z = z.unsqueeze(2)
                zb = z.to_broadcast(list(shape))
                self.op(eng, lambda e: e.tensor_scalar(t[:], zb, 0.0, None, ALU.mult), reads=self.zcol.toks, writes=tl.toks)
        return tl

    def ps(self, shape, dtype=F32, name=None, nreg=1):
        self.nt += 1
        name = (name or "p") + f"_{self.nt}"
        t = self.stack.enter_context(self.nc.psum_tensor(name, list(shape), dtype))
        return Tl(t, name, nreg, excl=True)

    def op(self, eng, fn, reads=(), writes=(), isdma=False, semkey=None):
        o = Op(eng, fn, isdma, semkey)
        o.idx = len(self.ops)
        deps = set()
        for r in reads:
            if r.lw is not None:
                deps.add(r.lw)
            if r.excl:
                for x in r.rd:
                    if x.eng != eng:
                        deps.add(x)
        for w in writes:
            if w.lw is not None:
                deps.add(w.lw)
            for x in w.rd:
                deps.add(x)
        for r in reads:
            r.rd.append(o)
        for w in writes:
            w.lw = o
            w.rd = []
        deps.discard(o)
        for d in deps:
            if d.eng == "pe" and eng == "pe" and not d.isdma and not isdma:
                continue
            if d.isdma and isdma and d.semkey == semkey and semkey.startswith("all:"):
                continue
            o.deps.append(d)
            d.signal = True
        self.ops.append(o)
        return o

    def emit(self, final_wait_ops=()):
        nc = self.nc
        eng_cnt = {e: 0 for e in ENGS}
        dma_cnt = {}
        for o in final_wait_ops:
            o.signal = True
        for o in self.ops:
            if o.isdma:
                o.signal = True
            if not o.signal:
                continue
            if o.isdma:
                k = o.semkey
                dma_cnt[k] = dma_cnt.get(k, 0) + 16
                o.semval = dma_cnt[k]
            else:
                eng_cnt[o.eng] += 1
                o.semval = eng_cnt[o.eng]
        for o in self.ops:
            if o.isdma and o.semkey.startswith("all:"):
                o.semval = dma_cnt[o.semkey]
        eng_sem = {}
        dma_sem = {}
        for e in ENGS:
            eng_sem[e] = self.stack.enter_context(nc.semaphore("sem_" + e))
        for i, k in enumerate(dma_cnt):
            dma_sem[k] = self.stack.enter_context(nc.semaphore("dsem_%d" % i))
        self.n_sems = len(eng_sem) + len(dma_sem)
        self.eng_cnt = eng_cnt

        def semof(o):
            return dma_sem[o.semkey] if o.isdma else eng_sem[o.eng]

        per_eng = {e: [] for e in ENGS}
        for o in self.ops:
            per_eng[o.eng].append(o)
        waited = {e: {} for e in ENGS}
        for o in self.ops:
            w = waited[o.eng]
            need = {}
            for d in o.deps:
                s = semof(d)
                key = id(s)
                v = d.semval
                if w.get(key, 0) >= v:
                    continue
                if key not in need or need[key][1] < v:
                    need[key] = (s, v)
            for key, (s, v) in need.items():
                w[key] = v
                o.waits.append((s, v))
        finals = [(semof(o), o.semval) for o in final_wait_ops]
        block = self.stack.enter_context(nc.Block())

        def run(engobj, lst, is_sp=False):
            for o in lst:
                for (s, v) in o.waits:
                    engobj.wait_ge(s, v)
                ins = o.fn(engobj)
                if o.signal:
                    ins.then_inc(semof(o), 16 if o.isdma else 1)
            if is_sp:
                for (s, v) in finals:
                    engobj.wait_ge(s, v)

        @block.tensor
        def _(e):
            run(e, per_eng["pe"])

        @block.scalar
        def _(e):
            run(e, per_eng["act"])

        @block.vector
        def _(e):
            run(e, per_eng["dve"])

        @block.gpsimd
        def _(e):
            run(e, per_eng["pool"])

        @block.sync
        def _(e):
            run(e, per_eng["sp"], True)

    def close(self):
        self.stack.close()


def make_consts():
    c = {}
    c["ident"] = np.eye(128, dtype=np.float32)
    ob = np.zeros((128, 128), np.float32)
    ob[:64, :64] = 1
    ob[64:, 64:] = 1
    c["ones_blk"] = ob
    p = np.arange(128)[:, None]
    f = np.arange(128)[None, :]
    lt = (p < f).astype(np.float32)
    le = (p <= f).astype(np.float32)
    gt = (p > f).astype(np.float32)
    ge = (p >= f).astype(np.float32)
    c["cmF"] = np.concatenate([le, lt], 1)
    c["cmB"] = np.concatenate([ge, gt], 1)
    c["maskF"] = np.concatenate([-lt, lt, le, -gt, -le], 1)
    c["maskB"] = np.concatenate([-gt, gt, ge, -lt, -ge], 1)
    c["ones"] = np.ones((128, 128), np.float32)
    c["i64x2"] = np.concatenate([np.eye(64), np.eye(64)], 0).astype(np.float32)
    return c


GROUPS = {
    "S": (512, [1024], 1),
    "P": (0, [256, 256], 0),
}


def dram_specs():
    s = {}
    s["xin"] = [1536, D]
    s["condT"] = [128, KC, 2]
    s["st0"] = [2, NH, HD, HD]
    s["ada_w"] = [2, D, 3 * D]
    s["adab_fm"] = [2, 128, 24]
    s["npre_fm"] = [2, 128, KC]
    s["npost_fm"] = [2, 128, KC]
    s["mu_fm"] = [128, 6, KC]
    s["w_in"] = [4, D, D]
    s["w1cat"] = [D, 128]
    s["a1cat"] = [D, 128]
    s["w2cat"] = [128, D]
    s["a2cat"] = [128, D]
    s["colpack"] = [KC, 128, 9]
    s["w_out"] = [D, D]
    s["cv_w_in"] = [D, 3 * D]
    s["cv_dw"] = [KC, 128, CK]
    s["cv_cols"] = [KC, 128, 3]
    s["cv_w_out"] = [D, D]
    for k, v in make_consts().items():
        s[k] = list(v.shape)
    return s


def prep_inputs(inp):
    f = lambda a: np.ascontiguousarray(np.asarray(a, dtype=np.float32))
    sh = {}
    ada_b = np.asarray(inp["ada_b"], np.float32)
    sh["ada_w"] = f(inp["ada_w"])
    sh["adab_fm"] = f(ada_b.reshape(2, 24, 128).transpose(0, 2, 1))
    sh["npre_fm"] = f(np.asarray(inp["norm_pre"]).reshape(2, KC, 128).transpose(0, 2, 1))
    sh["npost_fm"] = f(np.asarray(inp["norm_post"]).reshape(2, KC, 128).transpose(0, 2, 1))
    sh["mu_fm"] = f(np.asarray(inp["rw_mu"])[0].reshape(6, KC, 128).transpose(2, 0, 1))
    sh["w_in"] = f(np.asarray(inp["rw_w_in"])[0])
    w1 = np.asarray(inp["rw_w1"])[0]
    a1 = np.asarray(inp["rw_a1"])[0]
    sh["w1cat"] = f(np.concatenate([w1[0], w1[1]], 1))
    sh["a1cat"] = f(np.concatenate([a1[0], a1[1]], 1))
    sh["w2cat"] = f(np.asarray(inp["rw_w2"])[0].reshape(128, D))
    sh["a2cat"] = f(np.asarray(inp["rw_a2"])[0].reshape(128, D))
    cols = [np.asarray(inp["rw_w0"])[0, 0], np.asarray(inp["rw_w0"])[0, 1],
            np.asarray(inp["rw_a0"])[0, 0], np.asarray(inp["rw_a0"])[0, 1],
            np.asarray(inp["rw_k_k"])[0], np.asarray(inp["rw_k_a"])[0],
            np.asarray(inp["rw_r_k"])[0].reshape(D), np.asarray(inp["rw_lnx_g"])[0],
            np.asarray(inp["rw_lnx_b"])[0]]
    sh["colpack"] = f(np.stack(cols, 1).reshape(KC, 128, 9))
    sh["w_out"] = f(np.asarray(inp["rw_w_out"])[0])
    sh["cv_w_in"] = f(np.asarray(inp["cv_w_in"])[0])
    sh["cv_dw"] = f(np.asarray(inp["cv_dw_w"])[0].T.reshape(KC, 128, CK))
    cc = [np.asarray(inp["cv_dw_b"])[0], np.asarray(inp["cv_ln_g"])[0], np.asarray(inp["cv_ln_b"])[0]]
    sh["cv_cols"] = f(np.stack(cc, 1).reshape(KC, 128, 3))
    sh["cv_w_out"] = f(np.asarray(inp["cv_w_out"])[0])
    sh.update(make_consts())
    xp = np.asarray(inp["x_prompt"], np.float32)
    xs = np.asarray(inp["x_sample"], np.float32)
    st = np.asarray(inp["state_rwkv"], np.float32)
    c = np.asarray(inp["c"], np.float32)
    cctx = np.asarray(inp["c_ctx"], np.float32)
    maps = []
    for i in range(NCORES):
        m = dict(sh)
        m["xin"] = f(np.concatenate([xp[2 * i].reshape(256, D), xp[2 * i + 1].reshape(256, D), xs[i]], 0))
        cond = np.stack([cctx, c[i]], 1)
        m["condT"] = f(cond.reshape(KC, 128, 2).transpose(1, 0, 2))
        m["st0"] = f(st[i, 0].transpose(0, 1, 3, 2))
        maps.append(m)
    return maps


def toks(*xs):
    out = []
    for x in xs:
        if isinstance(x, Tl):
            out.extend(x.toks)
        elif isinstance(x, Tok):
            out.append(x)
        else:
            out.extend(toks(*x))
    return out


def build(stage="full", dbg_shape=None):
    nc = bass.Bass("TRN2", target_bir_lowering=False)
    specs = dram_specs()
    Dr = {k: nc.dram_tensor(k, shp, F32, kind="ExternalInput").ap() for k, shp in specs.items()}
    y_out = nc.dram_tensor("y_out", [1536, D], F32, kind="ExternalOutput").ap()
    st_out = nc.dram_tensor("st_out", [2, 2, NH, HD, HD], F32, kind="ExternalOutput").ap()
    dbg = None
    if dbg_shape is not None:
        dbg = nc.dram_tensor("dbg", list(dbg_shape), F32, kind="ExternalOutput").ap()
    P = Prog(nc)
    out_ops = []
    zcol_t = P.sb([128, 1], F32, name="zcol")

    def dma(eng, out_ap, in_ap, reads, writes, key):
        return P.op(eng, lambda e: e.dma_start(out=out_ap, in_=in_ap), reads=toks(reads), writes=toks(writes),
                    isdma=True, semkey=key)

    def dbg_dump(tile_ap, src, col0, ncols, rows=128):
        o = dma("sp", dbg[0:rows, col0:col0 + ncols], tile_ap, src, [], "dbg%d" % col0)
        out_ops.append(o)

    C = {}
    for k in make_consts():
        C[k] = P.sb(specs[k], F32, name=k, zero=False)
        dma("sp", C[k][:], Dr[k], [], C[k], "all:const")
    one_col = P.sb([128, 1], name="one")
    tiny_col = P.sb([128, 1], name="tiny")
    P.op("dve", lambda e: e.memset(one_col[:], 1.0), writes=toks(one_col))
    P.op("dve", lambda e: e.memset(tiny_col[:], 1e-24), writes=toks(tiny_col))
    eps_cols = {}
    for nm, val in (("rms", RMS_EPS), ("gn", GN_EPS), ("ln", LN_EPS)):
        eps_cols[nm] = P.sb([128, 1], name="eps" + nm)
        P.op("dve", (lambda t, v: lambda e: e.memset(t[:], v))(eps_cols[nm], val), writes=toks(eps_cols[nm]))

    mu_fm = P.sb([128, 6, KC], name="mu_fm", zero=False)
    dma("sp", mu_fm[:], Dr["mu_fm"], [], mu_fm, "all:const")

    cvcols = P.sb([128, KC, 3], name="cvcols", zero=False)
    cvdw = P.sb([128, KC, CK], name="cvdw", zero=False)
    for e_ in range(KC):
        dma("sp", cvcols[:, e_, :], Dr["cv_cols"][e_], [], cvcols, "all:const")
        dma("sp", cvdw[:, e_, :], Dr["cv_dw"][e_], [], cvdw, "all:const")
    muh = P.sb([128, 6, KC], name="muh")
    P.op("dve", lambda e: e.tensor_scalar(muh[:], mu_fm[:], 0.5, None, ALU.mult), reads=toks(mu_fm), writes=toks(muh))

    banks = [P.ps([128, 512], name="bank%d" % i, nreg=1) for i in range(8)]
    for bk_ in banks:
        P.op("dve", (lambda bk_: lambda e: e.memset(bk_[:, :], 0.0))(bk_), writes=toks(bk_))

    def bq(b, c0, c1):
        return list(banks[b].toks)

    def bh(b, half):
        return list(banks[b].toks)

    TG = 1024
    NT = 8
    xg = P.sb([128, NT, D], F32, name="xg", nreg=NT)
    h_fm = P.sb([128, KC, TG], F32R, name="h_fm", nreg=KC)
    out_fm = P.sb([128, KC, TG], F32R, name="out_fm", nreg=KC)

    condT = P.sb([128, KC, 2], name="condT", zero=False)
    dma("sp", condT[:], Dr["condT"], [], condT, "all:const")
    scond = P.sb([128, KC, 2], name="scond")
    stmp = P.sb([128, KC, 2], name="stmp")
    P.op("act", lambda e: e.activation(stmp[:], condT[:], AF.Exp, scale=-1.0), reads=toks(condT), writes=toks(stmp))
    P.op("dve", lambda e: e.tensor_scalar_add(stmp[:], stmp[:], 1.0), reads=toks(stmp), writes=toks(stmp))
    P.op("dve", lambda e: e.reciprocal(stmp[:], stmp[:]), reads=toks(stmp), writes=toks(stmp))
    P.op("dve", lambda e: e.tensor_tensor(scond[:], condT[:], stmp[:], ALU.mult), reads=toks(condT, stmp), writes=toks(scond))
    adab_fm = P.sb([128, 2, 24], name="adab_fm", zero=False)
    npre_fm = P.sb([128, 2, KC], name="npre_fm", zero=False)
    npost_fm = P.sb([128, 2, KC], name="npost_fm", zero=False)
    for l in range(2):
        dma("sp", adab_fm[:, l, :], Dr["adab_fm"][l], [], adab_fm, "all:const")
        dma("sp", npre_fm[:, l, :], Dr["npre_fm"][l], [], npre_fm, "all:const")
        dma("sp", npost_fm[:, l, :], Dr["npost_fm"][l], [], npost_fm, "all:const")
    modfm = P.sb([128, 2, 24, 2], name="modfm")
    scale_col = P.sb([128, 2, KC, 2], name="scale_col")
    gate_col = P.sb([128, 2, KC, 2], name="gate_col")
    stg = [Vw(xg[:, 4 * i:4 * i + 4, :].rearrange("p a (b c) -> p (a b) c", c=512), xg.toks[4 * i:4 * i + 4]) for i in range(2)]
    nstg = 0
    for l in range(2):
        for q in range(6):
            s_ = stg[nstg % 2]
            nstg += 1
            dma("sp", s_[:], Dr["ada_w"][l].rearrange("(kc p) c -> p kc c", p=128)[:, :, q * 512:(q + 1) * 512],
                [], s_, "adastg%d" % (nstg % 2))
            for b4 in range(4):
                cb = q * 4 + b4
                for kc in range(KC):
                    P.op("pe", (lambda s_, cb, b4, kc: lambda e: e.matmul(
                        banks[0][:, cb * 2:(cb + 1) * 2], s_[:, kc, b4 * 128:(b4 + 1) * 128], scond[:, kc, :],
                        start=(kc == 0), stop=(kc == KC - 1)))(s_, cb, b4, kc),
                         reads=toks(s_, scond), writes=toks(banks[0]))
        P.op("dve", (lambda l: lambda e: e.tensor_tensor(
            modfm[:, l, :, :], banks[0][:, 0:48].rearrange("p (c j) -> p c j", j=2),
            adab_fm[:, l, :].unsqueeze(2).to_broadcast([128, 24, 2]), ALU.add))(l),
             reads=toks(adab_fm, banks[0]), writes=toks(modfm))
        P.op("dve", (lambda l: lambda e: e.scalar_tensor_tensor(
            scale_col[:, l, :, :], modfm[:, l, 8:16, :], 1.0,
            npre_fm[:, l, :].unsqueeze(2).to_broadcast([128, KC, 2]), ALU.add, ALU.mult))(l),
             reads=toks(modfm, npre_fm), writes=toks(scale_col))
        P.op("dve", (lambda l: lambda e: e.tensor_tensor(
            gate_col[:, l, :, :], modfm[:, l, 16:24, :],
            npost_fm[:, l, :].unsqueeze(2).to_broadcast([128, KC, 2]), ALU.mult))(l),
             reads=toks(modfm, npost_fm), writes=toks(gate_col))
    if stage == "mod":
        dbg_dump(modfm[:].rearrange("p l c j -> p (l c j)"), modfm, 0, 96)
        dbg_dump(scale_col[:].rearrange("p l c j -> p (l c j)"), scale_col, 96, 32)
        dbg_dump(gate_col[:].rearrange("p l c j -> p (l c j)"), gate_col, 128, 32)
        P.emit(out_ops)
        P.close()
        return nc

    ss = P.sb([128, NT], name="ss")
    rstd = P.sb([128, NT], name="rstd")
    wring = [P.sb([128, KC, 128], F32R, name="wr%d" % i) for i in range(4)]
    wpall = P.sb([128, 4, KC, 128], F32R, name="wpall", nreg=4)
    wpring = [Vw(wpall[:, i], wpall.r(i)) for i in range(4)]
    lwla = P.sb([128, 2, TG], name="lwla", nreg=2)
    lw_fm = Vw(lwla[:, 0, :], lwla.r(0))
    la_fm = Vw(lwla[:, 1, :], lwla.r(1))
    cf = {}
    for i, nm in enumerate(("r", "k", "v", "sg", "kk", "b", "kd", "sw")):
        cf[nm] = Vw(xg[:, i, :], xg.r(i))
    for nm in ("ksum", "ta", "tb"):
        cf[nm] = P.sb([128, TG + 4], name=nm)
    Bpad, tmpd = cf["ta"], cf["tb"]
    P.op("pool", lambda e: e.memset(Bpad[:], 0.0), writes=toks(Bpad))
    P.op("pool", lambda e: e.memset(tmpd[:], 0.0), writes=toks(tmpd))
    xn_buf = [cf["tb"], cf["ksum"]]
    junk = cf["ta"]
    xcnt = [0]

    def ginfo(g):
        off, lens, cond = GROUPS[g]
        Tg = sum(lens)
        return off, lens, cond, Tg, Tg // 128, Tg // 512

    def seq_pad_offsets(lens):
        offs = []
        o = 1
        for L in lens:
            offs.append(o)
            o += L + 2
        return offs, o - 1

    def phase1(l, g, from_dram):
        off, lens, cond, Tg, nt, nblk = ginfo(g)
        P.op("dve", lambda e: e.memset(ss[:], 0.0), writes=toks(ss))
        for ti in range(nt):
            if from_dram:
                dma("sp", xg[:, ti, :], Dr["xin"][off + ti * 128: off + (ti + 1) * 128, :], [], xg.r(ti), "xg%d" % ti)
            P.op("act", (lambda ti: lambda e: e.activation(junk[:, 0:D], xg[:, ti, :], AF.Square, accum_out=ss[:, ti:ti + 1]))(ti),
                 reads=xg.r(ti), writes=toks(junk, ss))
        P.op("act", lambda e: e.activation(rstd[:, 0:nt], ss[:, 0:nt], AF.Ln, bias=eps_cols["rms"][:], scale=1.0 / D),
             reads=toks(ss, eps_cols["rms"]), writes=toks(rstd))
        P.op("act", lambda e: e.activation(rstd[:, 0:nt], rstd[:, 0:nt], AF.Exp, scale=-0.5), reads=toks(rstd), writes=toks(rstd))
        for ti in range(nt):
            xn = xn_buf[ti % 2]
            P.op("dve", (lambda ti, xn: lambda e: e.tensor_scalar(xn[:, 0:D], xg[:, ti, :], rstd[:, ti:ti + 1], None, ALU.mult))(ti, xn),
                 reads=xg.r(ti) + toks(rstd), writes=toks(xn))
            b0 = 0 if ti % 2 == 0 else 2
            for kc in range(KC):
                bk = banks[b0 + kc // 4]
                c0 = (kc % 4) * 128
                P.op("pe", (lambda bk, c0, xn, kc: lambda e: e.transpose(bk[:, c0:c0 + 128], xn[:, kc * 128:(kc + 1) * 128], C["ident"][:]))(bk, c0, xn, kc),
                     reads=toks(xn, C["ident"]), writes=toks(bk))
            for kc in range(KC):
                bk = banks[b0 + kc // 4]
                c0 = (kc % 4) * 128
                if kc % 2 == 0:
                    P.op("act", (lambda bk, c0, kc, ti: lambda e: e.activation(
                        h_fm[:, kc, ti * 128:(ti + 1) * 128], bk[:, c0:c0 + 128], AF.Identity,
                        bias=modfm[:, l, kc, cond:cond + 1], scale=scale_col[:, l, kc, cond:cond + 1]))(bk, c0, kc, ti),
                         reads=toks(bk, modfm, scale_col), writes=h_fm.r(kc))
                else:
                    P.op("dve", (lambda bk, c0, kc, ti: lambda e: e.tensor_scalar(
                        h_fm[:, kc, ti * 128:(ti + 1) * 128], bk[:, c0:c0 + 128],
                        scale_col[:, l, kc, cond:cond + 1], modfm[:, l, kc, cond:cond + 1], ALU.mult, ALU.add))(bk, c0, kc, ti),
                         reads=toks(bk, modfm, scale_col), writes=h_fm.r(kc))

    def load_w(slot, src_ap, mu_idx):
        wt, wp = wring[slot], wpring[slot]
        dma("pool", wt[:], src_ap.rearrange("(kc p) e -> p kc e", p=128), [], wt, "w%d" % slot)
        P.op("pool", lambda e: e.tensor_tensor(wp[:], wt[:], muh[:, mu_idx, :].unsqueeze(2).to_broadcast([128, KC, 128]), ALU.mult),
             reads=toks(wt, muh), writes=toks(wp))

    def project(g, slot, out_t, bset=0):
        off, lens, cond, Tg, nt, nblk = ginfo(g)
        wt, wp = wring[slot], wpring[slot]
        offs, width = seq_pad_offsets(lens)
        for b in range(nblk):
            bA, bB = banks[4 * bset + b], banks[4 * bset + 2 + b]
            for (bk, w) in ((bA, wt), (bB, wp)):
                for kc in range(KC):
                    P.op("pe", (lambda bk, w, kc, b: lambda e: e.matmul(
                        bk[:, :], w[:, kc, :], h_fm[:, kc, b * 512:(b + 1) * 512], start=(kc == 0), stop=(kc == KC - 1)))(bk, w, kc, b),
                         reads=toks(w) + h_fm.r(kc), writes=toks(bk))
            P.op("act", (lambda bA, b: lambda e: e.copy(out_t[:, b * 512:(b + 1) * 512], bA[:, :]))(bA, b),
                 reads=toks(bA), writes=toks(out_t))
            t0 = b * 512
            pos = 0
            for si, L in enumerate(lens):
                lo, hi = max(t0, pos), min(t0 + 512, pos + L)
                if lo < hi:
                    P.op("act", (lambda bB, lo, hi, t0, po: lambda e: e.copy(Bpad[:, po:po + (hi - lo)], bB[:, lo - t0:hi - t0]))(
                        bB, lo, hi, t0, offs[si] + lo - pos), reads=toks(bB), writes=toks(Bpad))
                pos += L
        W = width + 1
        P.op("dve", lambda e: e.tensor_tensor(tmpd[:, 1:W - 1], Bpad[:, 0:W - 2], Bpad[:, 2:W], ALU.add),
             reads=toks(Bpad), writes=toks(tmpd))
        P.op("dve", lambda e: e.scalar_tensor_tensor(tmpd[:, 1:W - 1], Bpad[:, 1:W - 1], -2.0, tmpd[:, 1:W - 1], ALU.mult, ALU.add),
             reads=toks(Bpad, tmpd), writes=toks(tmpd))
        pos = 0
        for si, L in enumerate(lens):
            P.op("dve", (lambda pos, L, po: lambda e: e.tensor_tensor(out_t[:, pos:pos + L], out_t[:, pos:pos + L], tmpd[:, po:po + L], ALU.add))(pos, L, offs[si]),
                 reads=toks(tmpd, out_t), writes=toks(out_t))
            pos += L

    def sigmoid(out_ap, in_ap, rd, wr, tmp_t, Tg, nbias_ap=None, scale=1.0):
        kw = dict(scale=-scale)
        rdx = toks(rd)
        if nbias_ap is not None:
            kw["bias"] = nbias_ap[0]
            rdx = rdx + toks(nbias_ap[1])
        P.op("act", lambda e: e.activation(tmp_t[:, 0:Tg], in_ap, AF.Exp, **kw), reads=rdx, writes=toks(tmp_t))
        P.op("act", lambda e: e.activation(tmp_t[:, 0:Tg], tmp_t[:, 0:Tg], AF.Ln, bias=one_col[:]), reads=toks(tmp_t, one_col), writes=toks(tmp_t))
        P.op("act", lambda e: e.activation(out_ap, tmp_t[:, 0:Tg], AF.Exp, scale=-1.0), reads=toks(tmp_t), writes=toks(wr))

    def zero_pads(g):
        off, lens, cond, Tg, nt, nblk = ginfo(g)
        offs_, width_ = seq_pad_offsets(lens)
        for si_, L_ in enumerate(lens):
            for col in (offs_[si_] - 1, offs_[si_] + L_):
                X("pool", "memset", Bpad[:, col:col + 1], 0.0, wr=[Bpad])

    def lora_stage(g):
        off, lens, cond, Tg, nt, nblk = ginfo(g)
        zero_pads(g)
        load_w(0, Dr["w1cat"], 4)
        load_w(1, Dr["a1cat"], 5)
        project(g, 0, lw_fm, 0)
        project(g, 1, la_fm, 1)
        sigmoid(cf["ta"][:, 0:Tg], lw_fm[:, 0:Tg], lw_fm, cf["ta"], cf["tb"], Tg, scale=2.0)
        P.op("dve", lambda e: e.tensor_scalar(lw_fm[:, 0:Tg], cf["ta"][:, 0:Tg], 2.0, -1.0, ALU.mult, ALU.add),
             reads=toks(cf["ta"]), writes=toks(lw_fm))

    def proj_stage(g, e_idx, prefetch_next=None):
        off, lens, cond, Tg, nt, nblk = ginfo(g)
        zero_pads(g)
        for n, nm in enumerate(("r", "k", "v", "sg")):
            project(g, n, cf[nm], n % 2)
        sigmoid(cf["ta"][:, 0:Tg], cf["sg"][:, 0:Tg], cf["sg"], cf["ta"], cf["tb"], Tg)
        P.op("dve", lambda e: e.tensor_tensor(cf["sg"][:, 0:Tg], cf["sg"][:, 0:Tg], cf["ta"][:, 0:Tg], ALU.mult),
             reads=toks(cf["sg"], cf["ta"]), writes=toks(cf["sg"]))

    def load_chunk_weights(e_idx):
        for n in range(4):
            load_w(n, Dr["w_in"][n][:, e_idx * 128:(e_idx + 1) * 128], n)


    ccb = [P.sb([128, 9], name="cc%d" % i) for i in range(2)]
    w2c = [P.sb([128, 128], name="w2c%d" % i) for i in range(2)]
    a2c = [P.sb([128, 128], name="a2c%d" % i) for i in range(2)]
    dcol = P.sb([128, 8], name="dcol")
    TM_2 = [P.sb([128, 4, 128], WDT, name="TM%d" % i) for i in range(3)]
    LwT = P.sb([128, 128], name="LwT")
    EF = P.sb([128, 384], name="EF")
    ET = P.sb([128, 256], name="ET")
    BK_2 = [P.sb([128, 256], WDT, name="BK%d" % i) for i in range(2)]
    QRP_2 = [P.sb([128, 320], WDT, name="QRP%d" % i) for i in range(2)]
    Zt_2 = [P.sb([128, 2, 128], WDT, name="Zt%d" % i) for i in range(2)]
    AM_2 = [P.sb([128, 2, 448], WDT, name="AM%d" % i, nreg=2) for i in range(2)]
    Kd_2 = [P.sb([128, 128], WDT, name="Kd%d" % i) for i in range(2)]
    YPTall = P.sb([128, 2, 2, 384], YDT, name="YPT", nreg=4)
    YPT = [Vw(YPTall[:, i], YPTall.toks[2 * i:2 * i + 2]) for i in range(2)]
    WU = P.sb([128, 2, 128], WDT, name="WU", nreg=2)
    TinvR = P.sb([128, 2, 128], WDT, name="TinvR", nreg=2)
    zpad_t = P.sb([128, 16 * (64 + CK - 1)], F32R, name="zpad")
    QMs = P.sb([64, 2, 192], WDT, name="QMs", nreg=2)
    ST = [P.sb([64, 2, 64], WDT, name="ST%d" % d) for d in range(2)]
    identR = P.sb([128, 128], WDT, name="identR")
    P.op("dve", lambda e: e.tensor_copy(identR[:], C["ident"][:]), reads=toks(C["ident"]), writes=toks(identR))
    o_acc = P.sb([128, NT, 128], name="o_acc", nreg=NT)
    gstat = P.sb([128, 4, NT * 2], name="gstat")
    ccnt = [0]

    def load_chunk_consts(e_idx):
        i = ccnt[0] % 2
        ccnt[0] += 1
        dma("sp", ccb[i][:], Dr["colpack"][e_idx], [], ccb[i], "cc%d" % i)
        dma("sp", w2c[i][:], Dr["w2cat"][:, e_idx * 128:(e_idx + 1) * 128], [], w2c[i], "w2c%d" % i)
        dma("sp", a2c[i][:], Dr["a2cat"][:, e_idx * 128:(e_idx + 1) * 128], [], a2c[i], "a2c%d" % i)
        return i

    def mm(out_ap, lhsT, rhs, rd, wr, start=True, stop=True):
        P.op("pe", lambda e: e.matmul(out_ap, lhsT, rhs, start=start, stop=stop), reads=toks(rd), writes=toks(wr))

    def X(eng, meth, *args, rd=(), wr=(), **kw):
        P.op(eng, lambda e: getattr(e, meth)(*args, **kw), reads=toks(rd), writes=toks(wr))

    def wkv_chunk(g, e_idx, ci):
        off, lens, cond, Tg, nt, nblk = ginfo(g)
        cc, w2, a2 = ccb[ci], w2c[ci], a2c[ci]
        r_, k_, v_, sg_, kk_, b_, kd_, sw_ = (cf[n] for n in ("r", "k", "v", "sg", "kk", "b", "kd", "sw"))
        ksum, ta, tb = cf["ksum"], cf["ta"], cf["tb"]
        ident, ones_blk, i64x2, ones_c = C["ident"], C["ones_blk"], C["i64x2"], C["ones"]
        X("dve", "tensor_scalar", dcol[:, 0:4], cc[:, 0:4], -1.0, None, ALU.mult, rd=[cc], wr=[dcol])
        X("dve", "tensor_scalar", dcol[:, 4:5], cc[:, 5:6], -1.0, 1.0, ALU.mult, ALU.add, rd=[cc], wr=[dcol])
        X("dve", "tensor_scalar", dcol[:, 5:6], cc[:, 6:7], 0.5, None, ALU.mult, rd=[cc], wr=[dcol])
        X("dve", "tensor_scalar", kk_[:, 0:Tg], k_[:, 0:Tg], cc[:, 4:5], None, ALU.mult, rd=[k_, cc], wr=[kk_])
        X("dve", "tensor_tensor", ta[:, 0:Tg], kk_[:, 0:Tg], kk_[:, 0:Tg], ALU.mult, rd=[kk_], wr=[ta])
        for b in range(nblk):
            sl = slice(b * 512, (b + 1) * 512)
            mm(banks[b][:, :], ones_blk[:], ta[:, sl], [ones_blk, ta], banks[b])
            X("act", "activation", tb[:, sl], banks[b][:, :], AF.Ln, bias=tiny_col[:], rd=[banks[b], tiny_col], wr=[tb])
        X("act", "activation", tb[:, 0:Tg], tb[:, 0:Tg], AF.Exp, scale=-0.5, rd=[tb], wr=[tb])
        X("dve", "tensor_tensor", kk_[:, 0:Tg], kk_[:, 0:Tg], tb[:, 0:Tg], ALU.mult, rd=[kk_, tb], wr=[kk_])

        seq_of_tile = []
        for si, L in enumerate(lens):
            seq_of_tile += [si] * (L // 128)
        first_tile, last_tile = {}, {}
        for ti, si in enumerate(seq_of_tile):
            first_tile.setdefault(si, ti)
            last_tile[si] = ti

        def h3(ap):
            return ap.rearrange("p (h j) -> p h j", h=2)

        def sig_fm(d, wt_, src, ncol_idx, dst):
            rows = slice(d * 64, (d + 1) * 64)
            bb = 0 if d == 0 else 2
            for b in range(nblk):
                sl = slice(b * 512, (b + 1) * 512)
                bk = banks[bb + b]
                mm(bk[:, :], wt_[rows, :], src[rows, sl], [wt_, src], bk)
                X("act", "activation", dst[:, sl], bk[:, :], AF.Exp, bias=dcol[:, ncol_idx:ncol_idx + 1], scale=-1.0, rd=[bk, dcol], wr=[dst])
            X("act", "activation", dst[:, 0:Tg], dst[:, 0:Tg], AF.Ln, bias=one_col[:], rd=[dst, one_col], wr=[dst])
            X("act", "activation", dst[:, 0:Tg], dst[:, 0:Tg], AF.Exp, scale=-1.0, rd=[dst], wr=[dst])

        def bkd_from_a(d):
            X("dve", "tensor_tensor", b_[:, 0:Tg], kk_[:, 0:Tg], ta[:, 0:Tg], ALU.mult, rd=[kk_, ta], wr=[b_])
            X("dve", "tensor_scalar", ta[:, 0:Tg], ta[:, 0:Tg], cc[:, 5:6], dcol[:, 4:5], ALU.mult, ALU.add, rd=[ta, cc, dcol], wr=[ta])
            X("dve", "tensor_tensor", kd_[:, 0:Tg], k_[:, 0:Tg], ta[:, 0:Tg], ALU.mult, rd=[k_, ta], wr=[kd_])
            if d == 0:
                X("pool", "tensor_copy", ksum[:, 0:Tg], kd_[:, 0:Tg], rd=[kd_], wr=[ksum])
            else:
                X("pool", "tensor_tensor", ksum[:, 0:Tg], ksum[:, 0:Tg], kd_[:, 0:Tg], ALU.add, rd=[kd_, ksum], wr=[ksum])

        for d in range(2):
            if d == 0:
                sig_fm(0, a2, la_fm, 2, ta)
                bkd_from_a(0)
                sig_fm(0, w2, lw_fm, 0, sw_)
                sig_fm(1, a2, la_fm, 3, ta)
                sig_fm(1, w2, lw_fm, 1, tb)
                sw_src = sw_
            else:
                bkd_from_a(1)
                sw_src = tb
            cm = C["cmF"] if d == 0 else C["cmB"]
            mEx = Vw(cm[:, 128:256], cm.toks)
            cmo = C["cmB"] if d == 0 else C["cmF"]
            mEd = Vw(cmo[:, 128:256], cmo.toks)
            mask = C["maskF"] if d == 0 else C["maskB"]
            pcc = 127 if d == 0 else 0
            tiles = list(range(nt)) if d == 0 else list(range(nt - 1, -1, -1))

            def head_pieces(ti, ui_):
                pb = ui_ % 2
                TM, BK, QRP, Zt, AM, Kd = TM_2[ui_ % 3], BK_2[pb], QRP_2[pb], Zt_2[pb], AM_2[pb], Kd_2[pb]
                tsl = slice(ti * 128, (ti + 1) * 128)
                cur = YPT[0]
                pcs = []

                def p0():
                    for j, src in enumerate((v_, kk_, b_, kd_)):
                        X("pe", "transpose", banks[4][:, j * 128:(j + 1) * 128], src[:, tsl], ident[:], rd=[src, ident], wr=[banks[4]])
                    X("pe", "transpose", banks[5][:, 0:128], sw_src[:, tsl], ident[:], rd=[sw_src, ident], wr=[banks[5]])
                    X("act", "copy", TM[:, 0:4, :].rearrange("p a b -> p (a b)"), banks[4][:, :], rd=[banks[4]], wr=[TM])
                    X("act", "activation", LwT[:, :], banks[5][:, 0:128], AF.Identity, scale=-DEC_C, rd=[banks[5]], wr=[LwT])
                pcs.append(p0)

                def p1():
                    mm(banks[5][:, 128:384], LwT[:, :], cm[:, :], [LwT, cm], banks[5])
                    mm(banks[4][:, 0:128], mEx[:, :], LwT[:, :], [LwT, mEx], banks[4])
                    mm(banks[4][:, 128:256], mEd[:, :], LwT[:, :], [LwT, mEd], banks[4])
                    X("act", "activation", EF[:, 0:256], banks[5][:, 128:384], AF.Exp, rd=[banks[5]], wr=[EF])
                    X("act", "activation", EF[:, 256:384], banks[5][:, 128:256], AF.Exp, scale=-1.0, rd=[banks[5]], wr=[EF])
                    X("act", "activation", ET[:, :], banks[4][:, 0:256], AF.Exp, rd=[banks[4]], wr=[ET])
                pcs.append(p1)

                def p2():
                    X("dve", "tensor_tensor", BK[:, 0:128], b_[:, tsl], EF[:, 256:384], ALU.mult, rd=[b_, EF], wr=[BK])
                    X("dve", "tensor_tensor", BK[:, 128:256], kd_[:, tsl], EF[:, 256:384], ALU.mult, rd=[kd_, EF], wr=[BK])
                    X("dve", "tensor_tensor", QRP[:, 0:128], kk_[:, tsl], EF[:, 128:256], ALU.mult, rd=[kk_, EF], wr=[QRP])
                    X("dve", "tensor_tensor", QRP[:, 128:256], r_[:, tsl], EF[:, 0:128], ALU.mult, rd=[r_, EF], wr=[QRP])
                pcs.append(p2)

                def p3():
                    X("dve", "tensor_scalar", QRP[:, 256:320], i64x2[:, :], EF[:, pcc:pcc + 1], None, ALU.mult, rd=[i64x2, EF], wr=[QRP])
                    X("dve", "tensor_tensor", Zt[:, :, 0:64], h3(TM[:, 1, :].bitcast(F32)), h3(ET[:, 0:128]), ALU.mult, rd=[TM, ET], wr=[Zt])
                    X("dve", "scalar_tensor_tensor", AM[:, :, 384:448], h3(TM[:, 2, :].bitcast(F32)), -1.0, h3(ET[:, 128:256]), ALU.mult, ALU.mult, rd=[TM, ET], wr=[AM])
                    X("dve", "tensor_tensor", Kd[:, :], TM[:, 3, :].bitcast(F32), ET[:, 128:256], ALU.mult, rd=[TM, ET], wr=[Kd])
                pcs.append(p3)

                def pa(hh):
                    def f():
                        hr = slice(hh * 64, (hh + 1) * 64)
                        bk = banks[1 + 2 * hh]
                        bk2 = banks[0 + 2 * hh]
                        mm(bk[:, 0:128], BK[hr, 0:128], QRP[hr, 0:128], [BK, QRP], bk)
                        mm(bk[:, 384:512], QRP[hr, 0:128], BK[hr, 0:128], [BK, QRP], bk)
                        mm(bk[:, 128:256], BK[hr, 128:256], QRP[hr, 0:128], [BK, QRP], bk)
                        mm(bk[:, 256:384], BK[hr, 128:256], QRP[hr, 128:256], [BK, QRP], bk)
                        mm(bk2[:, 0:128], BK[hr, 0:128], QRP[hr, 128:256], [BK, QRP], bk2)
                    return f

                def pm(hh):
                    def f():
                        bk = banks[1 + 2 * hh]
                        bk2 = banks[0 + 2 * hh]
                        ct = [cur.toks[hh]]
                        X("dve", "tensor_tensor", cur[:, hh, 0:128], bk[:, 0:128], mask[:, 0:128], ALU.mult, rd=[bk, mask], wr=ct)
                        X("dve", "tensor_tensor", cur[:, hh, 256:384], bk[:, 384:512], mask[:, 384:512], ALU.mult, rd=[bk, mask], wr=ct)
                        X("pool", "tensor_tensor", cur[:, hh, 128:256], cur[:, hh, 0:128].bitcast(F32), ident[:, :], ALU.add, rd=ct + [ident], wr=ct)
                        X("dve", "tensor_tensor", AM[:, hh, 0:256], bk[:, 128:384], mask[:, 128:384], ALU.mult, rd=[bk, mask], wr=AM.r(hh))
                        X("dve", "tensor_tensor", AM[:, hh, 256:384], bk2[:, 0:128], mask[:, 512:640], ALU.mult, rd=[bk2, mask], wr=AM.r(hh))
                    return f
                pcs += [pa(0), pa(1), pm(0), pm(1)]
                return pcs

            NLEV = 7

            def doubling_level(lev):
                cur, nxt = (YPT[0], YPT[1]) if lev % 2 == 0 else (YPT[1], YPT[0])
                for hh in range(2):
                    bk = banks[6 + hh]
                    ct = [cur.toks[hh]]
                    Y, Pm, YT = cur[:, hh, 0:128], cur[:, hh, 128:256], cur[:, hh, 256:384]
                    if lev == 0:
                        mm(bk[:, 256:384], Y, YT, ct, bk)
                        mm(bk[:, 0:128], YT, Y, ct, bk)
                    elif lev <= NLEV - 3:
                        mm(bk[:, 256:384], Y, YT, ct, bk)
                        mm(bk[:, 0:256], YT, cur[:, hh, 0:256], ct, bk)
                    elif lev == NLEV - 2:
                        mm(bk[:, 256:384], Y, YT, ct, bk)
                        mm(bk[:, 128:256], YT, Pm, ct, bk)
                    else:
                        mm(bk[:, 128:256], YT, Pm, ct, bk)
                for hh in range(2):
                    bk = banks[6 + hh]
                    ct = [cur.toks[hh]]
                    nt_ = [nxt.toks[hh]]
                    if lev == 0:
                        X("act", "copy", nxt[:, hh, 0:128], bk[:, 0:128], rd=[bk], wr=nt_)
                        X("act", "copy", nxt[:, hh, 256:384], bk[:, 256:384], rd=[bk], wr=nt_)
                        X("pool", "tensor_copy", nxt[:, hh, 128:256], cur[:, hh, 128:256].bitcast(F32), rd=ct, wr=nt_)
                    elif lev <= NLEV - 3:
                        X("act", "copy", nxt[:, hh, :], bk[:, 0:384], rd=[bk], wr=nt_)
                        X("dve", "tensor_tensor", nxt[:, hh, 128:256], nxt[:, hh, 128:256].bitcast(F32), cur[:, hh, 128:256].bitcast(F32), ALU.add, rd=ct + nt_, wr=nt_)
                    elif lev == NLEV - 2:
                        X("act", "copy", nxt[:, hh, 128:384], bk[:, 128:384], rd=[bk], wr=nt_)
                        X("dve", "tensor_tensor", nxt[:, hh, 128:256], nxt[:, hh, 128:256].bitcast(F32), cur[:, hh, 128:256].bitcast(F32), ALU.add, rd=ct + nt_, wr=nt_)
                    else:
                        X("dve", "tensor_tensor", TinvR[:, hh, :], bk[:, 128:256], cur[:, hh, 128:256].bitcast(F32), ALU.add, rd=[bk] + ct, wr=TinvR.r(hh))

            def tail_pieces(ti, ui_):
                pb = ui_ % 2
                TM, BK, QRP, Zt, AM, Kd = TM_2[ui_ % 3], BK_2[pb], QRP_2[pb], Zt_2[pb], AM_2[pb], Kd_2[pb]
                si = seq_of_tile[ti]
                seq_start = (ti == first_tile[si]) if d == 0 else (ti == last_tile[si])
                seq_end = (ti == last_tile[si]) if d == 0 else (ti == first_tile[si])
                tb_ = [banks[1], banks[3]]

                def t0():
                    if seq_start:
                        if g == "S":
                            dma("pool", ST[d][:], Dr["st0"][d, 2 * e_idx:2 * e_idx + 2].rearrange("h j i -> j h i"), [], ST[d], "st%d" % d)
                        else:
                            X("dve", "tensor_scalar", ST[d][:], ones_c[0:64, 0:1].unsqueeze(2).to_broadcast([64, 2, 64]), 0.0, None, ALU.mult, rd=[ones_c], wr=[ST[d]])
                    for hh in range(2):
                        bk = tb_[hh]
                        mm(bk[:, 0:64], AM[:, hh, 0:128], TM[:, 0, hh * 64:(hh + 1) * 64], AM.r(hh) + [TM], bk)
                        X("act", "copy", Zt[:, hh, 64:128], bk[:, 0:64], rd=[bk], wr=[Zt])

                def t0b():
                    for hh in range(2):
                        bk = tb_[hh]
                        mm(bk[:, 64:192], TinvR[:, hh, :], Zt[:, hh, :], TinvR.r(hh) + [Zt], bk)
                        X("act", "copy", WU[:, hh, :], bk[:, 64:192], rd=[bk], wr=WU.r(hh))

                def t1():
                    for hh in range(2):
                        hr = slice(hh * 64, (hh + 1) * 64)
                        bk = tb_[hh]
                        mm(bk[0:64, 192:384], WU[:, hh, 0:64], AM[:, hh, 256:448], AM.r(hh) + WU.r(hh), bk, start=True, stop=False)
                        mm(bk[0:64, 192:384], identR[hr, hr], QRP[hr, 128:320], [identR, QRP], bk, start=False, stop=True)
                        X("act", "copy", QMs[:, hh, :], bk[0:64, 192:384], rd=[bk], wr=QMs.r(hh))

                def t2():
                    for hh in range(2):
                        bk = tb_[hh]
                        vv = TM[:, 0, hh * 64:(hh + 1) * 64]
                        nu0 = WU[:, hh, 64:128]
                        mm(bk[:, 384:448], AM[:, hh, 256:384], nu0, AM.r(hh) + WU.r(hh), bk, start=True, stop=False)
                        mm(bk[:, 384:448], AM[:, hh, 128:256], vv, AM.r(hh) + [TM], bk, start=False, stop=False)
                        mm(bk[:, 384:448], QMs[:, hh, 0:128], ST[d][:, hh, :], QMs.r(hh) + [ST[d]], bk, start=False, stop=True)
                        mm(bk[0:64, 448:512], AM[:, hh, 384:448], nu0, AM.r(hh) + WU.r(hh), bk, start=True, stop=False)
                        mm(bk[0:64, 448:512], Kd[:, hh * 64:(hh + 1) * 64], vv, [Kd, TM], bk, start=False, stop=False)
                        mm(bk[0:64, 448:512], QMs[:, hh, 128:192], ST[d][:, hh, :], QMs.r(hh) + [ST[d]], bk, start=False, stop=True)
                        if d == 0:
                            X("act", "copy", o_acc[:, ti, hh * 64:(hh + 1) * 64], bk[:, 384:448], rd=[bk], wr=o_acc.r(ti))
                        else:
                            X("dve", "tensor_tensor", o_acc[:, ti, hh * 64:(hh + 1) * 64], o_acc[:, ti, hh * 64:(hh + 1) * 64], bk[:, 384:448], ALU.add,
                              rd=[bk] + o_acc.r(ti), wr=o_acc.r(ti))
                        X("act", "copy", ST[d][:, hh, :], bk[0:64, 448:512], rd=[bk], wr=[ST[d]])
                    if seq_end and g == "P":
                        o = dma("sp", st_out[si, d, 2 * e_idx:2 * e_idx + 2].rearrange("h j i -> j h i"), ST[d][:].bitcast(F32), ST[d], [], "sto%d" % d)
                        out_ops.append(o)
                return [t0, t0b, t1, t2]

            for pc in head_pieces(tiles[0], 0):
                pc()
            prev_tail = []
            for ui, ti in enumerate(tiles):
                nxt_pcs = head_pieces(tiles[ui + 1], ui + 1) if ui + 1 < len(tiles) else []
                for lev in range(NLEV):
                    doubling_level(lev)
                    if lev < 4 and prev_tail:
                        prev_tail[lev]()
                    if nxt_pcs:
                        if lev == 1:
                            nxt_pcs[0]()
                        elif lev == 2:
                            nxt_pcs[1]()
                        elif lev == 3:
                            nxt_pcs[2]()
                        elif lev == 4:
                            nxt_pcs[3]()
                        elif lev == 5:
                            nxt_pcs[4]()
                            nxt_pcs[5]()
                if nxt_pcs:
                    nxt_pcs[6]()
                    nxt_pcs[7]()
                prev_tail = tail_pieces(ti, ui)
            for pc in prev_tail:
                pc()

        n2 = nt * 2
        o3 = o_acc[:, 0:nt, :].rearrange("p t (h i) -> p (t h) i", h=2)
        X("dve", "tensor_reduce", gstat[:, 0, 0:n2], o3, AX.X, ALU.add, rd=[o_acc], wr=[gstat])
        sq3 = ta[:, 0:nt * 128].rearrange("p (a i) -> p a i", i=64)
        X("dve", "tensor_tensor", sq3, o3, o3, ALU.mult, rd=[o_acc], wr=[ta])
        X("dve", "tensor_reduce", gstat[:, 1, 0:n2], sq3, AX.X, ALU.add, rd=[ta], wr=[gstat])
        X("dve", "tensor_scalar", gstat[:, 0, 0:n2], gstat[:, 0, 0:n2], 1.0 / 64, None, ALU.mult, rd=[gstat], wr=[gstat])
        X("dve", "tensor_tensor", gstat[:, 2, 0:n2], gstat[:, 0, 0:n2], gstat[:, 0, 0:n2], ALU.mult, rd=[gstat], wr=[gstat])
        X("dve", "scalar_tensor_tensor", gstat[:, 1, 0:n2], gstat[:, 1, 0:n2], 1.0 / 64, gstat[:, 2, 0:n2], ALU.mult, ALU.subtract, rd=[gstat], wr=[gstat])
        X("act", "activation", gstat[:, 3, 0:n2], gstat[:, 1, 0:n2], AF.Ln, bias=eps_cols["gn"][:], rd=[gstat, eps_cols["gn"]], wr=[gstat])
        X("act", "activation", gstat[:, 3, 0:n2], gstat[:, 3, 0:n2], AF.Exp, scale=-0.5, rd=[gstat], wr=[gstat])
        X("dve", "tensor_tensor", o3, o3, gstat[:, 0, 0:n2].unsqueeze(2).to_broadcast([128, n2, 64]), ALU.subtract, rd=[o_acc, gstat], wr=[o_acc])
        X("dve", "tensor_tensor", o3, o3, gstat[:, 3, 0:n2].unsqueeze(2).to_broadcast([128, n2, 64]), ALU.mult, rd=[o_acc, gstat], wr=[o_acc])
        X("dve", "tensor_tensor", ta[:, 0:Tg], r_[:, 0:Tg], ksum[:, 0:Tg], ALU.mult, rd=[r_, ksum], wr=[ta])
        X("dve", "tensor_scalar", ta[:, 0:Tg], ta[:, 0:Tg], dcol[:, 5:6], None, ALU.mult, rd=[ta, dcol], wr=[ta])
        for b in range(nblk):
            sl = slice(b * 512, (b + 1) * 512)
            mm(banks[b][:, :], ones_blk[:], ta[:, sl], [ones_blk, ta], banks[b])
            X("dve", "tensor_tensor", tb[:, sl], banks[b][:, :], v_[:, sl], ALU.mult, rd=[banks[b], v_], wr=[tb])
        X("dve", "tensor_scalar", tb[:, 0:Tg], tb[:, 0:Tg], cc[:, 8:9], None, ALU.add, rd=[tb, cc], wr=[tb])
        for b in range(nblk):
            bk = banks[2 + b]
            for q in range(4):
                ti = b * 4 + q
                X("pe", "transpose", bk[:, q * 128:(q + 1) * 128], o_acc[:, ti, :], ident[:], rd=o_acc.r(ti) + [ident], wr=[bk])
            sl = slice(b * 512, (b + 1) * 512)
            X("dve", "scalar_tensor_tensor", ta[:, sl], bk[:, :], cc[:, 7:8], tb[:, sl], ALU.mult, ALU.add, rd=[bk, cc, tb], wr=[ta])
            X("dve", "tensor_tensor", out_fm[:, e_idx, sl], ta[:, sl], sg_[:, sl], ALU.mult, rd=[ta, sg_], wr=out_fm.r(e_idx))


    ss3 = P.sb([128, 2], name="ss3")
    rs3 = P.sb([128, 1], name="rs3")
    dgt = P.sb([128, 128], name="dgt")

    def phase3(l, g, wout_ap, reload_x, final):
        off, lens, cond, Tg, nt, nblk = ginfo(g)
        ta, tb = cf["ta"], cf["tb"]
        ident, ones = C["ident"], C["ones"]
        dma("pool", h_fm[:, :, :], wout_ap.rearrange("(kc p) e -> p kc e", p=128), [], h_fm, "wout")
        for kc in range(KC):
            X("dve", "tensor_scalar", dgt[:, :], ident[:, :], gate_col[:, l, kc, cond:cond + 1], None, ALU.mult, rd=[ident, gate_col], wr=[dgt])
            bk = banks[4 + kc // 4]
            mm(bk[:, (kc % 4) * 128:(kc % 4 + 1) * 128], ones[:, :], dgt[:, :], [ones, dgt], bk)
            if kc % 4 == 3:
                X("act", "copy", ta[:, (kc // 4) * 512:(kc // 4 + 1) * 512], bk[:, :], rd=[bk], wr=[ta])
        for ti in range(nt):
            b0 = 0 if ti % 2 == 0 else 2
            tsl = slice(ti * 128, (ti + 1) * 128)
            if reload_x:
                dma("sp", xg[:, ti, :], Dr["xin"][off + ti * 128: off + (ti + 1) * 128, :], [], xg.r(ti), "xg%d" % ti)
            for hb in range(2):
                bk = banks[b0 + hb]
                for kc in range(KC):
                    mm(bk[:, :], out_fm[:, kc, tsl], h_fm[:, kc, hb * 512:(hb + 1) * 512], out_fm.r(kc) + toks(h_fm), bk,
                       start=(kc == 0), stop=(kc == KC - 1))
                X("act", "activation", tb[:, hb * 512:(hb + 1) * 512], bk[:, :], AF.Square, accum_out=ss3[:, hb:hb + 1], rd=[bk], wr=[tb, ss3])
            X("dve", "tensor_tensor", rs3[:, :], ss3[:, 0:1], ss3[:, 1:2], ALU.add, rd=[ss3], wr=[rs3])
            X("act", "activation", rs3[:, :], rs3[:, :], AF.Ln, bias=eps_cols["rms"][:], scale=1.0 / D, rd=[rs3, eps_cols["rms"]], wr=[rs3])
            X("act", "activation", rs3[:, :], rs3[:, :], AF.Exp, scale=-0.5, rd=[rs3], wr=[rs3])
            for hb in range(2):
                bk = banks[b0 + hb]
                sl = slice(hb * 512, (hb + 1) * 512)
                X("dve", "scalar_tensor_tensor", tb[:, sl], bk[:, :], rs3[:, 0:1], ta[:, sl], ALU.mult, ALU.mult, rd=[bk, rs3, ta], wr=[tb])
            X("pool", "tensor_tensor", xg[:, ti, :], xg[:, ti, :], tb[:, 0:D], ALU.add, rd=xg.r(ti) + [tb], wr=xg.r(ti))
            if final:
                o = dma("sp", y_out[off + ti * 128: off + (ti + 1) * 128, :], xg[:, ti, :], xg.r(ti), [], "yo%d" % ti)
                out_ops.append(o)

    def layer0(g):
        phase1(0, g, True)
        lora_stage(g)
        load_chunk_weights(0)
        ci = load_chunk_consts(0)
        for e_idx in range(KC):
            proj_stage(g, e_idx)
            if e_idx + 1 < KC:
                load_chunk_weights(e_idx + 1)
                ci_next = load_chunk_consts(e_idx + 1)
            wkv_chunk(g, e_idx, ci)
            ci = ci_next
        phase3(0, g, Dr["w_out"], True, False)


    def load_w_plain(slot, src_ap):
        wt = wring[slot]
        dma("pool", wt[:], src_ap.rearrange("(kc p) e -> p kc e", p=128), [], wt, "w%d" % slot)

    def project_plain(g, slot, bank_base):
        off, lens, cond, Tg, nt, nblk = ginfo(g)
        wt = wring[slot]
        for b in range(nblk):
            bk = banks[bank_base + b]
            for kc in range(KC):
                mm(bk[:, :], wt[:, kc, :], h_fm[:, kc, b * 512:(b + 1) * 512], [wt] + h_fm.r(kc), bk, start=(kc == 0), stop=(kc == KC - 1))

    def layer1(g, from_dram=False):
        off, lens, cond, Tg, nt, nblk = ginfo(g)
        ta, tb, ksum = cf["ta"], cf["tb"], cf["ksum"]
        ident, ones = C["ident"], C["ones"]
        phase1(1, g, from_dram)
        seglen = 64 if g == "S" else 256
        nseg = Tg // seglen
        segw = seglen + CK - 1
        spb = 512 // seglen
        zpad_w = Vw(zpad_t[:, 0:nseg * segw], zpad_t.toks)
        zpad3 = zpad_w[:, :].rearrange("p (s w) -> p s w", w=segw)
        X("dve", "tensor_scalar", zpad_w[:, :], ones[:, 0:1].to_broadcast([128, nseg * segw]), 0.0, None, ALU.mult, rd=[ones], wr=[zpad_w])
        dgflat = wpall[:, :, :, :].rearrange("p a b c -> p (a b c)")[:, 0:CK * 128]
        dg = Vw(dgflat.rearrange("p (k c) -> p k c", c=128), wpall.toks)
        zc = Vw(out_fm[:, :, :].bitcast(F32), out_fm.toks)
        cw = Dr["cv_w_in"]

        def load_conv_chunk(e_idx):
            sp_ = (e_idx % 2) * 2
            load_w_plain(sp_, cw[:, e_idx * 128:(e_idx + 1) * 128])
            load_w_plain(sp_ + 1, cw[:, D + e_idx * 128:D + (e_idx + 1) * 128])

        load_conv_chunk(0)
        for e_idx in range(KC):
            sp_ = (e_idx % 2) * 2
            project_plain(g, sp_, 0)
            project_plain(g, sp_ + 1, 2)
            if e_idx + 1 < KC:
                load_conv_chunk(e_idx + 1)
            X("dve", "tensor_tensor", dg[:, :, :].bitcast(F32R), ident[:, :].unsqueeze(1).to_broadcast([128, CK, 128]),
              cvdw[:, e_idx, :].unsqueeze(2).to_broadcast([128, CK, 128]), ALU.mult, rd=[ident, cvdw], wr=[dg])
            for b in range(nblk):
                sl = slice(b * 512, (b + 1) * 512)
                bv, bg = banks[b], banks[2 + b]
                X("act", "activation", ta[:, sl], bg[:, :], AF.Exp, scale=-1.0, rd=[bg], wr=[ta])
                X("act", "activation", ta[:, sl], ta[:, sl], AF.Ln, bias=one_col[:], rd=[ta, one_col], wr=[ta])
                X("act", "activation", ta[:, sl], ta[:, sl], AF.Exp, scale=-1.0, rd=[ta], wr=[ta])
                X("dve", "tensor_tensor", zpad3[:, b * spb:(b + 1) * spb, CK // 2:CK // 2 + seglen],
                  bv[:, :].rearrange("p (s l) -> p s l", l=seglen), ta[:, sl].rearrange("p (s l) -> p s l", l=seglen), ALU.mult,
                  rd=[bv, ta], wr=[zpad_w])
            for b in range(nblk):
                sl = slice(b * 512, (b + 1) * 512)
                bk = banks[4 + b]
                for k in range(CK):
                    mm(bk[:, :].rearrange("p (s l) -> p s l", l=seglen), dg[:, k, :].bitcast(F32R), zpad3[:, b * spb:(b + 1) * spb, k:k + seglen],
                       [dg, zpad_w], bk, start=(k == 0), stop=(k == CK - 1))
                X("act", "activation", out_fm[:, e_idx, sl], bk[:, :], AF.Identity, bias=cvcols[:, e_idx, 0:1], rd=[bk, cvcols], wr=out_fm.r(e_idx))
        for b in range(nblk):
            sl = slice(b * 512, (b + 1) * 512)
            for kc in range(KC):
                mm(banks[0][:, :], ones[:, :], zc[:, kc, sl], [ones] + out_fm.r(kc), banks[0], start=(kc == 0), stop=(kc == KC - 1))
            for kc in range(KC):
                hs = slice((kc % 2) * 512, (kc % 2 + 1) * 512)
                X("act", "activation", ta[:, hs], zc[:, kc, sl], AF.Square, rd=out_fm.r(kc), wr=[ta])
                mm(banks[1][:, :], ones[:, :], ta[:, hs], [ones, ta], banks[1], start=(kc == 0), stop=(kc == KC - 1))
            X("act", "activation", lw_fm[:, sl], banks[0][:, :], AF.Identity, scale=1.0 / D, rd=[banks[0]], wr=[lw_fm])
            X("dve", "tensor_tensor", tb[:, sl], lw_fm[:, sl], lw_fm[:, sl], ALU.mult, rd=[lw_fm], wr=[tb])
            X("dve", "scalar_tensor_tensor", tb[:, sl], banks[1][:, :], 1.0 / D, tb[:, sl], ALU.mult, ALU.subtract, rd=[banks[1], tb], wr=[tb])
            X("act", "activation", tb[:, sl], tb[:, sl], AF.Ln, bias=eps_cols["ln"][:], rd=[tb, eps_cols["ln"]], wr=[tb])
            X("act", "activation", la_fm[:, sl], tb[:, sl], AF.Exp, scale=-0.5, rd=[tb], wr=[la_fm])
        load_w_plain(0, cw[:, 2 * D:2 * D + 128])
        for e_idx in range(KC):
            slot = e_idx % 4
            if e_idx + 1 < KC:
                load_w_plain((e_idx + 1) % 4, cw[:, 2 * D + (e_idx + 1) * 128:2 * D + (e_idx + 2) * 128])
            project_plain(g, slot, 2)
            for b in range(nblk):
                sl = slice(b * 512, (b + 1) * 512)
                bk = banks[2 + b]
                X("act", "activation", ksum[:, sl], bk[:, :], AF.Exp, scale=-1.0, rd=[bk], wr=[ksum])
                X("act", "activation", ksum[:, sl], ksum[:, sl], AF.Ln, bias=one_col[:], rd=[ksum, one_col], wr=[ksum])
                X("act", "activation", ksum[:, sl], ksum[:, sl], AF.Exp, scale=-1.0, rd=[ksum], wr=[ksum])
                X("dve", "tensor_tensor", ksum[:, sl], ksum[:, sl], bk[:, :], ALU.mult, rd=[ksum, bk], wr=[ksum])
            X("dve", "tensor_tensor", ta[:, 0:Tg], zc[:, e_idx, 0:Tg], lw_fm[:, 0:Tg], ALU.subtract, rd=out_fm.r(e_idx) + [lw_fm], wr=[ta])
            X("dve", "tensor_tensor", ta[:, 0:Tg], ta[:, 0:Tg], la_fm[:, 0:Tg], ALU.mult, rd=[ta, la_fm], wr=[ta])
            X("act", "activation", tb[:, 0:Tg], ta[:, 0:Tg], AF.Identity, bias=cvcols[:, e_idx, 2:3], scale=cvcols[:, e_idx, 1:2], rd=[ta, cvcols], wr=[tb])
            X("act", "activation", ta[:, 0:Tg], tb[:, 0:Tg], AF.Exp, scale=-1.0, rd=[tb], wr=[ta])
            X("act", "activation", ta[:, 0:Tg], ta[:, 0:Tg], AF.Ln, bias=one_col[:], rd=[ta, one_col], wr=[ta])
            X("act", "activation", ta[:, 0:Tg], ta[:, 0:Tg], AF.Exp, scale=-1.0, rd=[ta], wr=[ta])
            X("dve", "tensor_tensor", tb[:, 0:Tg], tb[:, 0:Tg], ta[:, 0:Tg], ALU.mult, rd=[ta, tb], wr=[tb])
            X("dve", "tensor_tensor", out_fm[:, e_idx, 0:Tg], tb[:, 0:Tg], ksum[:, 0:Tg], ALU.mult, rd=[tb, ksum], wr=out_fm.r(e_idx))
        phase3(1, g, Dr["cv_w_out"], False, True)

    if stage == "full":
        for g in ("S", "P"):
            layer0(g)
            layer1(g)
        P.emit(out_ops)
        print("ops", len(P.ops), "sems", P.n_sems, "eng_cnt", P.eng_cnt, "sb_bytes", P.sb_bytes, flush=True)
        P.close()
        return nc
    if stage == "l1":
        import os
        g = os.environ.get("WKV_G", "P")
        layer1(g, True)
        P.emit(out_ops)
        P.close()
        return nc
    if stage == "l0":
        import os
        g = os.environ.get("WKV_G", "P")
        layer0(g)
        nt = ginfo(g)[4]
        for ti in range(nt):
            dbg_dump(xg[:, ti, :], xg.r(ti), ti * D, D)
        P.emit(out_ops)
        print("ops", len(P.ops), "sems", P.n_sems, "eng_cnt", P.eng_cnt, "sb_bytes", P.sb_bytes)
        P.close()
        return nc
    if stage == "wkv":
        import os
        g = os.environ.get("WKV_G", "P")
        e_idx = 3
        phase1(0, g, True)
        lora_stage(g)
        load_chunk_weights(e_idx)
        ci = load_chunk_consts(e_idx)
        proj_stage(g, e_idx)
        wkv_chunk(g, e_idx, ci)
        Tg = ginfo(g)[3]
        dbg_dump(out_fm[:, e_idx, 0:Tg].bitcast(F32), out_fm, 0, Tg)
        dbg_dump(o_acc[:, 0:Tg // 128, :].rearrange("p t c -> p (t c)"), o_acc, Tg, Tg)
        P.emit(out_ops)
        print("ops", len(P.ops), "sems", P.n_sems, "eng_cnt", P.eng_cnt, "sb_bytes", P.sb_bytes)
        P.close()
        return nc
    if stage == "proj":
        import os
        cut = int(os.environ.get("PROJ_CUT", "9"))
        g = "P"
        phase1(0, g, True)
        if cut >= 2:
            load_w(0, Dr["w1cat"], 4)
        if cut >= 3:
            load_w(1, Dr["a1cat"], 5)
            project(g, 0, lw_fm)
        if cut >= 4:
            lora_stage(g)
        if cut >= 5:
            load_chunk_weights(3)
            proj_stage(g, 3)
        Tg = ginfo(g)[3]
        col = 0
        for t in (lw_fm, la_fm, cf["r"], cf["k"], cf["v"], cf["sg"]):
            dbg_dump(t[:, 0:Tg], t, col, Tg)
            col += Tg
        dbg_dump(h_fm[:, 5, 0:Tg].bitcast(F32), h_fm, col, Tg)
        P.emit(out_ops)
        P.close()
        return nc
    return nc


_NC_CACHE = {}


def kernel(**inputs):
    maps = prep_inputs(inputs)
    if "nc" not in _NC_CACHE:
        _NC_CACHE["nc"] = build(stage="full")
    nc = _NC_CACHE["nc"]
    res = run_bass_kernel_spmd(nc, maps, core_ids=list(range(NCORES)))
    y_prompt = np.zeros((16, 256, D), np.float32)
    y_sample = np.zeros((8, 1024, D), np.float32)
    new_state = np.zeros((16, 1, 2, NH, HD, HD), np.float32)
    for i in range(NCORES):
        r = res.results[i]
        yo = np.asarray(r["y_out"])
        y_prompt[2 * i] = yo[0:256]
        y_prompt[2 * i + 1] = yo[256:512]
        y_sample[i] = yo[512:1536]
        st = np.asarray(r["st_out"])
        for si in range(2):
            new_state[2 * i + si, 0] = st[si].transpose(0, 1, 3, 2)
    return (y_prompt, y_sample, new_state)
```

```python
import numpy as np
from contextlib import ExitStack
import concourse.bass as bass
import concourse.mybir as mybir
from concourse.bass_utils import run_bass_kernel_spmd

F32 = mybir.dt.float32
F32R = mybir.dt.float32r
AF = mybir.ActivationFunctionType
ALU = mybir.AluOpType
AX = mybir.AxisListType

ENGS = ("pe", "act", "dve", "pool", "sp")
NCORES = 8
D = 1024
KC = 8
NH = 16
HD = 64
CK = 31
RMS_EPS = 1e-6
GN_EPS = 64e-5
LN_EPS = 1e-5
DEC_C = float(np.exp(-0.5))
import os as _os
WDT = F32 if _os.environ.get('WKV_F32') == '1' else F32R
YDT = F32R if _os.environ.get('WKV_YR') == '1' else F32


class Tok:
    __slots__ = ("name", "lw", "rd", "excl")

    def __init__(self, name="", excl=False):
        self.name = name
        self.lw = None
        self.rd = []
        self.excl = excl


class Op:
    __slots__ = ("eng", "fn", "deps", "signal", "semkey", "semval", "idx", "isdma", "waits")

    def __init__(self, eng, fn, isdma=False, semkey=None):
        self.eng = eng
        self.fn = fn
        self.deps = []
        self.signal = False
        self.semkey = semkey
        self.semval = None
        self.isdma = isdma
        self.waits = []


class Tl:
    def __init__(self, t, name, nreg=1, excl=False):
        self.t = t
        self.toks = [Tok(f"{name}.{i}", excl) for i in range(nreg)]

    def __getitem__(self, k):
        return self.t[k]

    def all(self):
        return list(self.toks)

    def r(self, i):
        return [self.toks[i]]


class Vw(Tl):
    def __init__(self, ap, tk):
        self.t = ap
        self.toks = list(tk)


class Prog:
    def __init__(self, nc):
        self.nc = nc
        self.ops = []
        self.stack = ExitStack()
        self.nt = 0
        self.sb_bytes = 0
        self.zcol = None

    def sb(self, shape, dtype=F32, name=None, nreg=1, zero=True):
        self.nt += 1
        name = (name or "t") + f"_{self.nt}"
        t = self.stack.enter_context(self.nc.sbuf_tensor(name, list(shape), dtype))
        self.sb_bytes += int(np.prod(shape[1:])) * 4
        tl = Tl(t, name, nreg)
        if not zero:
            return tl
        if self.zcol is None:
            self.zcol = tl
            self.op("dve", lambda e: e.memset(t[:], 0.0), writes=tl.toks)
        else:
            eng = ("dve", "pool")[self.nt % 2]
            if dtype == F32:
                self.op(eng, lambda e: e.memset(t[:], 0.0), writes=tl.toks)
            else:
                z = self.zcol.t[0:shape[0], 0:1]
                for _ in range(len(shape) - 2):
                    z = z.unsqueeze(2)
                zb = z.to_broadcast(list(shape))
                self.op(eng, lambda e: e.tensor_scalar(t[:], zb, 0.0, None, ALU.mult), reads=self.zcol.toks, writes=tl.toks)
        return tl

    def ps(self, shape, dtype=F32, name=None, nreg=1):
        self.nt += 1
        name = (name or "p") + f"_{self.nt}"
        t = self.stack.enter_context(self.nc.psum_tensor(name, list(shape), dtype))
        return Tl(t, name, nreg, excl=True)

    def op(self, eng, fn, reads=(), writes=(), isdma=False, semkey=None):
        o = Op(eng, fn, isdma, semkey)
        o.idx = len(self.ops)
        deps = set()
        for r in reads:
            if r.lw is not None:
                deps.add(r.lw)
            if r.excl:
                for x in r.rd:
                    if x.eng != eng:
                        deps.add(x)
        for w in writes:
            if w.lw is not None:
                deps.add(w.lw)
            for x in w.rd:
                deps.add(x)
        for r in reads:
            r.rd.append(o)
        for w in writes:
            w.lw = o
            w.rd = []
        deps.discard(o)
        for d in deps:
            if d.eng == "pe" and eng == "pe" and not d.isdma and not isdma:
                continue
            if d.isdma and isdma and d.semkey == semkey and semkey.startswith("all:"):
                continue
            o.deps.append(d)
            d.signal = True
        self.ops.append(o)
        return o

    def emit(self, final_wait_ops=()):
        nc = self.nc
        eng_cnt = {e: 0 for e in ENGS}
        dma_cnt = {}
        for o in final_wait_ops:
            o.signal = True
        for o in self.ops:
            if o.isdma:
                o.signal = True
            if not o.signal:
                continue
            if o.isdma:
                k = o.semkey
                dma_cnt[k] = dma_cnt.get(k, 0) + 16
                o.semval = dma_cnt[k]
            else:
                eng_cnt[o.eng] += 1
                o.semval = eng_cnt[o.eng]
        for o in self.ops:
            if o.isdma and o.semkey.startswith("all:"):
                o.semval = dma_cnt[o.semkey]
        eng_sem = {}
        dma_sem = {}
        for e in ENGS:
            eng_sem[e] = self.stack.enter_context(nc.semaphore("sem_" + e))
        for i, k in enumerate(dma_cnt):
            dma_sem[k] = self.stack.enter_context(nc.semaphore("dsem_%d" % i))
        self.n_sems = len(eng_sem) + len(dma_sem)
        self.eng_cnt = eng_cnt

        def semof(o):
            return dma_sem[o.semkey] if o.isdma else eng_sem[o.eng]

        per_eng = {e: [] for e in ENGS}
        for o in self.ops:
            per_eng[o.eng].append(o)
        waited = {e: {} for e in ENGS}
        for o in self.ops:
            w = waited[o.eng]
            need = {}
            for d in o.deps:
                s = semof(d)
                key = id(s)
                v = d.semval
                if w.get(key, 0) >= v:
                    continue
                if key not in need or need[key][1] < v:
                    need[key] = (s, v)
            for key, (s, v) in need.items():
                w[key] = v
                o.waits.append((s, v))
        finals = [(semof(o), o.semval) for o in final_wait_ops]
        block = self.stack.enter_context(nc.Block())

        def run(engobj, lst, is_sp=False):
            for o in lst:
                for (s, v) in o.waits:
                    engobj.wait_ge(s, v)
                ins = o.fn(engobj)
                if o.signal:
                    ins.then_inc(semof(o), 16 if o.isdma else 1)
            if is_sp:
                for (s, v) in finals:
                    engobj.wait_ge(s, v)

        @block.tensor
        def _(e):
            run(e, per_eng["pe"])

        @block.scalar
        def _(e):
            run(e, per_eng["act"])

        @block.vector
        def _(e):
            run(e, per_eng["dve"])

        @block.gpsimd
        def _(e):
            run(e, per_eng["pool"])

        @block.sync
        def _(e):
            run(e, per_eng["sp"], True)

    def close(self):
        self.stack.close()


def make_consts():
    c = {}
    c["ident"] = np.eye(128, dtype=np.float32)
    ob = np.zeros((128, 128), np.float32)
    ob[:64, :64] = 1
    ob[64:, 64:] = 1
    c["ones_blk"] = ob
    p = np.arange(128)[:, None]
    f = np.arange(128)[None, :]
    lt = (p < f).astype(np.float32)
    le = (p <= f).astype(np.float32)
    gt = (p > f).astype(np.float32)
    ge = (p >= f).astype(np.float32)
    c["cmF"] = np.concatenate([le, lt], 1)
    c["cmB"] = np.concatenate([ge, gt], 1)
    c["maskF"] = np.concatenate([-lt, lt, le, -gt, -le], 1)
    c["maskB"] = np.concatenate([-gt, gt, ge, -lt, -ge], 1)
    c["ones"] = np.ones((128, 128), np.float32)
    c["i64x2"] = np.concatenate([np.eye(64), np.eye(64)], 0).astype(np.float32)
    return c


GROUPS = {
    "S": (512, [1024], 1),
    "P": (0, [256, 256], 0),
}


def dram_specs():
    s = {}
    s["xin"] = [1536, D]
    s["condT"] = [128, KC, 2]
    s["st0"] = [2, NH, HD, HD]
    s["ada_w"] = [2, D, 3 * D]
    s["adab_fm"] = [2, 128, 24]
    s["npre_fm"] = [2, 128, KC]
    s["npost_fm"] = [2, 128, KC]
    s["mu_fm"] = [128, 6, KC]
    s["w_in"] = [4, D, D]
    s["w1cat"] = [D, 128]
    s["a1cat"] = [D, 128]
    s["w2cat"] = [128, D]
    s["a2cat"] = [128, D]
    s["colpack"] = [KC, 128, 9]
    s["w_out"] = [D, D]
    s["cv_w_in"] = [D, 3 * D]
    s["cv_dw"] = [KC, 128, CK]
    s["cv_cols"] = [KC, 128, 3]
    s["cv_w_out"] = [D, D]
    for k, v in make_consts().items():
        s[k] = list(v.shape)
    return s


def prep_inputs(inp):
    f = lambda a: np.ascontiguousarray(np.asarray(a, dtype=np.float32))
    sh = {}
    ada_b = np.asarray(inp["ada_b"], np.float32)
    sh["ada_w"] = f(inp["ada_w"])
    sh["adab_fm"] = f(ada_b.reshape(2, 24, 128).transpose(0, 2, 1))
    sh["npre_fm"] = f(np.asarray(inp["norm_pre"]).reshape(2, KC, 128).transpose(0, 2, 1))
    sh["npost_fm"] = f(np.asarray(inp["norm_post"]).reshape(2, KC, 128).transpose(0, 2, 1))
    sh["mu_fm"] = f(np.asarray(inp["rw_mu"])[0].reshape(6, KC, 128).transpose(2, 0, 1))
    sh["w_in"] = f(np.asarray(inp["rw_w_in"])[0])
    w1 = np.asarray(inp["rw_w1"])[0]
    a1 = np.asarray(inp["rw_a1"])[0]
    sh["w1cat"] = f(np.concatenate([w1[0], w1[1]], 1))
    sh["a1cat"] = f(np.concatenate([a1[0], a1[1]], 1))
    sh["w2cat"] = f(np.asarray(inp["rw_w2"])[0].reshape(128, D))
    sh["a2cat"] = f(np.asarray(inp["rw_a2"])[0].reshape(128, D))
    cols = [np.asarray(inp["rw_w0"])[0, 0], np.asarray(inp["rw_w0"])[0, 1],
            np.asarray(inp["rw_a0"])[0, 0], np.asarray(inp["rw_a0"])[0, 1],
            np.asarray(inp["rw_k_k"])[0], np.asarray(inp["rw_k_a"])[0],
            np.asarray(inp["rw_r_k"])[0].reshape(D), np.asarray(inp["rw_lnx_g"])[0],
            np.asarray(inp["rw_lnx_b"])[0]]
    sh["colpack"] = f(np.stack(cols, 1).reshape(KC, 128, 9))
    sh["w_out"] = f(np.asarray(inp["rw_w_out"])[0])
    sh["cv_w_in"] = f(np.asarray(inp["cv_w_in"])[0])
    sh["cv_dw"] = f(np.asarray(inp["cv_dw_w"])[0].T.reshape(KC, 128, CK))
    cc = [np.asarray(inp["cv_dw_b"])[0], np.asarray(inp["cv_ln_g"])[0], np.asarray(inp["cv_ln_b"])[0]]
    sh["cv_cols"] = f(np.stack(cc, 1).reshape(KC, 128, 3))
    sh["cv_w_out"] = f(np.asarray(inp["cv_w_out"])[0])
    sh.update(make_consts())
    xp = np.asarray(inp["x_prompt"], np.float32)
    xs = np.asarray(inp["x_sample"], np.float32)
    st = np.asarray(inp["state_rwkv"], np.float32)
    c = np.asarray(inp["c"], np.float32)
    cctx = np.asarray(inp["c_ctx"], np.float32)
    maps = []
    for i in range(NCORES):
        m = dict(sh)
        m["xin"] = f(np.concatenate([xp[2 * i].reshape(256, D), xp[2 * i + 1].reshape(256, D), xs[i]], 0))
        cond = np.stack([cctx, c[i]], 1)
        m["condT"] = f(cond.reshape(KC, 128, 2).transpose(1, 0, 2))
        m["st0"] = f(st[i, 0].transpose(0, 1, 3, 2))
        maps.append(m)
    return maps


def toks(*xs):
    out = []
    for x in xs:
        if isinstance(x, Tl):
            out.extend(x.toks)
        elif isinstance(x, Tok):
            out.append(x)
        else:
            out.extend(toks(*x))
    return out


def build(stage="full", dbg_shape=None):
    nc = bass.Bass("TRN2", target_bir_lowering=False)
    specs = dram_specs()
    Dr = {k: nc.dram_tensor(k, shp, F32, kind="ExternalInput").ap() for k, shp in specs.items()}
    y_out = nc.dram_tensor("y_out", [1536, D], F32, kind="ExternalOutput").ap()
    st_out = nc.dram_tensor("st_out", [2, 2, NH, HD, HD], F32, kind="ExternalOutput").ap()
    dbg = None
    if dbg_shape is not None:
        dbg = nc.dram_tensor("dbg", list(dbg_shape), F32, kind="ExternalOutput").ap()
    P = Prog(nc)
    out_ops = []
    zcol_t = P.sb([128, 1], F32, name="zcol")

    def dma(eng, out_ap, in_ap, reads, writes, key):
        return P.op(eng, lambda e: e.dma_start(out=out_ap, in_=in_ap), reads=toks(reads), writes=toks(writes),
                    isdma=True, semkey=key)

    def dbg_dump(tile_ap, src, col0, ncols, rows=128):
        o = dma("sp", dbg[0:rows, col0:col0 + ncols], tile_ap, src, [], "dbg%d" % col0)
        out_ops.append(o)

    C = {}
    for k in make_consts():
        C[k] = P.sb(specs[k], F32, name=k, zero=False)
        dma("sp", C[k][:], Dr[k], [], C[k], "all:const")
    one_col = P.sb([128, 1], name="one")
    tiny_col = P.sb([128, 1], name="tiny")
    P.op("dve", lambda e: e.memset(one_col[:], 1.0), writes=toks(one_col))
    P.op("dve", lambda e: e.memset(tiny_col[:], 1e-24), writes=toks(tiny_col))
    eps_cols = {}
    for nm, val in (("rms", RMS_EPS), ("gn", GN_EPS), ("ln", LN_EPS)):
        eps_cols[nm] = P.sb([128, 1], name="eps" + nm)
        P.op("dve", (lambda t, v: lambda e: e.memset(t[:], v))(eps_cols[nm], val), writes=toks(eps_cols[nm]))

    mu_fm = P.sb([128, 6, KC], name="mu_fm", zero=False)
    dma("sp", mu_fm[:], Dr["mu_fm"], [], mu_fm, "all:const")

    cvcols = P.sb([128, KC, 3], name="cvcols", zero=False)
    cvdw = P.sb([128, KC, CK], name="cvdw", zero=False)
    for e_ in range(KC):
        dma("sp", cvcols[:, e_, :], Dr["cv_cols"][e_], [], cvcols, "all:const")
        dma("sp", cvdw[:, e_, :], Dr["cv_dw"][e_], [], cvdw, "all:const")
    muh = P.sb([128, 6, KC], name="muh")
    P.op("dve", lambda e: e.tensor_scalar(muh[:], mu_fm[:], 0.5, None, ALU.mult), reads=toks(mu_fm), writes=toks(muh))

    banks = [P.ps([128, 512], name="bank%d" % i, nreg=1) for i in range(8)]
    for bk_ in banks:
        P.op("dve", (lambda bk_: lambda e: e.memset(bk_[:, :], 0.0))(bk_), writes=toks(bk_))

    def bq(b, c0, c1):
        return list(banks[b].toks)

    def bh(b, half):
        return list(banks[b].toks)

    TG = 1024
    NT = 8
    xg = P.sb([128, NT, D], F32, name="xg", nreg=NT, zero=False)
    h_fm = P.sb([128, KC, TG], F32R, name="h_fm", nreg=KC, zero=False)
    out_fm = P.sb([128, KC, TG], F32R, name="out_fm", nreg=KC, zero=False)

    condT = P.sb([128, KC, 2], name="condT", zero=False)
    dma("sp", condT[:], Dr["condT"], [], condT, "all:const")
    scond = P.sb([128, KC, 2], name="scond")
    stmp = P.sb([128, KC, 2], name="stmp")
    P.op("act", lambda e: e.activation(stmp[:], condT[:], AF.Exp, scale=-1.0), reads=toks(condT), writes=toks(stmp))
    P.op("dve", lambda e: e.tensor_scalar_add(stmp[:], stmp[:], 1.0), reads=toks(stmp), writes=toks(stmp))
    P.op("dve", lambda e: e.reciprocal(stmp[:], stmp[:]), reads=toks(stmp), writes=toks(stmp))
    P.op("dve", lambda e: e.tensor_tensor(scond[:], condT[:], stmp[:], ALU.mult), reads=toks(condT, stmp), writes=toks(scond))
    adab_fm = P.sb([128, 2, 24], name="adab_fm", zero=False)
    npre_fm = P.sb([128, 2, KC], name="npre_fm", zero=False)
    npost_fm = P.sb([128, 2, KC], name="npost_fm", zero=False)
    for l in range(2):
        dma("sp", adab_fm[:, l, :], Dr["adab_fm"][l], [], adab_fm, "all:const")
        dma("sp", npre_fm[:, l, :], Dr["npre_fm"][l], [], npre_fm, "all:const")
        dma("sp", npost_fm[:, l, :], Dr["npost_fm"][l], [], npost_fm, "all:const")
    modfm = P.sb([128, 2, 24, 2], name="modfm")
    scale_col = P.sb([128, 2, KC, 2], name="scale_col")
    gate_col = P.sb([128, 2, KC, 2], name="gate_col")
    stg = [Vw(xg[:, 4 * i:4 * i + 4, :].rearrange("p a (b c) -> p (a b) c", c=512), xg.toks[4 * i:4 * i + 4]) for i in range(2)]
    nstg = 0
    for l in range(2):
        for q in range(6):
            s_ = stg[nstg % 2]
            nstg += 1
            dma("sp", s_[:], Dr["ada_w"][l].rearrange("(kc p) c -> p kc c", p=128)[:, :, q * 512:(q + 1) * 512],
                [], s_, "adastg%d" % (nstg % 2))
            for b4 in range(4):
                cb = q * 4 + b4
                for kc in range(KC):
                    P.op("pe", (lambda s_, cb, b4, kc: lambda e: e.matmul(
                        banks[0][:, cb * 2:(cb + 1) * 2], s_[:, kc, b4 * 128:(b4 + 1) * 128], scond[:, kc, :],
                        start=(kc == 0), stop=(kc == KC - 1)))(s_, cb, b4, kc),
                         reads=toks(s_, scond), writes=toks(banks[0]))
        P.op("dve", (lambda l: lambda e: e.tensor_tensor(
            modfm[:, l, :, :], banks[0][:, 0:48].rearrange("p (c j) -> p c j", j=2),
            adab_fm[:, l, :].unsqueeze(2).to_broadcast([128, 24, 2]), ALU.add))(l),
             reads=toks(adab_fm, banks[0]), writes=toks(modfm))
        P.op("dve", (lambda l: lambda e: e.scalar_tensor_tensor(
            scale_col[:, l, :, :], modfm[:, l, 8:16, :], 1.0,
            npre_fm[:, l, :].unsqueeze(2).to_broadcast([128, KC, 2]), ALU.add, ALU.mult))(l),
             reads=toks(modfm, npre_fm), writes=toks(scale_col))
        P.op("dve", (lambda l: lambda e: e.tensor_tensor(
            gate_col[:, l, :, :], modfm[:, l, 16:24, :],
            npost_fm[:, l, :].unsqueeze(2).to_broadcast([128, KC, 2]), ALU.mult))(l),
             reads=toks(modfm, npost_fm), writes=toks(gate_col))
    if stage == "mod":
        dbg_dump(modfm[:].rearrange("p l c j -> p (l c j)"), modfm, 0, 96)
        dbg_dump(scale_col[:].rearrange("p l c j -> p (l c j)"), scale_col, 96, 32)
        dbg_dump(gate_col[:].rearrange("p l c j -> p (l c j)"), gate_col, 128, 32)
        P.emit(out_ops)
        P.close()
        return nc

    ss = P.sb([128, NT], name="ss")
    rstd = P.sb([128, NT], name="rstd")
    wring = [P.sb([128, KC, 128], F32R, name="wr%d" % i, zero=False) for i in range(4)]
    wpall = P.sb([128, 4, KC, 128], F32R, name="wpall", nreg=4, zero=False)
    wpring = [Vw(wpall[:, i], wpall.r(i)) for i in range(4)]
    lwla = P.sb([128, 2, TG], name="lwla", nreg=2, zero=False)
    lw_fm = Vw(lwla[:, 0, :], lwla.r(0))
    la_fm = Vw(lwla[:, 1, :], lwla.r(1))
    cf = {}
    for i, nm in enumerate(("r", "k", "v", "sg", "kk", "b", "kd", "sw")):
        cf[nm] = Vw(xg[:, i, :], xg.r(i))
    for nm in ("ksum", "ta", "tb"):
        cf[nm] = P.sb([128, TG + 4], name=nm)
    Bpad, tmpd = cf["ta"], cf["tb"]
    P.op("pool", lambda e: e.memset(Bpad[:], 0.0), writes=toks(Bpad))
    P.op("pool", lambda e: e.memset(tmpd[:], 0.0), writes=toks(tmpd))
    xn_buf = [cf["tb"], cf["ksum"]]
    junk = cf["ta"]
    xcnt = [0]

    def ginfo(g):
        off, lens, cond = GROUPS[g]
        Tg = sum(lens)
        return off, lens, cond, Tg, Tg // 128, Tg // 512

    def seq_pad_offsets(lens):
        offs = []
        o = 1
        for L in lens:
            offs.append(o)
            o += L + 2
        return offs, o - 1

    def phase1(l, g, from_dram):
        off, lens, cond, Tg, nt, nblk = ginfo(g)
        P.op("dve", lambda e: e.memset(ss[:], 0.0), writes=toks(ss))
        for ti in range(nt):
            if from_dram:
                dma("sp", xg[:, ti, :], Dr["xin"][off + ti * 128: off + (ti + 1) * 128, :], [], xg.r(ti), "xg%d" % ti)
            P.op("act", (lambda ti: lambda e: e.activation(junk[:, 0:D], xg[:, ti, :], AF.Square, accum_out=ss[:, ti:ti + 1]))(ti),
                 reads=xg.r(ti), writes=toks(junk, ss))
        P.op("act", lambda e: e.activation(rstd[:, 0:nt], ss[:, 0:nt], AF.Ln, bias=eps_cols["rms"][:], scale=1.0 / D),
             reads=toks(ss, eps_cols["rms"]), writes=toks(rstd))
        P.op("act", lambda e: e.activation(rstd[:, 0:nt], rstd[:, 0:nt], AF.Exp, scale=-0.5), reads=toks(rstd), writes=toks(rstd))
        for ti in range(nt):
            xn = xn_buf[ti % 2]
            P.op("dve", (lambda ti, xn: lambda e: e.tensor_scalar(xn[:, 0:D], xg[:, ti, :], rstd[:, ti:ti + 1], None, ALU.mult))(ti, xn),
                 reads=xg.r(ti) + toks(rstd), writes=toks(xn))
            b0 = 0 if ti % 2 == 0 else 2
            for kc in range(KC):
                bk = banks[b0 + kc // 4]
                c0 = (kc % 4) * 128
                P.op("pe", (lambda bk, c0, xn, kc: lambda e: e.transpose(bk[:, c0:c0 + 128], xn[:, kc * 128:(kc + 1) * 128], C["ident"][:]))(bk, c0, xn, kc),
                     reads=toks(xn, C["ident"]), writes=toks(bk))
            for kc in range(KC):
                bk = banks[b0 + kc // 4]
                c0 = (kc % 4) * 128
                if kc % 2 == 0:
                    P.op("act", (lambda bk, c0, kc, ti: lambda e: e.activation(
                        h_fm[:, kc, ti * 128:(ti + 1) * 128], bk[:, c0:c0 + 128], AF.Identity,
                        bias=modfm[:, l, kc, cond:cond + 1], scale=scale_col[:, l, kc, cond:cond + 1]))(bk, c0, kc, ti),
                         reads=toks(bk, modfm, scale_col), writes=h_fm.r(kc))
                else:
                    P.op("dve", (lambda bk, c0, kc, ti: lambda e: e.tensor_scalar(
                        h_fm[:, kc, ti * 128:(ti + 1) * 128], bk[:, c0:c0 + 128],
                        scale_col[:, l, kc, cond:cond + 1], modfm[:, l, kc, cond:cond + 1], ALU.mult, ALU.add))(bk, c0, kc, ti),
                         reads=toks(bk, modfm, scale_col), writes=h_fm.r(kc))

    def load_w(slot, src_ap, mu_idx):
        wt, wp = wring[slot], wpring[slot]
        dma("pool", wt[:], src_ap.rearrange("(kc p) e -> p kc e", p=128), [], wt, "w%d" % slot)
        P.op("pool", lambda e: e.tensor_tensor(wp[:], wt[:], muh[:, mu_idx, :].unsqueeze(2).to_broadcast([128, KC, 128]), ALU.mult),
             reads=toks(wt, muh), writes=toks(wp))

    def project(g, slot, out_t, bset=0):
        off, lens, cond, Tg, nt, nblk = ginfo(g)
        wt, wp = wring[slot], wpring[slot]
        offs, width = seq_pad_offsets(lens)
        for b in range(nblk):
            bA, bB = banks[4 * bset + b], banks[4 * bset + 2 + b]
            for (bk, w) in ((bA, wt), (bB, wp)):
                for kc in range(KC):
                    P.op("pe", (lambda bk, w, kc, b: lambda e: e.matmul(
                        bk[:, :], w[:, kc, :], h_fm[:, kc, b * 512:(b + 1) * 512], start=(kc == 0), stop=(kc == KC - 1)))(bk, w, kc, b),
                         reads=toks(w) + h_fm.r(kc), writes=toks(bk))
            P.op("act", (lambda bA, b: lambda e: e.copy(out_t[:, b * 512:(b + 1) * 512], bA[:, :]))(bA, b),
                 reads=toks(bA), writes=toks(out_t))
            t0 = b * 512
            pos = 0
            for si, L in enumerate(lens):
                lo, hi = max(t0, pos), min(t0 + 512, pos + L)
                if lo < hi:
                    P.op("act", (lambda bB, lo, hi, t0, po: lambda e: e.copy(Bpad[:, po:po + (hi - lo)], bB[:, lo - t0:hi - t0]))(
                        bB, lo, hi, t0, offs[si] + lo - pos), reads=toks(bB), writes=toks(Bpad))
                pos += L
        W = width + 1
        P.op("dve", lambda e: e.tensor_tensor(tmpd[:, 1:W - 1], Bpad[:, 0:W - 2], Bpad[:, 2:W], ALU.add),
             reads=toks(Bpad), writes=toks(tmpd))
        P.op("dve", lambda e: e.scalar_tensor_tensor(tmpd[:, 1:W - 1], Bpad[:, 1:W - 1], -2.0, tmpd[:, 1:W - 1], ALU.mult, ALU.add),
             reads=toks(Bpad, tmpd), writes=toks(tmpd))
        pos = 0
        for si, L in enumerate(lens):
            P.op("dve", (lambda pos, L, po: lambda e: e.tensor_tensor(out_t[:, pos:pos + L], out_t[:, pos:pos + L], tmpd[:, po:po + L], ALU.add))(pos, L, offs[si]),
                 reads=toks(tmpd, out_t), writes=toks(out_t))
            pos += L

    def sigmoid(out_ap, in_ap, rd, wr, tmp_t, Tg, nbias_ap=None, scale=1.0):
        kw = dict(scale=-scale)
        rdx = toks(rd)
        if nbias_ap is not None:
            kw["bias"] = nbias_ap[0]
            rdx = rdx + toks(nbias_ap[1])
        P.op("act", lambda e: e.activation(tmp_t[:, 0:Tg], in_ap, AF.Exp, **kw), reads=rdx, writes=toks(tmp_t))
        P.op("act", lambda e: e.activation(tmp_t[:, 0:Tg], tmp_t[:, 0:Tg], AF.Ln, bias=one_col[:]), reads=toks(tmp_t, one_col), writes=toks(tmp_t))
        P.op("act", lambda e: e.activation(out_ap, tmp_t[:, 0:Tg], AF.Exp, scale=-1.0), reads=toks(tmp_t), writes=toks(wr))

    def zero_pads(g):
        off, lens, cond, Tg, nt, nblk = ginfo(g)
        offs_, width_ = seq_pad_offsets(lens)
        for si_, L_ in enumerate(lens):
            for col in (offs_[si_] - 1, offs_[si_] + L_):
                X("pool", "memset", Bpad[:, col:col + 1], 0.0, wr=[Bpad])

    def lora_stage(g):
        off, lens, cond, Tg, nt, nblk = ginfo(g)
        zero_pads(g)
        load_w(0, Dr["w1cat"], 4)
        load_w(1, Dr["a1cat"], 5)
        project(g, 0, lw_fm, 0)
        project(g, 1, la_fm, 1)
        sigmoid(cf["ta"][:, 0:Tg], lw_fm[:, 0:Tg], lw_fm, cf["ta"], cf["tb"], Tg, scale=2.0)
        P.op("dve", lambda e: e.tensor_scalar(lw_fm[:, 0:Tg], cf["ta"][:, 0:Tg], 2.0, -1.0, ALU.mult, ALU.add),
             reads=toks(cf["ta"]), writes=toks(lw_fm))

    def proj_stage(g, e_idx, prefetch_next=None):
        off, lens, cond, Tg, nt, nblk = ginfo(g)
        zero_pads(g)
        for n, nm in enumerate(("r", "k", "v", "sg")):
            project(g, n, cf[nm], n % 2)
        sigmoid(cf["ta"][:, 0:Tg], cf["sg"][:, 0:Tg], cf["sg"], cf["ta"], cf["tb"], Tg)
        P.op("dve", lambda e: e.tensor_tensor(cf["sg"][:, 0:Tg], cf["sg"][:, 0:Tg], cf["ta"][:, 0:Tg], ALU.mult),
             reads=toks(cf["sg"], cf["ta"]), writes=toks(cf["sg"]))

    def load_chunk_weights(e_idx):
        for n in range(4):
            load_w(n, Dr["w_in"][n][:, e_idx * 128:(e_idx + 1) * 128], n)


    ccb = [P.sb([128, 9], name="cc%d" % i) for i in range(2)]
    w2c = [P.sb([128, 128], name="w2c%d" % i) for i in range(2)]
    a2c = [P.sb([128, 128], name="a2c%d" % i) for i in range(2)]
    dcol = P.sb([128, 8], name="dcol")
    TM_2 = [P.sb([128, 4, 128], WDT, name="TM%d" % i) for i in range(3)]
    LwT = P.sb([128, 128], name="LwT")
    EF = P.sb([128, 384], name="EF")
    ET = P.sb([128, 256], name="ET")
    BK_2 = [P.sb([128, 256], WDT, name="BK%d" % i) for i in range(2)]
    QRP_2 = [P.sb([128, 320], WDT, name="QRP%d" % i) for i in range(2)]
    Zt_2 = [P.sb([128, 2, 128], WDT, name="Zt%d" % i) for i in range(2)]
    AM_2 = [P.sb([128, 2, 448], WDT, name="AM%d" % i, nreg=2) for i in range(2)]
    Kd_2 = [P.sb([128, 128], WDT, name="Kd%d" % i) for i in range(2)]
    YPTall = P.sb([128, 2, 2, 384], YDT, name="YPT", nreg=4)
    YPT = [Vw(YPTall[:, i], YPTall.toks[2 * i:2 * i + 2]) for i in range(2)]
    WU = P.sb([128, 2, 128], WDT, name="WU", nreg=2)
    TinvR = P.sb([128, 2, 128], WDT, name="TinvR", nreg=2)
    zpad_t = P.sb([128, 16 * (64 + CK - 1)], F32R, name="zpad")
    QMs = P.sb([64, 2, 192], WDT, name="QMs", nreg=2)
    ST = [P.sb([64, 2, 64], WDT, name="ST%d" % d) for d in range(2)]
    identR = P.sb([128, 128], WDT, name="identR")
    P.op("dve", lambda e: e.tensor_copy(identR[:], C["ident"][:]), reads=toks(C["ident"]), writes=toks(identR))
    o_acc = P.sb([128, NT, 128], name="o_acc", nreg=NT)
    gstat = P.sb([128, 4, NT * 2], name="gstat")
    ccnt = [0]

    def load_chunk_consts(e_idx):
        i = ccnt[0] % 2
        ccnt[0] += 1
        dma("sp", ccb[i][:], Dr["colpack"][e_idx], [], ccb[i], "cc%d" % i)
        dma("sp", w2c[i][:], Dr["w2cat"][:, e_idx * 128:(e_idx + 1) * 128], [], w2c[i], "w2c%d" % i)
        dma("sp", a2c[i][:], Dr["a2cat"][:, e_idx * 128:(e_idx + 1) * 128], [], a2c[i], "a2c%d" % i)
        return i

    def mm(out_ap, lhsT, rhs, rd, wr, start=True, stop=True):
        P.op("pe", lambda e: e.matmul(out_ap, lhsT, rhs, start=start, stop=stop), reads=toks(rd), writes=toks(wr))

    def X(eng, meth, *args, rd=(), wr=(), **kw):
        P.op(eng, lambda e: getattr(e, meth)(*args, **kw), reads=toks(rd), writes=toks(wr))

    def wkv_chunk(g, e_idx, ci):
        off, lens, cond, Tg, nt, nblk = ginfo(g)
        cc, w2, a2 = ccb[ci], w2c[ci], a2c[ci]
        r_, k_, v_, sg_, kk_, b_, kd_, sw_ = (cf[n] for n in ("r", "k", "v", "sg", "kk", "b", "kd", "sw"))
        ksum, ta, tb = cf["ksum"], cf["ta"], cf["tb"]
        ident, ones_blk, i64x2, ones_c = C["ident"], C["ones_blk"], C["i64x2"], C["ones"]
        X("dve", "tensor_scalar", dcol[:, 0:4], cc[:, 0:4], -1.0, None, ALU.mult, rd=[cc], wr=[dcol])
        X("dve", "tensor_scalar", dcol[:, 4:5], cc[:, 5:6], -1.0, 1.0, ALU.mult, ALU.add, rd=[cc], wr=[dcol])
        X("dve", "tensor_scalar", dcol[:, 5:6], cc[:, 6:7], 0.5, None, ALU.mult, rd=[cc], wr=[dcol])
        X("dve", "tensor_scalar", kk_[:, 0:Tg], k_[:, 0:Tg], cc[:, 4:5], None, ALU.mult, rd=[k_, cc], wr=[kk_])
        X("dve", "tensor_tensor", ta[:, 0:Tg], kk_[:, 0:Tg], kk_[:, 0:Tg], ALU.mult, rd=[kk_], wr=[ta])
        for b in range(nblk):
            sl = slice(b * 512, (b + 1) * 512)
            mm(banks[b][:, :], ones_blk[:], ta[:, sl], [ones_blk, ta], banks[b])
            X("act", "activation", tb[:, sl], banks[b][:, :], AF.Ln, bias=tiny_col[:], rd=[banks[b], tiny_col], wr=[tb])
        X("act", "activation", tb[:, 0:Tg], tb[:, 0:Tg], AF.Exp, scale=-0.5, rd=[tb], wr=[tb])
        X("dve", "tensor_tensor", kk_[:, 0:Tg], kk_[:, 0:Tg], tb[:, 0:Tg], ALU.mult, rd=[kk_, tb], wr=[kk_])

        seq_of_tile = []
        for si, L in enumerate(lens):
            seq_of_tile += [si] * (L // 128)
        first_tile, last_tile = {}, {}
        for ti, si in enumerate(seq_of_tile):
            first_tile.setdefault(si, ti)
            last_tile[si] = ti

        def h3(ap):
            return ap.rearrange("p (h j) -> p h j", h=2)

        def sig_fm(d, wt_, src, ncol_idx, dst):
            rows = slice(d * 64, (d + 1) * 64)
            bb = 0 if d == 0 else 2
            for b in range(nblk):
                sl = slice(b * 512, (b + 1) * 512)
                bk = banks[bb + b]
                mm(bk[:, :], wt_[rows, :], src[rows, sl], [wt_, src], bk)
                X("act", "activation", dst[:, sl], bk[:, :], AF.Exp, bias=dcol[:, ncol_idx:ncol_idx + 1], scale=-1.0, rd=[bk, dcol], wr=[dst])
            X("act", "activation", dst[:, 0:Tg], dst[:, 0:Tg], AF.Ln, bias=one_col[:], rd=[dst, one_col], wr=[dst])
            X("act", "activation", dst[:, 0:Tg], dst[:, 0:Tg], AF.Exp, scale=-1.0, rd=[dst], wr=[dst])

        def bkd_from_a(d):
            X("dve", "tensor_tensor", b_[:, 0:Tg], kk_[:, 0:Tg], ta[:, 0:Tg], ALU.mult, rd=[kk_, ta], wr=[b_])
            X("dve", "tensor_scalar", ta[:, 0:Tg], ta[:, 0:Tg], cc[:, 5:6], dcol[:, 4:5], ALU.mult, ALU.add, rd=[ta, cc, dcol], wr=[ta])
            X("dve", "tensor_tensor", kd_[:, 0:Tg], k_[:, 0:Tg], ta[:, 0:Tg], ALU.mult, rd=[k_, ta], wr=[kd_])
            if d == 0:
                X("pool", "tensor_copy", ksum[:, 0:Tg], kd_[:, 0:Tg], rd=[kd_], wr=[ksum])
            else:
                X("pool", "tensor_tensor", ksum[:, 0:Tg], ksum[:, 0:Tg], kd_[:, 0:Tg], ALU.add, rd=[kd_, ksum], wr=[ksum])

        for d in range(2):
            if d == 0:
                sig_fm(0, a2, la_fm, 2, ta)
                bkd_from_a(0)
                sig_fm(0, w2, lw_fm, 0, sw_)
                sig_fm(1, a2, la_fm, 3, ta)
                sig_fm(1, w2, lw_fm, 1, tb)
                sw_src = sw_
            else:
                bkd_from_a(1)
                sw_src = tb
            cm = C["cmF"] if d == 0 else C["cmB"]
            mEx = Vw(cm[:, 128:256], cm.toks)
            cmo = C["cmB"] if d == 0 else C["cmF"]
            mEd = Vw(cmo[:, 128:256], cmo.toks)
            mask = C["maskF"] if d == 0 else C["maskB"]
            pcc = 127 if d == 0 else 0
            tiles = list(range(nt)) if d == 0 else list(range(nt - 1, -1, -1))

            def head_pieces(ti, ui_):
                pb = ui_ % 2
                TM, BK, QRP, Zt, AM, Kd = TM_2[ui_ % 3], BK_2[pb], QRP_2[pb], Zt_2[pb], AM_2[pb], Kd_2[pb]
                tsl = slice(ti * 128, (ti + 1) * 128)
                cur = YPT[0]
                pcs = []

                def p0():
                    for j, src in enumerate((v_, kk_, b_, kd_)):
                        X("pe", "transpose", banks[4][:, j * 128:(j + 1) * 128], src[:, tsl], ident[:], rd=[src, ident], wr=[banks[4]])
                    X("pe", "transpose", banks[5][:, 0:128], sw_src[:, tsl], ident[:], rd=[sw_src, ident], wr=[banks[5]])
                    X("act", "copy", TM[:, 0:4, :].rearrange("p a b -> p (a b)"), banks[4][:, :], rd=[banks[4]], wr=[TM])
                    X("act", "activation", LwT[:, :], banks[5][:, 0:128], AF.Identity, scale=-DEC_C, rd=[banks[5]], wr=[LwT])
                pcs.append(p0)

                def p1():
                    mm(banks[5][:, 128:384], LwT[:, :], cm[:, :], [LwT, cm], banks[5])
                    mm(banks[4][:, 0:128], mEx[:, :], LwT[:, :], [LwT, mEx], banks[4])
                    mm(banks[4][:, 128:256], mEd[:, :], LwT[:, :], [LwT, mEd], banks[4])
                    X("act", "activation", EF[:, 0:256], banks[5][:, 128:384], AF.Exp, rd=[banks[5]], wr=[EF])
                    X("act", "activation", EF[:, 256:384], banks[5][:, 128:256], AF.Exp, scale=-1.0, rd=[banks[5]], wr=[EF])
                    X("act", "activation", ET[:, :], banks[4][:, 0:256], AF.Exp, rd=[banks[4]], wr=[ET])
                pcs.append(p1)

                def p2():
                    X("dve", "tensor_tensor", BK[:, 0:128], b_[:, tsl], EF[:, 256:384], ALU.mult, rd=[b_, EF], wr=[BK])
                    X("dve", "tensor_tensor", BK[:, 128:256], kd_[:, tsl], EF[:, 256:384], ALU.mult, rd=[kd_, EF], wr=[BK])
                    X("dve", "tensor_tensor", QRP[:, 0:128], kk_[:, tsl], EF[:, 128:256], ALU.mult, rd=[kk_, EF], wr=[QRP])
                    X("dve", "tensor_tensor", QRP[:, 128:256], r_[:, tsl], EF[:, 0:128], ALU.mult, rd=[r_, EF], wr=[QRP])
                pcs.append(p2)

                def p3():
                    X("dve", "tensor_scalar", QRP[:, 256:320], i64x2[:, :], EF[:, pcc:pcc + 1], None, ALU.mult, rd=[i64x2, EF], wr=[QRP])
                    X("dve", "tensor_tensor", Zt[:, :, 0:64], h3(TM[:, 1, :].bitcast(F32)), h3(ET[:, 0:128]), ALU.mult, rd=[TM, ET], wr=[Zt])
                    X("dve", "scalar_tensor_tensor", AM[:, :, 384:448], h3(TM[:, 2, :].bitcast(F32)), -1.0, h3(ET[:, 128:256]), ALU.mult, ALU.mult, rd=[TM, ET], wr=[AM])
                    X("dve", "tensor_tensor", Kd[:, :], TM[:, 3, :].bitcast(F32), ET[:, 128:256], ALU.mult, rd=[TM, ET], wr=[Kd])
                pcs.append(p3)

                def pa(hh):
                    def f():
                        hr = slice(hh * 64, (hh + 1) * 64)
                        bk = banks[1 + 2 * hh]
                        bk2 = banks[0 + 2 * hh]
                        mm(bk[:, 0:128], BK[hr, 0:128], QRP[hr, 0:128], [BK, QRP], bk)
                        mm(bk[:, 384:512], QRP[hr, 0:128], BK[hr, 0:128], [BK, QRP], bk)
                        mm(bk[:, 128:256], BK[hr, 128:256], QRP[hr, 0:128], [BK, QRP], bk)
                        mm(bk[:, 256:384], BK[hr, 128:256], QRP[hr, 128:256], [BK, QRP], bk)
                        mm(bk2[:, 0:128], BK[hr, 0:128], QRP[hr, 128:256], [BK, QRP], bk2)
                    return f

                def pm(hh):
                    def f():
                        bk = banks[1 + 2 * hh]
                        bk2 = banks[0 + 2 * hh]
                        ct = [cur.toks[hh]]
                        X("dve", "tensor_tensor", cur[:, hh, 0:128], bk[:, 0:128], mask[:, 0:128], ALU.mult, rd=[bk, mask], wr=ct)
                        X("dve", "tensor_tensor", cur[:, hh, 256:384], bk[:, 384:512], mask[:, 384:512], ALU.mult, rd=[bk, mask], wr=ct)
                        X("pool", "tensor_tensor", cur[:, hh, 128:256], cur[:, hh, 0:128].bitcast(F32), ident[:, :], ALU.add, rd=ct + [ident], wr=ct)
                        X("dve", "tensor_tensor", AM[:, hh, 0:256], bk[:, 128:384], mask[:, 128:384], ALU.mult, rd=[bk, mask], wr=AM.r(hh))
                        X("dve", "tensor_tensor", AM[:, hh, 256:384], bk2[:, 0:128], mask[:, 512:640], ALU.mult, rd=[bk2, mask], wr=AM.r(hh))
                    return f
                pcs += [pa(0), pa(1), pm(0), pm(1)]
                return pcs

            NLEV = 7

            def doubling_level(lev):
                cur, nxt = (YPT[0], YPT[1]) if lev % 2 == 0 else (YPT[1], YPT[0])
                for hh in range(2):
                    bk = banks[6 + hh]
                    ct = [cur.toks[hh]]
                    Y, Pm, YT = cur[:, hh, 0:128], cur[:, hh, 128:256], cur[:, hh, 256:384]
                    if lev == 0:
                        mm(bk[:, 256:384], Y, YT, ct, bk)
                        mm(bk[:, 0:128], YT, Y, ct, bk)
                    elif lev <= NLEV - 3:
                        mm(bk[:, 256:384], Y, YT, ct, bk)
                        mm(bk[:, 0:256], YT, cur[:, hh, 0:256], ct, bk)
                    elif lev == NLEV - 2:
                        mm(bk[:, 256:384], Y, YT, ct, bk)
                        mm(bk[:, 128:256], YT, Pm, ct, bk)
                    else:
                        mm(bk[:, 128:256], YT, Pm, ct, bk)
                for hh in range(2):
                    bk = banks[6 + hh]
                    ct = [cur.toks[hh]]
                    nt_ = [nxt.toks[hh]]
                    if lev == 0:
                        X("act", "copy", nxt[:, hh, 0:128], bk[:, 0:128], rd=[bk], wr=nt_)
                        X("act", "copy", nxt[:, hh, 256:384], bk[:, 256:384], rd=[bk], wr=nt_)
                        X("pool", "tensor_copy", nxt[:, hh, 128:256], cur[:, hh, 128:256].bitcast(F32), rd=ct, wr=nt_)
                    elif lev <= NLEV - 3:
                        X("act", "copy", nxt[:, hh, :], bk[:, 0:384], rd=[bk], wr=nt_)
                        X("dve", "tensor_tensor", nxt[:, hh, 128:256], nxt[:, hh, 128:256].bitcast(F32), cur[:, hh, 128:256].bitcast(F32), ALU.add, rd=ct + nt_, wr=nt_)
                    elif lev == NLEV - 2:
                        X("act", "copy", nxt[:, hh, 128:384], bk[:, 128:384], rd=[bk], wr=nt_)
                        X("dve", "tensor_tensor", nxt[:, hh, 128:256], nxt[:, hh, 128:256].bitcast(F32), cur[:, hh, 128:256].bitcast(F32), ALU.add, rd=ct + nt_, wr=nt_)
                    else:
                        X("dve", "tensor_tensor", TinvR[:, hh, :], bk[:, 128:256], cur[:, hh, 128:256].bitcast(F32), ALU.add, rd=[bk] + ct, wr=TinvR.r(hh))

            def tail_pieces(ti, ui_):
                pb = ui_ % 2
                TM, BK, QRP, Zt, AM, Kd = TM_2[ui_ % 3], BK_2[pb], QRP_2[pb], Zt_2[pb], AM_2[pb], Kd_2[pb]
                si = seq_of_tile[ti]
                seq_start = (ti == first_tile[si]) if d == 0 else (ti == last_tile[si])
                seq_end = (ti == last_tile[si]) if d == 0 else (ti == first_tile[si])
                tb_ = [banks[1], banks[3]]

                def t0():
                    if seq_start:
                        if g == "S":
                            dma("pool", ST[d][:], Dr["st0"][d, 2 * e_idx:2 * e_idx + 2].rearrange("h j i -> j h i"), [], ST[d], "st%d" % d)
                        else:
                            X("dve", "tensor_scalar", ST[d][:], ones_c[0:64, 0:1].unsqueeze(2).to_broadcast([64, 2, 64]), 0.0, None, ALU.mult, rd=[ones_c], wr=[ST[d]])
                    for hh in range(2):
                        bk = tb_[hh]
                        mm(bk[:, 0:64], AM[:, hh, 0:128], TM[:, 0, hh * 64:(hh + 1) * 64], AM.r(hh) + [TM], bk)
                        X("act", "copy", Zt[:, hh, 64:128], bk[:, 0:64], rd=[bk], wr=[Zt])

                def t0b():
                    for hh in range(2):
                        bk = tb_[hh]
                        mm(bk[:, 64:192], TinvR[:, hh, :], Zt[:, hh, :], TinvR.r(hh) + [Zt], bk)
                        X("act", "copy", WU[:, hh, :], bk[:, 64:192], rd=[bk], wr=WU.r(hh))

                def t1():
                    for hh in range(2):
                        hr = slice(hh * 64, (hh + 1) * 64)
                        bk = tb_[hh]
                        mm(bk[0:64, 192:384], WU[:, hh, 0:64], AM[:, hh, 256:448], AM.r(hh) + WU.r(hh), bk, start=True, stop=False)
                        mm(bk[0:64, 192:384], identR[hr, hr], QRP[hr, 128:320], [identR, QRP], bk, start=False, stop=True)
                        X("act", "copy", QMs[:, hh, :], bk[0:64, 192:384], rd=[bk], wr=QMs.r(hh))

                def t2():
                    for hh in range(2):
                        bk = tb_[hh]
                        vv = TM[:, 0, hh * 64:(hh + 1) * 64]
                        nu0 = WU[:, hh, 64:128]
                        mm(bk[:, 384:448], AM[:, hh, 256:384], nu0, AM.r(hh) + WU.r(hh), bk, start=True, stop=False)
                        mm(bk[:, 384:448], AM[:, hh, 128:256], vv, AM.r(hh) + [TM], bk, start=False, stop=False)
                        mm(bk[:, 384:448], QMs[:, hh, 0:128], ST[d][:, hh, :], QMs.r(hh) + [ST[d]], bk, start=False, stop=True)
                        mm(bk[0:64, 448:512], AM[:, hh, 384:448], nu0, AM.r(hh) + WU.r(hh), bk, start=True, stop=False)
                        mm(bk[0:64, 448:512], Kd[:, hh * 64:(hh + 1) * 64], vv, [Kd, TM], bk, start=False, stop=False)
                        mm(bk[0:64, 448:512], QMs[:, hh, 128:192], ST[d][:, hh, :], QMs.r(hh) + [ST[d]], bk, start=False, stop=True)
                        if d == 0:
                            X("act", "copy", o_acc[:, ti, hh * 64:(hh + 1) * 64], bk[:, 384:448], rd=[bk], wr=o_acc.r(ti))
                        else:
                            X("dve", "tensor_tensor", o_acc[:, ti, hh * 64:(hh + 1) * 64], o_acc[:, ti, hh * 64:(hh + 1) * 64], bk[:, 384:448], ALU.add,
                              rd=[bk] + o_acc.r(ti), wr=o_acc.r(ti))
                        X("act", "copy", ST[d][:, hh, :], bk[0:64, 448:512], rd=[bk], wr=[ST[d]])
                    if seq_end and g == "P":
                        o = dma("sp", st_out[si, d, 2 * e_idx:2 * e_idx + 2].rearrange("h j i -> j h i"), ST[d][:].bitcast(F32), ST[d], [], "sto%d" % d)
                        out_ops.append(o)
                return [t0, t0b, t1, t2]

            for pc in head_pieces(tiles[0], 0):
                pc()
            prev_tail = []
            for ui, ti in enumerate(tiles):
                nxt_pcs = head_pieces(tiles[ui + 1], ui + 1) if ui + 1 < len(tiles) else []
                for lev in range(NLEV):
                    doubling_level(lev)
                    if lev < 4 and prev_tail:
                        prev_tail[lev]()
                    if nxt_pcs:
                        if lev == 1:
                            nxt_pcs[0]()
                        elif lev == 2:
                            nxt_pcs[1]()
                        elif lev == 3:
                            nxt_pcs[2]()
                        elif lev == 4:
                            nxt_pcs[3]()
                        elif lev == 5:
                            nxt_pcs[4]()
                            nxt_pcs[5]()
                if nxt_pcs:
                    nxt_pcs[6]()
                    nxt_pcs[7]()
                prev_tail = tail_pieces(ti, ui)
            for pc in prev_tail:
                pc()

        n2 = nt * 2
        o3 = o_acc[:, 0:nt, :].rearrange("p t (h i) -> p (t h) i", h=2)
        X("dve", "tensor_reduce", gstat[:, 0, 0:n2], o3, AX.X, ALU.add, rd=[o_acc], wr=[gstat])
        sq3 = ta[:, 0:nt * 128].rearrange("p (a i) -> p a i", i=64)
        X("dve", "tensor_tensor", sq3, o3, o3, ALU.mult, rd=[o_acc], wr=[ta])
        X("dve", "tensor_reduce", gstat[:, 1, 0:n2], sq3, AX.X, ALU.add, rd=[ta], wr=[gstat])
        X("dve", "tensor_scalar", gstat[:, 0, 0:n2], gstat[:, 0, 0:n2], 1.0 / 64, None, ALU.mult, rd=[gstat], wr=[gstat])
        X("dve", "tensor_tensor", gstat[:, 2, 0:n2], gstat[:, 0, 0:n2], gstat[:, 0, 0:n2], ALU.mult, rd=[gstat], wr=[gstat])
        X("dve", "scalar_tensor_tensor", gstat[:, 1, 0:n2], gstat[:, 1, 0:n2], 1.0 / 64, gstat[:, 2, 0:n2], ALU.mult, ALU.subtract, rd=[gstat], wr=[gstat])
        X("act", "activation", gstat[:, 3, 0:n2], gstat[:, 1, 0:n2], AF.Ln, bias=eps_cols["gn"][:], rd=[gstat, eps_cols["gn"]], wr=[gstat])
        X("act", "activation", gstat[:, 3, 0:n2], gstat[:, 3, 0:n2], AF.Exp, scale=-0.5, rd=[gstat], wr=[gstat])
        X("dve", "tensor_tensor", o3, o3, gstat[:, 0, 0:n2].unsqueeze(2).to_broadcast([128, n2, 64]), ALU.subtract, rd=[o_acc, gstat], wr=[o_acc])
        X("dve", "tensor_tensor", o3, o3, gstat[:, 3, 0:n2].unsqueeze(2).to_broadcast([128, n2, 64]), ALU.mult, rd=[o_acc, gstat], wr=[o_acc])
        X("dve", "tensor_tensor", ta[:, 0:Tg], r_[:, 0:Tg], ksum[:, 0:Tg], ALU.mult, rd=[r_, ksum], wr=[ta])
        X("dve", "tensor_scalar", ta[:, 0:Tg], ta[:, 0:Tg], dcol[:, 5:6], None, ALU.mult, rd=[ta, dcol], wr=[ta])
        for b in range(nblk):
            sl = slice(b * 512, (b + 1) * 512)
            mm(banks[b][:, :], ones_blk[:], ta[:, sl], [ones_blk, ta], banks[b])
            X("dve", "tensor_tensor", tb[:, sl], banks[b][:, :], v_[:, sl], ALU.mult, rd=[banks[b], v_], wr=[tb])
        X("dve", "tensor_scalar", tb[:, 0:Tg], tb[:, 0:Tg], cc[:, 8:9], None, ALU.add, rd=[tb, cc], wr=[tb])
        for b in range(nblk):
            bk = banks[2 + b]
            for q in range(4):
                ti = b * 4 + q
                X("pe", "transpose", bk[:, q * 128:(q + 1) * 128], o_acc[:, ti, :], ident[:], rd=o_acc.r(ti) + [ident], wr=[bk])
            sl = slice(b * 512, (b + 1) * 512)
            X("dve", "scalar_tensor_tensor", ta[:, sl], bk[:, :], cc[:, 7:8], tb[:, sl], ALU.mult, ALU.add, rd=[bk, cc, tb], wr=[ta])
            X("dve", "tensor_tensor", out_fm[:, e_idx, sl], ta[:, sl], sg_[:, sl], ALU.mult, rd=[ta, sg_], wr=out_fm.r(e_idx))


    ss3 = P.sb([128, 2], name="ss3")
    rs3 = P.sb([128, 1], name="rs3")
    dgt = P.sb([128, 128], name="dgt")

    def phase3(l, g, wout_ap, reload_x, final):
        off, lens, cond, Tg, nt, nblk = ginfo(g)
        ta, tb = cf["ta"], cf["tb"]
        ident, ones = C["ident"], C["ones"]
        dma("pool", h_fm[:, :, :], wout_ap.rearrange("(kc p) e -> p kc e", p=128), [], h_fm, "wout")
        for kc in range(KC):
            X("dve", "tensor_scalar", dgt[:, :], ident[:, :], gate_col[:, l, kc, cond:cond + 1], None, ALU.mult, rd=[ident, gate_col], wr=[dgt])
            bk = banks[4 + kc // 4]
            mm(bk[:, (kc % 4) * 128:(kc % 4 + 1) * 128], ones[:, :], dgt[:, :], [ones, dgt], bk)
            if kc % 4 == 3:
                X("act", "copy", ta[:, (kc // 4) * 512:(kc // 4 + 1) * 512], bk[:, :], rd=[bk], wr=[ta])
        for ti in range(nt):
            b0 = 0 if ti % 2 == 0 else 2
            tsl = slice(ti * 128, (ti + 1) * 128)
            if reload_x:
                dma("sp", xg[:, ti, :], Dr["xin"][off + ti * 128: off + (ti + 1) * 128, :], [], xg.r(ti), "xg%d" % ti)
            for hb in range(2):
                bk = banks[b0 + hb]
                for kc in range(KC):
                    mm(bk[:, :], out_fm[:, kc, tsl], h_fm[:, kc, hb * 512:(hb + 1) * 512], out_fm.r(kc) + toks(h_fm), bk,
                       start=(kc == 0), stop=(kc == KC - 1))
                X("act", "activation", tb[:, hb * 512:(hb + 1) * 512], bk[:, :], AF.Square, accum_out=ss3[:, hb:hb + 1], rd=[bk], wr=[tb, ss3])
            X("dve", "tensor_tensor", rs3[:, :], ss3[:, 0:1], ss3[:, 1:2], ALU.add, rd=[ss3], wr=[rs3])
            X("act", "activation", rs3[:, :], rs3[:, :], AF.Ln, bias=eps_cols["rms"][:], scale=1.0 / D, rd=[rs3, eps_cols["rms"]], wr=[rs3])
            X("act", "activation", rs3[:, :], rs3[:, :], AF.Exp, scale=-0.5, rd=[rs3], wr=[rs3])
            for hb in range(2):
                bk = banks[b0 + hb]
                sl = slice(hb * 512, (hb + 1) * 512)
                X("dve", "scalar_tensor_tensor", tb[:, sl], bk[:, :], rs3[:, 0:1], ta[:, sl], ALU.mult, ALU.mult, rd=[bk, rs3, ta], wr=[tb])
            X("pool", "tensor_tensor", xg[:, ti, :], xg[:, ti, :], tb[:, 0:D], ALU.add, rd=xg.r(ti) + [tb], wr=xg.r(ti))
            if final:
                o = dma("sp", y_out[off + ti * 128: off + (ti + 1) * 128, :], xg[:, ti, :], xg.r(ti), [], "yo%d" % ti)
                out_ops.append(o)

    def layer0(g):
        phase1(0, g, True)
        lora_stage(g)
        load_chunk_weights(0)
        ci = load_chunk_consts(0)
        for e_idx in range(KC):
            proj_stage(g, e_idx)
            if e_idx + 1 < KC:
                load_chunk_weights(e_idx + 1)
                ci_next = load_chunk_consts(e_idx + 1)
            wkv_chunk(g, e_idx, ci)
            ci = ci_next
        phase3(0, g, Dr["w_out"], True, False)


    def load_w_plain(slot, src_ap):
        wt = wring[slot]
        dma("pool", wt[:], src_ap.rearrange("(kc p) e -> p kc e", p=128), [], wt, "w%d" % slot)

    def project_plain(g, slot, bank_base):
        off, lens, cond, Tg, nt, nblk = ginfo(g)
        wt = wring[slot]
        for b in range(nblk):
            bk = banks[bank_base + b]
            for kc in range(KC):
                mm(bk[:, :], wt[:, kc, :], h_fm[:, kc, b * 512:(b + 1) * 512], [wt] + h_fm.r(kc), bk, start=(kc == 0), stop=(kc == KC - 1))

    def layer1(g, from_dram=False):
        off, lens, cond, Tg, nt, nblk = ginfo(g)
        ta, tb, ksum = cf["ta"], cf["tb"], cf["ksum"]
        ident, ones = C["ident"], C["ones"]
        phase1(1, g, from_dram)
        seglen = 64 if g == "S" else 256
        nseg = Tg // seglen
        segw = seglen + CK - 1
        spb = 512 // seglen
        zpad_w = Vw(zpad_t[:, 0:nseg * segw], zpad_t.toks)
        zpad3 = zpad_w[:, :].rearrange("p (s w) -> p s w", w=segw)
        X("dve", "tensor_scalar", zpad_w[:, :], ones[:, 0:1].to_broadcast([128, nseg * segw]), 0.0, None, ALU.mult, rd=[ones], wr=[zpad_w])
        dgflat = wpall[:, :, :, :].rearrange("p a b c -> p (a b c)")[:, 0:CK * 128]
        dg = Vw(dgflat.rearrange("p (k c) -> p k c", c=128), wpall.toks)
        zc = Vw(out_fm[:, :, :].bitcast(F32), out_fm.toks)
        cw = Dr["cv_w_in"]

        def load_conv_chunk(e_idx):
            sp_ = (e_idx % 2) * 2
            load_w_plain(sp_, cw[:, e_idx * 128:(e_idx + 1) * 128])
            load_w_plain(sp_ + 1, cw[:, D + e_idx * 128:D + (e_idx + 1) * 128])

        load_conv_chunk(0)
        for e_idx in range(KC):
            sp_ = (e_idx % 2) * 2
            project_plain(g, sp_, 0)
            project_plain(g, sp_ + 1, 2)
            if e_idx + 1 < KC:
                load_conv_chunk(e_idx + 1)
            X("dve", "tensor_tensor", dg[:, :, :].bitcast(F32R), ident[:, :].unsqueeze(1).to_broadcast([128, CK, 128]),
              cvdw[:, e_idx, :].unsqueeze(2).to_broadcast([128, CK, 128]), ALU.mult, rd=[ident, cvdw], wr=[dg])
            for b in range(nblk):
                sl = slice(b * 512, (b + 1) * 512)
                bv, bg = banks[b], banks[2 + b]
                X("act", "activation", ta[:, sl], bg[:, :], AF.Exp, scale=-1.0, rd=[bg], wr=[ta])
                X("act", "activation", ta[:, sl], ta[:, sl], AF.Ln, bias=one_col[:], rd=[ta, one_col], wr=[ta])
                X("act", "activation", ta[:, sl], ta[:, sl], AF.Exp, scale=-1.0, rd=[ta], wr=[ta])
                X("dve", "tensor_tensor", zpad3[:, b * spb:(b + 1) * spb, CK // 2:CK // 2 + seglen],
                  bv[:, :].rearrange("p (s l) -> p s l", l=seglen), ta[:, sl].rearrange("p (s l) -> p s l", l=seglen), ALU.mult,
                  rd=[bv, ta], wr=[zpad_w])
            for b in range(nblk):
                sl = slice(b * 512, (b + 1) * 512)
                bk = banks[4 + b]
                for k in range(CK):
                    mm(bk[:, :].rearrange("p (s l) -> p s l", l=seglen), dg[:, k, :].bitcast(F32R), zpad3[:, b * spb:(b + 1) * spb, k:k + seglen],
                       [dg, zpad_w], bk, start=(k == 0), stop=(k == CK - 1))
                X("act", "activation", out_fm[:, e_idx, sl], bk[:, :], AF.Identity, bias=cvcols[:, e_idx, 0:1], rd=[bk, cvcols], wr=out_fm.r(e_idx))
        for b in range(nblk):
            sl = slice(b * 512, (b + 1) * 512)
            for kc in range(KC):
                mm(banks[0][:, :], ones[:, :], zc[:, kc, sl], [ones] + out_fm.r(kc), banks[0], start=(kc == 0), stop=(kc == KC - 1))
            for kc in range(KC):
                hs = slice((kc % 2) * 512, (kc % 2 + 1) * 512)
                X("act", "activation", ta[:, hs], zc[:, kc, sl], AF.Square, rd=out_fm.r(kc), wr=[ta])
                mm(banks[1][:, :], ones[:, :], ta[:, hs], [ones, ta], banks[1], start=(kc == 0), stop=(kc == KC - 1))
            X("act", "activation", lw_fm[:, sl], banks[0][:, :], AF.Identity, scale=1.0 / D, rd=[banks[0]], wr=[lw_fm])
            X("dve", "tensor_tensor", tb[:, sl], lw_fm[:, sl], lw_fm[:, sl], ALU.mult, rd=[lw_fm], wr=[tb])
            X("dve", "scalar_tensor_tensor", tb[:, sl], banks[1][:, :], 1.0 / D, tb[:, sl], ALU.mult, ALU.subtract, rd=[banks[1], tb], wr=[tb])
            X("act", "activation", tb[:, sl], tb[:, sl], AF.Ln, bias=eps_cols["ln"][:], rd=[tb, eps_cols["ln"]], wr=[tb])
            X("act", "activation", la_fm[:, sl], tb[:, sl], AF.Exp, scale=-0.5, rd=[tb], wr=[la_fm])
        load_w_plain(0, cw[:, 2 * D:2 * D + 128])
        for e_idx in range(KC):
            slot = e_idx % 4
            if e_idx + 1 < KC:
                load_w_plain((e_idx + 1) % 4, cw[:, 2 * D + (e_idx + 1) * 128:2 * D + (e_idx + 2) * 128])
            project_plain(g, slot, 2)
            for b in range(nblk):
                sl = slice(b * 512, (b + 1) * 512)
                bk = banks[2 + b]
                X("act", "activation", ksum[:, sl], bk[:, :], AF.Exp, scale=-1.0, rd=[bk], wr=[ksum])
                X("act", "activation", ksum[:, sl], ksum[:, sl], AF.Ln, bias=one_col[:], rd=[ksum, one_col], wr=[ksum])
                X("act", "activation", ksum[:, sl], ksum[:, sl], AF.Exp, scale=-1.0, rd=[ksum], wr=[ksum])
                X("dve", "tensor_tensor", ksum[:, sl], ksum[:, sl], bk[:, :], ALU.mult, rd=[ksum, bk], wr=[ksum])
            X("dve", "tensor_tensor", ta[:, 0:Tg], zc[:, e_idx, 0:Tg], lw_fm[:, 0:Tg], ALU.subtract, rd=out_fm.r(e_idx) + [lw_fm], wr=[ta])
            X("dve", "tensor_tensor", ta[:, 0:Tg], ta[:, 0:Tg], la_fm[:, 0:Tg], ALU.mult, rd=[ta, la_fm], wr=[ta])
            X("act", "activation", tb[:, 0:Tg], ta[:, 0:Tg], AF.Identity, bias=cvcols[:, e_idx, 2:3], scale=cvcols[:, e_idx, 1:2], rd=[ta, cvcols], wr=[tb])
            X("act", "activation", ta[:, 0:Tg], tb[:, 0:Tg], AF.Exp, scale=-1.0, rd=[tb], wr=[ta])
            X("act", "activation", ta[:, 0:Tg], ta[:, 0:Tg], AF.Ln, bias=one_col[:], rd=[ta, one_col], wr=[ta])
            X("act", "activation", ta[:, 0:Tg], ta[:, 0:Tg], AF.Exp, scale=-1.0, rd=[ta], wr=[ta])
            X("dve", "tensor_tensor", tb[:, 0:Tg], tb[:, 0:Tg], ta[:, 0:Tg], ALU.mult, rd=[ta, tb], wr=[tb])
            X("dve", "tensor_tensor", out_fm[:, e_idx, 0:Tg], tb[:, 0:Tg], ksum[:, 0:Tg], ALU.mult, rd=[tb, ksum], wr=out_fm.r(e_idx))
        phase3(1, g, Dr["cv_w_out"], False, True)

    if stage == "full":
        for g in ("S", "P"):
            layer0(g)
            layer1(g)
        P.emit(out_ops)
        print("ops", len(P.ops), "sems", P.n_sems, "eng_cnt", P.eng_cnt, "sb_bytes", P.sb_bytes, flush=True)
        P.close()
        return nc
    if stage == "l1":
        import os
        g = os.environ.get("WKV_G", "P")
        layer1(g, True)
        P.emit(out_ops)
        P.close()
        return nc
    if stage == "l0":
        import os
        g = os.environ.get("WKV_G", "P")
        layer0(g)
        nt = ginfo(g)[4]
        for ti in range(nt):
            dbg_dump(xg[:, ti, :], xg.r(ti), ti * D, D)
        P.emit(out_ops)
        print("ops", len(P.ops), "sems", P.n_sems, "eng_cnt", P.eng_cnt, "sb_bytes", P.sb_bytes)
        P.close()
        return nc
    if stage == "wkv":
        import os
        g = os.environ.get("WKV_G", "P")
        e_idx = 3
        phase1(0, g, True)
        lora_stage(g)
        load_chunk_weights(e_idx)
        ci = load_chunk_consts(e_idx)
        proj_stage(g, e_idx)
        wkv_chunk(g, e_idx, ci)
        Tg = ginfo(g)[3]
        dbg_dump(out_fm[:, e_idx, 0:Tg].bitcast(F32), out_fm, 0, Tg)
        dbg_dump(o_acc[:, 0:Tg // 128, :].rearrange("p t c -> p (t c)"), o_acc, Tg, Tg)
        P.emit(out_ops)
        print("ops", len(P.ops), "sems", P.n_sems, "eng_cnt", P.eng_cnt, "sb_bytes", P.sb_bytes)
        P.close()
        return nc
    if stage == "proj":
        import os
        cut = int(os.environ.get("PROJ_CUT", "9"))
        g = "P"
        phase1(0, g, True)
        if cut >= 2:
            load_w(0, Dr["w1cat"], 4)
        if cut >= 3:
            load_w(1, Dr["a1cat"], 5)
            project(g, 0, lw_fm)
        if cut >= 4:
            lora_stage(g)
        if cut >= 5:
            load_chunk_weights(3)
            proj_stage(g, 3)
        Tg = ginfo(g)[3]
        col = 0
        for t in (lw_fm, la_fm, cf["r"], cf["k"], cf["v"], cf["sg"]):
            dbg_dump(t[:, 0:Tg], t, col, Tg)
            col += Tg
        dbg_dump(h_fm[:, 5, 0:Tg].bitcast(F32), h_fm, col, Tg)
        P.emit(out_ops)
        P.close()
        return nc
    return nc


_NC_CACHE = {}


def kernel(**inputs):
    maps = prep_inputs(inputs)
    if "nc" not in _NC_CACHE:
        _NC_CACHE["nc"] = build(stage="full")
    nc = _NC_CACHE["nc"]
    res = run_bass_kernel_spmd(nc, maps, core_ids=list(range(NCORES)))
    y_prompt = np.zeros((16, 256, D), np.float32)
    y_sample = np.zeros((8, 1024, D), np.float32)
    new_state = np.zeros((16, 1, 2, NH, HD, HD), np.float32)
    for i in range(NCORES):
        r = res.results[i]
        yo = np.asarray(r["y_out"])
        y_prompt[2 * i] = yo[0:256]
        y_prompt[2 * i + 1] = yo[256:512]
        y_sample[i] = yo[512:1536]
        st = np.asarray(r["st_out"])
        for si in range(2):
            new_state[2 * i + si, 0] = st[si].transpose(0, 1, 3, 2)
    return (y_prompt, y_sample, new_state)
```

```python
import numpy as np
from contextlib import ExitStack
import concourse.bass as bass
import concourse.mybir as mybir
from concourse.bass_utils import run_bass_kernel_spmd

F32 = mybir.dt.float32
F32R = mybir.dt.float32r
AF = mybir.ActivationFunctionType
ALU = mybir.AluOpType
AX = mybir.AxisListType

ENGS = ("pe", "act", "dve", "pool", "sp")
NCORES = 8
D = 1024
KC = 8
NH = 16
HD = 64
CK = 31
RMS_EPS = 1e-6
GN_EPS = 64e-5
LN_EPS = 1e-5
DEC_C = float(np.exp(-0.5))
import os as _os
WDT = F32 if _os.environ.get('WKV_F32') == '1' else F32R
YDT = F32R if _os.environ.get('WKV_YR') == '1' else F32


class Tok:
    __slots__ = ("name", "lw", "rd", "excl")

    def __init__(self, name="", excl=False):
        self.name = name
        self.lw = None
        self.rd = []
        self.excl = excl


class Op:
    __slots__ = ("eng", "fn", "deps", "signal", "semkey", "semval", "idx", "isdma", "waits")

    def __init__(self, eng, fn, isdma=False, semkey=None):
        self.eng = eng
        self.fn = fn
        self.deps = []
        self.signal = False
        self.semkey = semkey
        self.semval = None
        self.isdma = isdma
        self.waits = []


class Tl:
    def __init__(self, t, name, nreg=1, excl=False):
        self.t = t
        self.toks = [Tok(f"{name}.{i}", excl) for i in range(nreg)]

    def __getitem__(self, k):
        return self.t[k]

    def all(self):
        return list(self.toks)

    def r(self, i):
        return [self.toks[i]]


class Vw(Tl):
    def __init__(self, ap, tk):
        self.t = ap
        self.toks = list(tk)


class Prog:
    def __init__(self, nc):
        self.nc = nc
        self.ops = []
        self.stack = ExitStack()
        self.nt = 0
        self.sb_bytes = 0
        self.zcol = None

    def sb(self, shape, dtype=F32, name=None, nreg=1, zero=True):
        self.nt += 1
        name = (name or "t") + f"_{self.nt}"
        t = self.stack.enter_context(self.nc.sbuf_tensor(name, list(shape), dtype))
        self.sb_bytes += int(np.prod(shape[1:])) * 4
        tl = Tl(t, name, nreg)
        if not zero:
            return tl
        if self.zcol is None:
            self.zcol = tl
            self.op("dve", lambda e: e.memset(t[:], 0.0), writes=tl.toks)
        else:
            eng = ("dve", "pool")[self.nt % 2]
            if dtype == F32:
                self.op(eng, lambda e: e.memset(t[:], 0.0), writes=tl.toks)
            else:
                z = self.zcol.t[0:shape[0], 0:1]
                for _ in range(len(shape) - 2):
                    z = z.unsqueeze(2)
                zb = z.to_broadcast(list(shape))
                self.op(eng, lambda e: e.tensor_scalar(t[:], zb, 0.0, None, ALU.mult), reads=self.zcol.toks, writes=tl.toks)
        return tl

    def ps(self, shape, dtype=F32, name=None, nreg=1):
        self.nt += 1
        name = (name or "p") + f"_{self.nt}"
        t = self.stack.enter_context(self.nc.psum_tensor(name, list(shape), dtype))
        return Tl(t, name, nreg, excl=True)

    def op(self, eng, fn, reads=(), writes=(), isdma=False, semkey=None):
        o = Op(eng, fn, isdma, semkey)
        o.idx = len(self.ops)
        deps = set()
        for r in reads:
            if r.lw is not None:
                deps.add(r.lw)
            if r.excl:
                for x in r.rd:
                    if x.eng != eng:
                        deps.add(x)
        for w in writes:
            if w.lw is not None:
                deps.add(w.lw)
            for x in w.rd:
                deps.add(x)
        for r in reads:
            r.rd.append(o)
        for w in writes:
            w.lw = o
            w.rd = []
        deps.discard(o)
        for d in deps:
            if d.eng == "pe" and eng == "pe" and not d.isdma and not isdma:
                continue
            if d.isdma and isdma and d.semkey == semkey and semkey.startswith("all:"):
                continue
            o.deps.append(d)
            d.signal = True
        self.ops.append(o)
        return o

    def emit(self, final_wait_ops=()):
        nc = self.nc
        eng_cnt = {e: 0 for e in ENGS}
        dma_cnt = {}
        for o in final_wait_ops:
            o.signal = True
        for o in self.ops:
            if o.isdma:
                o.signal = True
            if not o.signal:
                continue
            if o.isdma:
                k = o.semkey
                dma_cnt[k] = dma_cnt.get(k, 0) + 16
                o.semval = dma_cnt[k]
            else:
                eng_cnt[o.eng] += 1
                o.semval = eng_cnt[o.eng]
        for o in self.ops:
            if o.isdma and o.semkey.startswith("all:"):
                o.semval = dma_cnt[o.semkey]
        eng_sem = {}
        dma_sem = {}
        for e in ENGS:
            eng_sem[e] = self.stack.enter_context(nc.semaphore("sem_" + e))
        for i, k in enumerate(dma_cnt):
            dma_sem[k] = self.stack.enter_context(nc.semaphore("dsem_%d" % i))
        self.n_sems = len(eng_sem) + len(dma_sem)
        self.eng_cnt = eng_cnt

        def semof(o):
            return dma_sem[o.semkey] if o.isdma else eng_sem[o.eng]

        per_eng = {e: [] for e in ENGS}
        for o in self.ops:
            per_eng[o.eng].append(o)
        waited = {e: {} for e in ENGS}
        for o in self.ops:
            w = waited[o.eng]
            need = {}
            for d in o.deps:
                s = semof(d)
                key = id(s)
                v = d.semval
                if w.get(key, 0) >= v:
                    continue
                if key not in need or need[key][1] < v:
                    need[key] = (s, v)
            for key, (s, v) in need.items():
                w[key] = v
                o.waits.append((s, v))
        finals = [(semof(o), o.semval) for o in final_wait_ops]
        block = self.stack.enter_context(nc.Block())

        def run(engobj, lst, is_sp=False):
            for o in lst:
                for (s, v) in o.waits:
                    engobj.wait_ge(s, v)
                ins = o.fn(engobj)
                if o.signal:
                    ins.then_inc(semof(o), 16 if o.isdma else 1)
            if is_sp:
                for (s, v) in finals:
                    engobj.wait_ge(s, v)

        @block.tensor
        def _(e):
            run(e, per_eng["pe"])

        @block.scalar
        def _(e):
            run(e, per_eng["act"])

        @block.vector
        def _(e):
            run(e, per_eng["dve"])

        @block.gpsimd
        def _(e):
            run(e, per_eng["pool"])

        @block.sync
        def _(e):
            run(e, per_eng["sp"], True)

    def close(self):
        self.stack.close()


def make_consts():
    c = {}
    c["ident"] = np.eye(128, dtype=np.float32)
    ob = np.zeros((128, 128), np.float32)
    ob[:64, :64] = 1
    ob[64:, 64:] = 1
    c["ones_blk"] = ob
    p = np.arange(128)[:, None]
    f = np.arange(128)[None, :]
    lt = (p < f).astype(np.float32)
    le = (p <= f).astype(np.float32)
    gt = (p > f).astype(np.float32)
    ge = (p >= f).astype(np.float32)
    c["cmF"] = np.concatenate([le, lt], 1)
    c["cmB"] = np.concatenate([ge, gt], 1)
    c["maskF"] = np.concatenate([-lt, lt, le, -gt, -le], 1)
    c["maskB"] = np.concatenate([-gt, gt, ge, -lt, -ge], 1)
    c["ones"] = np.ones((128, 128), np.float32)
    c["i64x2"] = np.concatenate([np.eye(64), np.eye(64)], 0).astype(np.float32)
    return c


GROUPS = {
    "S": (512, [1024], 1),
    "P": (0, [256, 256], 0),
}


def dram_specs():
    s = {}
    s["xin"] = [1536, D]
    s["condT"] = [128, KC, 2]
    s["st0"] = [2, NH, HD, HD]
    s["ada_w"] = [2, D, 3 * D]
    s["adab_fm"] = [2, 128, 24]
    s["npre_fm"] = [2, 128, KC]
    s["npost_fm"] = [2, 128, KC]
    s["mu_fm"] = [128, 6, KC]
    s["w_in"] = [4, D, D]
    s["w1cat"] = [D, 128]
    s["a1cat"] = [D, 128]
    s["w2cat"] = [128, D]
    s["a2cat"] = [128, D]
    s["colpack"] = [KC, 128, 9]
    s["w_out"] = [D, D]
    s["cv_w_in"] = [D, 3 * D]
    s["cv_dw"] = [KC, 128, CK]
    s["cv_cols"] = [KC, 128, 3]
    s["cv_w_out"] = [D, D]
    for k, v in make_consts().items():
        s[k] = list(v.shape)
    return s


def prep_inputs(inp):
    f = lambda a: np.ascontiguousarray(np.asarray(a, dtype=np.float32))
    sh = {}
    ada_b = np.asarray(inp["ada_b"], np.float32)
    sh["ada_w"] = f(inp["ada_w"])
    sh["adab_fm"] = f(ada_b.reshape(2, 24, 128).transpose(0, 2, 1))
    sh["npre_fm"] = f(np.asarray(inp["norm_pre"]).reshape(2, KC, 128).transpose(0, 2, 1))
    sh["npost_fm"] = f(np.asarray(inp["norm_post"]).reshape(2, KC, 128).transpose(0, 2, 1))
    sh["mu_fm"] = f(np.asarray(inp["rw_mu"])[0].reshape(6, KC, 128).transpose(2, 0, 1))
    sh["w_in"] = f(np.asarray(inp["rw_w_in"])[0])
    w1 = np.asarray(inp["rw_w1"])[0]
    a1 = np.asarray(inp["rw_a1"])[0]
    sh["w1cat"] = f(np.concatenate([w1[0], w1[1]], 1))
    sh["a1cat"] = f(np.concatenate([a1[0], a1[1]], 1))
    sh["w2cat"] = f(np.asarray(inp["rw_w2"])[0].reshape(128, D))
    sh["a2cat"] = f(np.asarray(inp["rw_a2"])[0].reshape(128, D))
    cols = [np.asarray(inp["rw_w0"])[0, 0], np.asarray(inp["rw_w0"])[0, 1],
            np.asarray(inp["rw_a0"])[0, 0], np.asarray(inp["rw_a0"])[0, 1],
            np.asarray(inp["rw_k_k"])[0], np.asarray(inp["rw_k_a"])[0],
            np.asarray(inp["rw_r_k"])[0].reshape(D), np.asarray(inp["rw_lnx_g"])[0],
            np.asarray(inp["rw_lnx_b"])[0]]
    sh["colpack"] = f(np.stack(cols, 1).reshape(KC, 128, 9))
    sh["w_out"] = f(np.asarray(inp["rw_w_out"])[0])
    sh["cv_w_in"] = f(np.asarray(inp["cv_w_in"])[0])
    sh["cv_dw"] = f(np.asarray(inp["cv_dw_w"])[0].T.reshape(KC, 128, CK))
    cc = [np.asarray(inp["cv_dw_b"])[0], np.asarray(inp["cv_ln_g"])[0], np.asarray(inp["cv_ln_b"])[0]]
    sh["cv_cols"] = f(np.stack(cc, 1).reshape(KC, 128, 3))
    sh["cv_w_out"] = f(np.asarray(inp["cv_w_out"])[0])
    sh.update(make_consts())
    xp = np.asarray(inp["x_prompt"], np.float32)
    xs = np.asarray(inp["x_sample"], np.float32)
    st = np.asarray(inp["state_rwkv"], np.float32)
    c = np.asarray(inp["c"], np.float32)
    cctx = np.asarray(inp["c_ctx"], np.float32)
    maps = []
    for i in range(NCORES):
        m = dict(sh)
        m["xin"] = f(np.concatenate([xp[2 * i].reshape(256, D), xp[2 * i + 1].reshape(256, D), xs[i]], 0))
        cond = np.stack([cctx, c[i]], 1)
        m["condT"] = f(cond.reshape(KC, 128, 2).transpose(1, 0, 2))
        m["st0"] = f(st[i, 0].transpose(0, 1, 3, 2))
        maps.append(m)
    return maps


def toks(*xs):
    out = []
    for x in xs:
        if isinstance(x, Tl):
            out.extend(x.toks)
        elif isinstance(x, Tok):
            out.append(x)
        else:
            out.extend(toks(*x))
    return out


def build(stage="full", dbg_shape=None):
    nc = bass.Bass("TRN2", target_bir_lowering=False)
    specs = dram_specs()
    Dr = {k: nc.dram_tensor(k, shp, F32, kind="ExternalInput").ap() for k, shp in specs.items()}
    y_out = nc.dram_tensor("y_out", [1536, D], F32, kind="ExternalOutput").ap()
    st_out = nc.dram_tensor("st_out", [2, 2, NH, HD, HD], F32, kind="ExternalOutput").ap()
    dbg = None
    if dbg_shape is not None:
        dbg = nc.dram_tensor("dbg", list(dbg_shape), F32, kind="ExternalOutput").ap()
    P = Prog(nc)
    out_ops = []
    zcol_t = P.sb([128, 1], F32, name="zcol")

    def dma(eng, out_ap, in_ap, reads, writes, key):
        return P.op(eng, lambda e: e.dma_start(out=out_ap, in_=in_ap), reads=toks(reads), writes=toks(writes),
                    isdma=True, semkey=key)

    def dbg_dump(tile_ap, src, col0, ncols, rows=128):
        o = dma("sp", dbg[0:rows, col0:col0 + ncols], tile_ap, src, [], "dbg%d" % col0)
        out_ops.append(o)

    C = {}
    for k in make_consts():
        C[k] = P.sb(specs[k], F32, name=k, zero=False)
        dma("sp", C[k][:], Dr[k], [], C[k], "all:const")
    one_col = P.sb([128, 1], name="one")
    tiny_col = P.sb([128, 1], name="tiny")
    P.op("dve", lambda e: e.memset(one_col[:], 1.0), writes=toks(one_col))
    P.op("dve", lambda e: e.memset(tiny_col[:], 1e-24), writes=toks(tiny_col))
    eps_cols = {}
    for nm, val in (("rms", RMS_EPS), ("gn", GN_EPS), ("ln", LN_EPS)):
        eps_cols[nm] = P.sb([128, 1], name="eps" + nm)
        P.op("dve", (lambda t, v: lambda e: e.memset(t[:], v))(eps_cols[nm], val), writes=toks(eps_cols[nm]))

    mu_fm = P.sb([128, 6, KC], name="mu_fm", zero=False)
    dma("sp", mu_fm[:], Dr["mu_fm"], [], mu_fm, "all:const")

    cvcols = P.sb([128, KC, 3], name="cvcols", zero=False)
    cvdw = P.sb([128, KC, CK], name="cvdw", zero=False)
    for e_ in range(KC):
        dma("sp", cvcols[:, e_, :], Dr["cv_cols"][e_], [], cvcols, "all:const")
        dma("sp", cvdw[:, e_, :], Dr["cv_dw"][e_], [], cvdw, "all:const")
    muh = P.sb([128, 6, KC], name="muh")
    P.op("dve", lambda e: e.tensor_scalar(muh[:], mu_fm[:], 0.5, None, ALU.mult), reads=toks(mu_fm), writes=toks(muh))

    banks = [P.ps([128, 512], name="bank%d" % i, nreg=1) for i in range(8)]
    for bk_ in banks:
        P.op("dve", (lambda bk_: lambda e: e.memset(bk_[:, :], 0.0))(bk_), writes=toks(bk_))

    def bq(b, c0, c1):
        return list(banks[b].toks)

    def bh(b, half):
        return list(banks[b].toks)

    TG = 1024
    NT = 8
    xg = P.sb([128, NT, D], F32, name="xg", nreg=NT, zero=False)
    h_fm = P.sb([128, KC, TG], F32R, name="h_fm", nreg=KC, zero=False)
    out_fm = P.sb([128, KC, TG], F32R, name="out_fm", nreg=KC, zero=False)

    condT = P.sb([128, KC, 2], name="condT", zero=False)
    dma("sp", condT[:], Dr["condT"], [], condT, "all:const")
    scond = P.sb([128, KC, 2], name="scond")
    stmp = P.sb([128, KC, 2], name="stmp")
    P.op("act", lambda e: e.activation(stmp[:], condT[:], AF.Exp, scale=-1.0), reads=toks(condT), writes=toks(stmp))
    P.op("dve", lambda e: e.tensor_scalar_add(stmp[:], stmp[:], 1.0), reads=toks(stmp), writes=toks(stmp))
    P.op("dve", lambda e: e.reciprocal(stmp[:], stmp[:]), reads=toks(stmp), writes=toks(stmp))
    P.op("dve", lambda e: e.tensor_tensor(scond[:], condT[:], stmp[:], ALU.mult), reads=toks(condT, stmp), writes=toks(scond))
    adab_fm = P.sb([128, 2, 24], name="adab_fm", zero=False)
    npre_fm = P.sb([128, 2, KC], name="npre_fm", zero=False)
    npost_fm = P.sb([128, 2, KC], name="npost_fm", zero=False)
    for l in range(2):
        dma("sp", adab_fm[:, l, :], Dr["adab_fm"][l], [], adab_fm, "all:const")
        dma("sp", npre_fm[:, l, :], Dr["npre_fm"][l], [], npre_fm, "all:const")
        dma("sp", npost_fm[:, l, :], Dr["npost_fm"][l], [], npost_fm, "all:const")
    modfm = P.sb([128, 2, 24, 2], name="modfm")
    scale_col = P.sb([128, 2, KC, 2], name="scale_col")
    gate_col = P.sb([128, 2, KC, 2], name="gate_col")
    stg = [Vw(xg[:, 4 * i:4 * i + 4, :].rearrange("p a (b c) -> p (a b) c", c=512), xg.toks[4 * i:4 * i + 4]) for i in range(2)]
    nstg = 0
    for l in range(2):
        for q in range(6):
            s_ = stg[nstg % 2]
            nstg += 1
            dma("sp", s_[:], Dr["ada_w"][l].rearrange("(kc p) c -> p kc c", p=128)[:, :, q * 512:(q + 1) * 512],
                [], s_, "adastg%d" % (nstg % 2))
            for b4 in range(4):
                cb = q * 4 + b4
                for kc in range(KC):
                    P.op("pe", (lambda s_, cb, b4, kc: lambda e: e.matmul(
                        banks[0][:, cb * 2:(cb + 1) * 2], s_[:, kc, b4 * 128:(b4 + 1) * 128], scond[:, kc, :],
                        start=(kc == 0), stop=(kc == KC - 1)))(s_, cb, b4, kc),
                         reads=toks(s_, scond), writes=toks(banks[0]))
        P.op("dve", (lambda l: lambda e: e.tensor_tensor(
            modfm[:, l, :, :], banks[0][:, 0:48].rearrange("p (c j) -> p c j", j=2),
            adab_fm[:, l, :].unsqueeze(2).to_broadcast([128, 24, 2]), ALU.add))(l),
             reads=toks(adab_fm, banks[0]), writes=toks(modfm))
        P.op("dve", (lambda l: lambda e: e.scalar_tensor_tensor(
            scale_col[:, l, :, :], modfm[:, l, 8:16, :], 1.0,
            npre_fm[:, l, :].unsqueeze(2).to_broadcast([128, KC, 2]), ALU.add, ALU.mult))(l),
             reads=toks(modfm, npre_fm), writes=toks(scale_col))
        P.op("dve", (lambda l: lambda e: e.tensor_tensor(
            gate_col[:, l, :, :], modfm[:, l, 16:24, :],
            npost_fm[:, l, :].unsqueeze(2).to_broadcast([128, KC, 2]), ALU.mult))(l),
             reads=toks(modfm, npost_fm), writes=toks(gate_col))
    if stage == "mod":
        dbg_dump(modfm[:].rearrange("p l c j -> p (l c j)"), modfm, 0, 96)
        dbg_dump(scale_col[:].rearrange("p l c j -> p (l c j)"), scale_col, 96, 32)
        dbg_dump(gate_col[:].rearrange("p l c j -> p (l c j)"), gate_col, 128, 32)
        P.emit(out_ops)
        P.close()
        return nc

    ss = P.sb([128, NT], name="ss")
    rstd = P.sb([128, NT], name="rstd")
    wring = [P.sb([128, KC, 128], F32R, name="wr%d" % i, zero=False) for i in range(4)]
    wpall = P.sb([128, 4, KC, 128], F32R, name="wpall", nreg=4, zero=False)
    wpring = [Vw(wpall[:, i], wpall.r(i)) for i in range(4)]
    lwla = P.sb([128, 2, TG], name="lwla", nreg=2, zero=False)
    lw_fm = Vw(lwla[:, 0, :], lwla.r(0))
    la_fm = Vw(lwla[:, 1, :], lwla.r(1))
    cf = {}
    for i, nm in enumerate(("r", "k", "v", "sg", "kk", "b", "kd", "sw")):
        cf[nm] = Vw(xg[:, i, :], xg.r(i))
    for nm in ("ksum", "ta", "tb"):
        cf[nm] = P.sb([128, TG + 4], name=nm)
    Bpad, tmpd = cf["ta"], cf["tb"]
    P.op("pool", lambda e: e.memset(Bpad[:], 0.0), writes=toks(Bpad))
    P.op("pool", lambda e: e.memset(tmpd[:], 0.0), writes=toks(tmpd))
    xn_buf = [cf["tb"], cf["ksum"]]
    junk = cf["ta"]
    xcnt = [0]

    def ginfo(g):
        off, lens, cond = GROUPS[g]
        Tg = sum(lens)
        return off, lens, cond, Tg, Tg // 128, Tg // 512

    def seq_pad_offsets(lens):
        offs = []
        o = 1
        for L in lens:
            offs.append(o)
            o += L + 2
        return offs, o - 1

    def phase1(l, g, from_dram):
        off, lens, cond, Tg, nt, nblk = ginfo(g)
        P.op("dve", lambda e: e.memset(ss[:], 0.0), writes=toks(ss))
        for ti in range(nt):
            if from_dram:
                dma("sp", xg[:, ti, :], Dr["xin"][off + ti * 128: off + (ti + 1) * 128, :], [], xg.r(ti), "xg%d" % ti)
            P.op("act", (lambda ti: lambda e: e.activation(junk[:, 0:D], xg[:, ti, :], AF.Square, accum_out=ss[:, ti:ti + 1]))(ti),
                 reads=xg.r(ti), writes=toks(junk, ss))
        P.op("act", lambda e: e.activation(rstd[:, 0:nt], ss[:, 0:nt], AF.Ln, bias=eps_cols["rms"][:], scale=1.0 / D),
             reads=toks(ss, eps_cols["rms"]), writes=toks(rstd))
        P.op("act", lambda e: e.activation(rstd[:, 0:nt], rstd[:, 0:nt], AF.Exp, scale=-0.5), reads=toks(rstd), writes=toks(rstd))
        for ti in range(nt):
            xn = xn_buf[ti % 2]
            P.op("dve", (lambda ti, xn: lambda e: e.tensor_scalar(xn[:, 0:D], xg[:, ti, :], rstd[:, ti:ti + 1], None, ALU.mult))(ti, xn),
                 reads=xg.r(ti) + toks(rstd), writes=toks(xn))
            b0 = 0 if ti % 2 == 0 else 2
            for kc in range(KC):
                bk = banks[b0 + kc // 4]
                c0 = (kc % 4) * 128
                P.op("pe", (lambda bk, c0, xn, kc: lambda e: e.transpose(bk[:, c0:c0 + 128], xn[:, kc * 128:(kc + 1) * 128], C["ident"][:]))(bk, c0, xn, kc),
                     reads=toks(xn, C["ident"]), writes=toks(bk))
            for kc in range(KC):
                bk = banks[b0 + kc // 4]
                c0 = (kc % 4) * 128
                if kc % 2 == 0:
                    P.op("act", (lambda bk, c0, kc, ti: lambda e: e.activation(
                        h_fm[:, kc, ti * 128:(ti + 1) * 128], bk[:, c0:c0 + 128], AF.Identity,
                        bias=modfm[:, l, kc, cond:cond + 1], scale=scale_col[:, l, kc, cond:cond + 1]))(bk, c0, kc, ti),
                         reads=toks(bk, modfm, scale_col), writes=h_fm.r(kc))
                else:
                    P.op("dve", (lambda bk, c0, kc, ti: lambda e: e.tensor_scalar(
                        h_fm[:, kc, ti * 128:(ti + 1) * 128], bk[:, c0:c0 + 128],
                        scale_col[:, l, kc, cond:cond + 1], modfm[:, l, kc, cond:cond + 1], ALU.mult, ALU.add))(bk, c0, kc, ti),
                         reads=toks(bk, modfm, scale_col), writes=h_fm.r(kc))

    def load_w(slot, src_ap, mu_idx):
        wt, wp = wring[slot], wpring[slot]
        dma("pool", wt[:], src_ap.rearrange("(kc p) e -> p kc e", p=128), [], wt, "w%d" % slot)
        P.op("pool", lambda e: e.tensor_tensor(wp[:], wt[:], muh[:, mu_idx, :].unsqueeze(2).to_broadcast([128, KC, 128]), ALU.mult),
             reads=toks(wt, muh), writes=toks(wp))

    def project_mm(g, slot, bset=0):
        off, lens, cond, Tg, nt, nblk = ginfo(g)
        wt, wp = wring[slot], wpring[slot]
        for b in range(nblk):
            bA, bB = banks[4 * bset + b], banks[4 * bset + 2 + b]
            for (bk, w) in ((bA, wt), (bB, wp)):
                for kc in range(KC):
                    P.op("pe", (lambda bk, w, kc, b: lambda e: e.matmul(
                        bk[:, :], w[:, kc, :], h_fm[:, kc, b * 512:(b + 1) * 512], start=(kc == 0), stop=(kc == KC - 1)))(bk, w, kc, b),
                         reads=toks(w) + h_fm.r(kc), writes=toks(bk))

    def project_post(g, slot, out_t, bset=0):
        off, lens, cond, Tg, nt, nblk = ginfo(g)
        offs, width = seq_pad_offsets(lens)
        for b in range(nblk):
            bA, bB = banks[4 * bset + b], banks[4 * bset + 2 + b]
            P.op("act", (lambda bA, b: lambda e: e.copy(out_t[:, b * 512:(b + 1) * 512], bA[:, :]))(bA, b),
                 reads=toks(bA), writes=toks(out_t))
            t0 = b * 512
            pos = 0
            for si, L in enumerate(lens):
                lo, hi = max(t0, pos), min(t0 + 512, pos + L)
                if lo < hi:
                    P.op("act", (lambda bB, lo, hi, t0, po: lambda e: e.copy(Bpad[:, po:po + (hi - lo)], bB[:, lo - t0:hi - t0]))(
                        bB, lo, hi, t0, offs[si] + lo - pos), reads=toks(bB), writes=toks(Bpad))
                pos += L
        W = width + 1
        P.op("dve", lambda e: e.tensor_tensor(tmpd[:, 1:W - 1], Bpad[:, 0:W - 2], Bpad[:, 2:W], ALU.add),
             reads=toks(Bpad), writes=toks(tmpd))
        P.op("dve", lambda e: e.scalar_tensor_tensor(tmpd[:, 1:W - 1], Bpad[:, 1:W - 1], -2.0, tmpd[:, 1:W - 1], ALU.mult, ALU.add),
             reads=toks(Bpad, tmpd), writes=toks(tmpd))
        pos = 0
        for si, L in enumerate(lens):
            P.op("dve", (lambda pos, L, po: lambda e: e.tensor_tensor(out_t[:, pos:pos + L], out_t[:, pos:pos + L], tmpd[:, po:po + L], ALU.add))(pos, L, offs[si]),
                 reads=toks(tmpd, out_t), writes=toks(out_t))
            pos += L

    def project(g, slot, out_t, bset=0):
        project_mm(g, slot, bset)
        project_post(g, slot, out_t, bset)

    def sigmoid(out_ap, in_ap, rd, wr, tmp_t, Tg, nbias_ap=None, scale=1.0):
        kw = dict(scale=-scale)
        rdx = toks(rd)
        if nbias_ap is not None:
            kw["bias"] = nbias_ap[0]
            rdx = rdx + toks(nbias_ap[1])
        P.op("act", lambda e: e.activation(tmp_t[:, 0:Tg], in_ap, AF.Exp, **kw), reads=rdx, writes=toks(tmp_t))
        P.op("act", lambda e: e.activation(tmp_t[:, 0:Tg], tmp_t[:, 0:Tg], AF.Ln, bias=one_col[:]), reads=toks(tmp_t, one_col), writes=toks(tmp_t))
        P.op("act", lambda e: e.activation(out_ap, tmp_t[:, 0:Tg], AF.Exp, scale=-1.0), reads=toks(tmp_t), writes=toks(wr))

    def zero_pads(g):
        off, lens, cond, Tg, nt, nblk = ginfo(g)
        offs_, width_ = seq_pad_offsets(lens)
        for si_, L_ in enumerate(lens):
            for col in (offs_[si_] - 1, offs_[si_] + L_):
                X("pool", "memset", Bpad[:, col:col + 1], 0.0, wr=[Bpad])

    def lora_stage(g):
        off, lens, cond, Tg, nt, nblk = ginfo(g)
        zero_pads(g)
        load_w(0, Dr["w1cat"], 4)
        load_w(1, Dr["a1cat"], 5)
        project(g, 0, lw_fm, 0)
        project(g, 1, la_fm, 1)
        sigmoid(cf["ta"][:, 0:Tg], lw_fm[:, 0:Tg], lw_fm, cf["ta"], cf["tb"], Tg, scale=2.0)
        P.op("dve", lambda e: e.tensor_scalar(lw_fm[:, 0:Tg], cf["ta"][:, 0:Tg], 2.0, -1.0, ALU.mult, ALU.add),
             reads=toks(cf["ta"]), writes=toks(lw_fm))

    def proj_stage(g, e_idx, pre_mm0=False):
        off, lens, cond, Tg, nt, nblk = ginfo(g)
        zero_pads(g)
        for n, nm in enumerate(("r", "k", "v", "sg")):
            bset = (n + 1) % 2
            if not (n == 0 and pre_mm0):
                project_mm(g, n, bset)
            project_post(g, n, cf[nm], bset)
        sigmoid(cf["ta"][:, 0:Tg], cf["sg"][:, 0:Tg], cf["sg"], cf["ta"], cf["tb"], Tg)
        P.op("dve", lambda e: e.tensor_tensor(cf["sg"][:, 0:Tg], cf["sg"][:, 0:Tg], cf["ta"][:, 0:Tg], ALU.mult),
             reads=toks(cf["sg"], cf["ta"]), writes=toks(cf["sg"]))

    def load_chunk_weights(e_idx):
        for n in range(4):
            load_w(n, Dr["w_in"][n][:, e_idx * 128:(e_idx + 1) * 128], n)


    ccb = [P.sb([128, 9], name="cc%d" % i) for i in range(2)]
    w2c = [P.sb([128, 128], name="w2c%d" % i) for i in range(2)]
    a2c = [P.sb([128, 128], name="a2c%d" % i) for i in range(2)]
    dcol = P.sb([128, 8], name="dcol")
    TM_2 = [P.sb([128, 4, 128], WDT, name="TM%d" % i) for i in range(3)]
    LwT = P.sb([128, 128], name="LwT")
    EF = P.sb([128, 384], name="EF")
    ET = P.sb([128, 256], name="ET")
    BK_2 = [P.sb([128, 256], WDT, name="BK%d" % i) for i in range(2)]
    QRP_2 = [P.sb([128, 320], WDT, name="QRP%d" % i) for i in range(2)]
    Zt_2 = [P.sb([128, 2, 128], WDT, name="Zt%d" % i) for i in range(2)]
    AM_2 = [P.sb([128, 2, 448], WDT, name="AM%d" % i, nreg=2) for i in range(2)]
    Kd_2 = [P.sb([128, 128], WDT, name="Kd%d" % i) for i in range(2)]
    YPTall = P.sb([128, 2, 2, 384], YDT, name="YPT", nreg=4)
    YPT = [Vw(YPTall[:, i], YPTall.toks[2 * i:2 * i + 2]) for i in range(2)]
    WU = P.sb([128, 2, 128], WDT, name="WU", nreg=2)
    TinvR = P.sb([128, 2, 128], WDT, name="TinvR", nreg=2)
    zpad_t = P.sb([128, 16 * (64 + CK - 1)], F32R, name="zpad")
    QMs = P.sb([64, 2, 192], WDT, name="QMs", nreg=2)
    ST = [P.sb([64, 2, 64], WDT, name="ST%d" % d) for d in range(2)]
    identR = P.sb([128, 128], WDT, name="identR")
    P.op("dve", lambda e: e.tensor_copy(identR[:], C["ident"][:]), reads=toks(C["ident"]), writes=toks(identR))
    o_acc = P.sb([128, NT, 128], name="o_acc", nreg=NT)
    gstat = P.sb([128, 4, NT * 2], name="gstat")
    ccnt = [0]

    def load_chunk_consts(e_idx):
        i = ccnt[0] % 2
        ccnt[0] += 1
        dma("sp", ccb[i][:], Dr["colpack"][e_idx], [], ccb[i], "cc%d" % i)
        dma("sp", w2c[i][:], Dr["w2cat"][:, e_idx * 128:(e_idx + 1) * 128], [], w2c[i], "w2c%d" % i)
        dma("sp", a2c[i][:], Dr["a2cat"][:, e_idx * 128:(e_idx + 1) * 128], [], a2c[i], "a2c%d" % i)
        return i

    def mm(out_ap, lhsT, rhs, rd, wr, start=True, stop=True):
        P.op("pe", lambda e: e.matmul(out_ap, lhsT, rhs, start=start, stop=stop), reads=toks(rd), writes=toks(wr))

    def X(eng, meth, *args, rd=(), wr=(), **kw):
        P.op(eng, lambda e: getattr(e, meth)(*args, **kw), reads=toks(rd), writes=toks(wr))

    def wkv_chunk(g, e_idx, ci, next_mm=None):
        off, lens, cond, Tg, nt, nblk = ginfo(g)
        cc, w2, a2 = ccb[ci], w2c[ci], a2c[ci]
        r_, k_, v_, sg_, kk_, b_, kd_, sw_ = (cf[n] for n in ("r", "k", "v", "sg", "kk", "b", "kd", "sw"))
        ksum, ta, tb = cf["ksum"], cf["ta"], cf["tb"]
        ident, ones_blk, i64x2, ones_c = C["ident"], C["ones_blk"], C["i64x2"], C["ones"]
        X("dve", "tensor_scalar", dcol[:, 0:4], cc[:, 0:4], -1.0, None, ALU.mult, rd=[cc], wr=[dcol])
        X("dve", "tensor_scalar", dcol[:, 4:5], cc[:, 5:6], -1.0, 1.0, ALU.mult, ALU.add, rd=[cc], wr=[dcol])
        X("dve", "tensor_scalar", dcol[:, 5:6], cc[:, 6:7], 0.5, None, ALU.mult, rd=[cc], wr=[dcol])
        X("dve", "tensor_scalar", kk_[:, 0:Tg], k_[:, 0:Tg], cc[:, 4:5], None, ALU.mult, rd=[k_, cc], wr=[kk_])
        X("dve", "tensor_tensor", ta[:, 0:Tg], kk_[:, 0:Tg], kk_[:, 0:Tg], ALU.mult, rd=[kk_], wr=[ta])
        for b in range(nblk):
            sl = slice(b * 512, (b + 1) * 512)
            mm(banks[b][:, :], ones_blk[:], ta[:, sl], [ones_blk, ta], banks[b])
            X("act", "activation", tb[:, sl], banks[b][:, :], AF.Ln, bias=tiny_col[:], rd=[banks[b], tiny_col], wr=[tb])
        X("act", "activation", tb[:, 0:Tg], tb[:, 0:Tg], AF.Exp, scale=-0.5, rd=[tb], wr=[tb])
        X("dve", "tensor_tensor", kk_[:, 0:Tg], kk_[:, 0:Tg], tb[:, 0:Tg], ALU.mult, rd=[kk_, tb], wr=[kk_])

        seq_of_tile = []
        for si, L in enumerate(lens):
            seq_of_tile += [si] * (L // 128)
        first_tile, last_tile = {}, {}
        for ti, si in enumerate(seq_of_tile):
            first_tile.setdefault(si, ti)
            last_tile[si] = ti

        def h3(ap):
            return ap.rearrange("p (h j) -> p h j", h=2)

        def sig_fm(d, wt_, src, ncol_idx, dst):
            rows = slice(d * 64, (d + 1) * 64)
            bb = 0 if d == 0 else 2
            for b in range(nblk):
                sl = slice(b * 512, (b + 1) * 512)
                bk = banks[bb + b]
                mm(bk[:, :], wt_[rows, :], src[rows, sl], [wt_, src], bk)
                X("act", "activation", dst[:, sl], bk[:, :], AF.Exp, bias=dcol[:, ncol_idx:ncol_idx + 1], scale=-1.0, rd=[bk, dcol], wr=[dst])
            X("act", "activation", dst[:, 0:Tg], dst[:, 0:Tg], AF.Ln, bias=one_col[:], rd=[dst, one_col], wr=[dst])
            X("act", "activation", dst[:, 0:Tg], dst[:, 0:Tg], AF.Exp, scale=-1.0, rd=[dst], wr=[dst])

        def bkd_from_a(d):
            X("dve", "tensor_tensor", b_[:, 0:Tg], kk_[:, 0:Tg], ta[:, 0:Tg], ALU.mult, rd=[kk_, ta], wr=[b_])
            X("dve", "tensor_scalar", ta[:, 0:Tg], ta[:, 0:Tg], cc[:, 5:6], dcol[:, 4:5], ALU.mult, ALU.add, rd=[ta, cc, dcol], wr=[ta])
            X("dve", "tensor_tensor", kd_[:, 0:Tg], k_[:, 0:Tg], ta[:, 0:Tg], ALU.mult, rd=[k_, ta], wr=[kd_])
            if d == 0:
                X("pool", "tensor_copy", ksum[:, 0:Tg], kd_[:, 0:Tg], rd=[kd_], wr=[ksum])
            else:
                X("pool", "tensor_tensor", ksum[:, 0:Tg], ksum[:, 0:Tg], kd_[:, 0:Tg], ALU.add, rd=[kd_, ksum], wr=[ksum])

        for d in range(2):
            if d == 0:
                sig_fm(0, a2, la_fm, 2, ta)
                bkd_from_a(0)
                sig_fm(0, w2, lw_fm, 0, sw_)
                sig_fm(1, a2, la_fm, 3, ta)
                sig_fm(1, w2, lw_fm, 1, tb)
                sw_src = sw_
            else:
                bkd_from_a(1)
                sw_src = tb
            cm = C["cmF"] if d == 0 else C["cmB"]
            mEx = Vw(cm[:, 128:256], cm.toks)
            cmo = C["cmB"] if d == 0 else C["cmF"]
            mEd = Vw(cmo[:, 128:256], cmo.toks)
            mask = C["maskF"] if d == 0 else C["maskB"]
            pcc = 127 if d == 0 else 0
            tiles = list(range(nt)) if d == 0 else list(range(nt - 1, -1, -1))

            def head_pieces(ti, ui_):
                pb = ui_ % 2
                TM, BK, QRP, Zt, AM, Kd = TM_2[ui_ % 3], BK_2[pb], QRP_2[pb], Zt_2[pb], AM_2[pb], Kd_2[pb]
                tsl = slice(ti * 128, (ti + 1) * 128)
                cur = YPT[0]
                pcs = []

                def p0():
                    for j, src in enumerate((v_, kk_, b_, kd_)):
                        X("pe", "transpose", banks[4][:, j * 128:(j + 1) * 128], src[:, tsl], ident[:], rd=[src, ident], wr=[banks[4]])
                    X("pe", "transpose", banks[5][:, 0:128], sw_src[:, tsl], ident[:], rd=[sw_src, ident], wr=[banks[5]])
                    X("act", "copy", TM[:, 0:4, :].rearrange("p a b -> p (a b)"), banks[4][:, :], rd=[banks[4]], wr=[TM])
                    X("act", "activation", LwT[:, :], banks[5][:, 0:128], AF.Identity, scale=-DEC_C, rd=[banks[5]], wr=[LwT])
                pcs.append(p0)

                def p1():
                    mm(banks[5][:, 128:384], LwT[:, :], cm[:, :], [LwT, cm], banks[5])
                    mm(banks[4][:, 0:128], mEx[:, :], LwT[:, :], [LwT, mEx], banks[4])
                    mm(banks[4][:, 128:256], mEd[:, :], LwT[:, :], [LwT, mEd], banks[4])
                    X("act", "activation", EF[:, 0:256], banks[5][:, 128:384], AF.Exp, rd=[banks[5]], wr=[EF])
                    X("act", "activation", EF[:, 256:384], banks[5][:, 128:256], AF.Exp, scale=-1.0, rd=[banks[5]], wr=[EF])
                    X("act", "activation", ET[:, :], banks[4][:, 0:256], AF.Exp, rd=[banks[4]], wr=[ET])
                pcs.append(p1)

                def p2():
                    X("dve", "tensor_tensor", BK[:, 0:128], b_[:, tsl], EF[:, 256:384], ALU.mult, rd=[b_, EF], wr=[BK])
                    X("dve", "tensor_tensor", BK[:, 128:256], kd_[:, tsl], EF[:, 256:384], ALU.mult, rd=[kd_, EF], wr=[BK])
                    X("dve", "tensor_tensor", QRP[:, 0:128], kk_[:, tsl], EF[:, 128:256], ALU.mult, rd=[kk_, EF], wr=[QRP])
                    X("dve", "tensor_tensor", QRP[:, 128:256], r_[:, tsl], EF[:, 0:128], ALU.mult, rd=[r_, EF], wr=[QRP])
                pcs.append(p2)

                def p3():
                    X("dve", "tensor_scalar", QRP[:, 256:320], i64x2[:, :], EF[:, pcc:pcc + 1], None, ALU.mult, rd=[i64x2, EF], wr=[QRP])
                    X("dve", "tensor_tensor", Zt[:, :, 0:64], h3(TM[:, 1, :].bitcast(F32)), h3(ET[:, 0:128]), ALU.mult, rd=[TM, ET], wr=[Zt])
                    X("dve", "scalar_tensor_tensor", AM[:, :, 384:448], h3(TM[:, 2, :].bitcast(F32)), -1.0, h3(ET[:, 128:256]), ALU.mult, ALU.mult, rd=[TM, ET], wr=[AM])
                    X("dve", "tensor_tensor", Kd[:, :], TM[:, 3, :].bitcast(F32), ET[:, 128:256], ALU.mult, rd=[TM, ET], wr=[Kd])
                pcs.append(p3)

                def pa(hh):
                    def f():
                        hr = slice(hh * 64, (hh + 1) * 64)
                        bk = banks[1 + 2 * hh]
                        bk2 = banks[0 + 2 * hh]
                        mm(bk[:, 0:128], BK[hr, 0:128], QRP[hr, 0:128], [BK, QRP], bk)
                        mm(bk[:, 384:512], QRP[hr, 0:128], BK[hr, 0:128], [BK, QRP], bk)
                        mm(bk[:, 128:256], BK[hr, 128:256], QRP[hr, 0:128], [BK, QRP], bk)
                        mm(bk[:, 256:384], BK[hr, 128:256], QRP[hr, 128:256], [BK, QRP], bk)
                        mm(bk2[:, 0:128], BK[hr, 0:128], QRP[hr, 128:256], [BK, QRP], bk2)
                    return f

                def pm(hh):
                    def f():
                        bk = banks[1 + 2 * hh]
                        bk2 = banks[0 + 2 * hh]
                        ct = [cur.toks[hh]]
                        X("dve", "tensor_tensor", cur[:, hh, 0:128], bk[:, 0:128], mask[:, 0:128], ALU.mult, rd=[bk, mask], wr=ct)
                        X("dve", "tensor_tensor", cur[:, hh, 256:384], bk[:, 384:512], mask[:, 384:512], ALU.mult, rd=[bk, mask], wr=ct)
                        X("pool", "tensor_tensor", cur[:, hh, 128:256], cur[:, hh, 0:128].bitcast(F32), ident[:, :], ALU.add, rd=ct + [ident], wr=ct)
                        X("dve", "tensor_tensor", AM[:, hh, 0:256], bk[:, 128:384], mask[:, 128:384], ALU.mult, rd=[bk, mask], wr=AM.r(hh))
                        X("dve", "tensor_tensor", AM[:, hh, 256:384], bk2[:, 0:128], mask[:, 512:640], ALU.mult, rd=[bk2, mask], wr=AM.r(hh))
                    return f
                pcs += [pa(0), pa(1), pm(0), pm(1)]
                return pcs

            NLEV = 7

            def doubling_level(lev):
                cur, nxt = (YPT[0], YPT[1]) if lev % 2 == 0 else (YPT[1], YPT[0])
                for hh in range(2):
                    bk = banks[6 + hh]
                    ct = [cur.toks[hh]]
                    Y, Pm, YT = cur[:, hh, 0:128], cur[:, hh, 128:256], cur[:, hh, 256:384]
                    if lev == 0:
                        mm(bk[:, 256:384], Y, YT, ct, bk)
                        mm(bk[:, 0:128], YT, Y, ct, bk)
                    elif lev <= NLEV - 3:
                        mm(bk[:, 256:384], Y, YT, ct, bk)
                        mm(bk[:, 0:256], YT, cur[:, hh, 0:256], ct, bk)
                    elif lev == NLEV - 2:
                        mm(bk[:, 256:384], Y, YT, ct, bk)
                        mm(bk[:, 128:256], YT, Pm, ct, bk)
                    else:
                        mm(bk[:, 128:256], YT, Pm, ct, bk)
                for hh in range(2):
                    bk = banks[6 + hh]
                    ct = [cur.toks[hh]]
                    nt_ = [nxt.toks[hh]]
                    if lev == 0:
                        X("act", "copy", nxt[:, hh, 0:128], bk[:, 0:128], rd=[bk], wr=nt_)
                        X("act", "copy", nxt[:, hh, 256:384], bk[:, 256:384], rd=[bk], wr=nt_)
                        X("pool", "tensor_copy", nxt[:, hh, 128:256], cur[:, hh, 128:256].bitcast(F32), rd=ct, wr=nt_)
                    elif lev <= NLEV - 3:
                        X("act", "copy", nxt[:, hh, :], bk[:, 0:384], rd=[bk], wr=nt_)
                        X("dve", "tensor_tensor", nxt[:, hh, 128:256], nxt[:, hh, 128:256].bitcast(F32), cur[:, hh, 128:256].bitcast(F32), ALU.add, rd=ct + nt_, wr=nt_)
                    elif lev == NLEV - 2:
                        X("act", "copy", nxt[:, hh, 128:384], bk[:, 128:384], rd=[bk], wr=nt_)
                        X("dve", "tensor_tensor", nxt[:, hh, 128:256], nxt[:, hh, 128:256].bitcast(F32), cur[:, hh, 128:256].bitcast(F32), ALU.add, rd=ct + nt_, wr=nt_)
                    else:
                        X("dve", "tensor_tensor", TinvR[:, hh, :], bk[:, 128:256], cur[:, hh, 128:256].bitcast(F32), ALU.add, rd=[bk] + ct, wr=TinvR.r(hh))

            def tail_pieces(ti, ui_):
                pb = ui_ % 2
                TM, BK, QRP, Zt, AM, Kd = TM_2[ui_ % 3], BK_2[pb], QRP_2[pb], Zt_2[pb], AM_2[pb], Kd_2[pb]
                si = seq_of_tile[ti]
                seq_start = (ti == first_tile[si]) if d == 0 else (ti == last_tile[si])
                seq_end = (ti == last_tile[si]) if d == 0 else (ti == first_tile[si])
                tb_ = [banks[1], banks[3]]

                def t0():
                    if seq_start:
                        if g == "S":
                            dma("pool", ST[d][:], Dr["st0"][d, 2 * e_idx:2 * e_idx + 2].rearrange("h j i -> j h i"), [], ST[d], "st%d" % d)
                        else:
                            X("dve", "tensor_scalar", ST[d][:], ones_c[0:64, 0:1].unsqueeze(2).to_broadcast([64, 2, 64]), 0.0, None, ALU.mult, rd=[ones_c], wr=[ST[d]])
                    for hh in range(2):
                        bk = tb_[hh]
                        mm(bk[:, 0:64], AM[:, hh, 0:128], TM[:, 0, hh * 64:(hh + 1) * 64], AM.r(hh) + [TM], bk)
                        X("act", "copy", Zt[:, hh, 64:128], bk[:, 0:64], rd=[bk], wr=[Zt])

                def t0b():
                    for hh in range(2):
                        bk = tb_[hh]
                        mm(bk[:, 64:192], TinvR[:, hh, :], Zt[:, hh, :], TinvR.r(hh) + [Zt], bk)
                        X("act", "copy", WU[:, hh, :], bk[:, 64:192], rd=[bk], wr=WU.r(hh))

                def t1():
                    for hh in range(2):
                        hr = slice(hh * 64, (hh + 1) * 64)
                        bk = tb_[hh]
                        mm(bk[0:64, 192:384], WU[:, hh, 0:64], AM[:, hh, 256:448], AM.r(hh) + WU.r(hh), bk, start=True, stop=False)
                        mm(bk[0:64, 192:384], identR[hr, hr], QRP[hr, 128:320], [identR, QRP], bk, start=False, stop=True)
                        X("act", "copy", QMs[:, hh, :], bk[0:64, 192:384], rd=[bk], wr=QMs.r(hh))

                def t2():
                    for hh in range(2):
                        bk = tb_[hh]
                        vv = TM[:, 0, hh * 64:(hh + 1) * 64]
                        nu0 = WU[:, hh, 64:128]
                        mm(bk[:, 384:448], AM[:, hh, 256:384], nu0, AM.r(hh) + WU.r(hh), bk, start=True, stop=False)
                        mm(bk[:, 384:448], AM[:, hh, 128:256], vv, AM.r(hh) + [TM], bk, start=False, stop=False)
                        mm(bk[:, 384:448], QMs[:, hh, 0:128], ST[d][:, hh, :], QMs.r(hh) + [ST[d]], bk, start=False, stop=True)
                        mm(bk[0:64, 448:512], AM[:, hh, 384:448], nu0, AM.r(hh) + WU.r(hh), bk, start=True, stop=False)
                        mm(bk[0:64, 448:512], Kd[:, hh * 64:(hh + 1) * 64], vv, [Kd, TM], bk, start=False, stop=False)
                        mm(bk[0:64, 448:512], QMs[:, hh, 128:192], ST[d][:, hh, :], QMs.r(hh) + [ST[d]], bk, start=False, stop=True)
                        if d == 0:
                            X("act", "copy", o_acc[:, ti, hh * 64:(hh + 1) * 64], bk[:, 384:448], rd=[bk], wr=o_acc.r(ti))
                        else:
                            X("dve", "tensor_tensor", o_acc[:, ti, hh * 64:(hh + 1) * 64], o_acc[:, ti, hh * 64:(hh + 1) * 64], bk[:, 384:448], ALU.add,
                              rd=[bk] + o_acc.r(ti), wr=o_acc.r(ti))
                        X("act", "copy", ST[d][:, hh, :], bk[0:64, 448:512], rd=[bk], wr=[ST[d]])
                    if seq_end and g == "P":
                        o = dma("sp", st_out[si, d, 2 * e_idx:2 * e_idx + 2].rearrange("h j i -> j h i"), ST[d][:].bitcast(F32), ST[d], [], "sto%d" % d)
                        out_ops.append(o)
                return [t0, t0b, t1, t2]

            for pc in head_pieces(tiles[0], 0):
                pc()
            prev_tail = []
            for ui, ti in enumerate(tiles):
                nxt_pcs = head_pieces(tiles[ui + 1], ui + 1) if ui + 1 < len(tiles) else []
                for lev in range(NLEV):
                    doubling_level(lev)
                    if lev < 4 and prev_tail:
                        prev_tail[lev]()
                    if nxt_pcs:
                        if lev == 1:
                            nxt_pcs[0]()
                        elif lev == 2:
                            nxt_pcs[1]()
                        elif lev == 3:
                            nxt_pcs[2]()
                        elif lev == 4:
                            nxt_pcs[3]()
                        elif lev == 5:
                            nxt_pcs[4]()
                            nxt_pcs[5]()
                if nxt_pcs:
                    nxt_pcs[6]()
                    nxt_pcs[7]()
                prev_tail = tail_pieces(ti, ui)
            for pc in prev_tail:
                pc()

        n2 = nt * 2
        o3 = o_acc[:, 0:nt, :].rearrange("p t (h i) -> p (t h) i", h=2)
        X("dve", "tensor_reduce", gstat[:, 0, 0:n2], o3, AX.X, ALU.add, rd=[o_acc], wr=[gstat])
        sq3 = ta[:, 0:nt * 128].rearrange("p (a i) -> p a i", i=64)
        X("dve", "tensor_tensor", sq3, o3, o3, ALU.mult, rd=[o_acc], wr=[ta])
        X("dve", "tensor_reduce", gstat[:, 1, 0:n2], sq3, AX.X, ALU.add, rd=[ta], wr=[gstat])
        X("dve", "tensor_scalar", gstat[:, 0, 0:n2], gstat[:, 0, 0:n2], 1.0 / 64, None, ALU.mult, rd=[gstat], wr=[gstat])
        X("dve", "tensor_tensor", gstat[:, 2, 0:n2], gstat[:, 0, 0:n2], gstat[:, 0, 0:n2], ALU.mult, rd=[gstat], wr=[gstat])
        X("dve", "scalar_tensor_tensor", gstat[:, 1, 0:n2], gstat[:, 1, 0:n2], 1.0 / 64, gstat[:, 2, 0:n2], ALU.mult, ALU.subtract, rd=[gstat], wr=[gstat])
        X("act", "activation", gstat[:, 3, 0:n2], gstat[:, 1, 0:n2], AF.Ln, bias=eps_cols["gn"][:], rd=[gstat, eps_cols["gn"]], wr=[gstat])
        X("act", "activation", gstat[:, 3, 0:n2], gstat[:, 3, 0:n2], AF.Exp, scale=-0.5, rd=[gstat], wr=[gstat])
        X("dve", "tensor_tensor", o3, o3, gstat[:, 0, 0:n2].unsqueeze(2).to_broadcast([128, n2, 64]), ALU.subtract, rd=[o_acc, gstat], wr=[o_acc])
        X("dve", "tensor_tensor", o3, o3, gstat[:, 3, 0:n2].unsqueeze(2).to_broadcast([128, n2, 64]), ALU.mult, rd=[o_acc, gstat], wr=[o_acc])
        if next_mm is not None:
            next_mm()
        X("dve", "tensor_tensor", ta[:, 0:Tg], r_[:, 0:Tg], ksum[:, 0:Tg], ALU.mult, rd=[r_, ksum], wr=[ta])
        X("dve", "tensor_scalar", ta[:, 0:Tg], ta[:, 0:Tg], dcol[:, 5:6], None, ALU.mult, rd=[ta, dcol], wr=[ta])
        for b in range(nblk):
            sl = slice(b * 512, (b + 1) * 512)
            mm(banks[b][:, :], ones_blk[:], ta[:, sl], [ones_blk, ta], banks[b])
            X("dve", "tensor_tensor", tb[:, sl], banks[b][:, :], v_[:, sl], ALU.mult, rd=[banks[b], v_], wr=[tb])
        X("dve", "tensor_scalar", tb[:, 0:Tg], tb[:, 0:Tg], cc[:, 8:9], None, ALU.add, rd=[tb, cc], wr=[tb])
        for b in range(nblk):
            bk = banks[2 + b]
            for q in range(4):
                ti = b * 4 + q
                X("pe", "transpose", bk[:, q * 128:(q + 1) * 128], o_acc[:, ti, :], ident[:], rd=o_acc.r(ti) + [ident], wr=[bk])
            sl = slice(b * 512, (b + 1) * 512)
            X("dve", "scalar_tensor_tensor", ta[:, sl], bk[:, :], cc[:, 7:8], tb[:, sl], ALU.mult, ALU.add, rd=[bk, cc, tb], wr=[ta])
            X("dve", "tensor_tensor", out_fm[:, e_idx, sl], ta[:, sl], sg_[:, sl], ALU.mult, rd=[ta, sg_], wr=out_fm.r(e_idx))


    ss3 = P.sb([128, 2], name="ss3")
    rs3 = P.sb([128, 1], name="rs3")
    dgt = P.sb([128, 128], name="dgt")

    def phase3(l, g, wout_ap, reload_x, final):
        off, lens, cond, Tg, nt, nblk = ginfo(g)
        ta, tb = cf["ta"], cf["tb"]
        ident, ones = C["ident"], C["ones"]
        dma("pool", h_fm[:, :, :], wout_ap.rearrange("(kc p) e -> p kc e", p=128), [], h_fm, "wout")
        for kc in range(KC):
            X("dve", "tensor_scalar", dgt[:, :], ident[:, :], gate_col[:, l, kc, cond:cond + 1], None, ALU.mult, rd=[ident, gate_col], wr=[dgt])
            bk = banks[4 + kc // 4]
            mm(bk[:, (kc % 4) * 128:(kc % 4 + 1) * 128], ones[:, :], dgt[:, :], [ones, dgt], bk)
            if kc % 4 == 3:
                X("act", "copy", ta[:, (kc // 4) * 512:(kc // 4 + 1) * 512], bk[:, :], rd=[bk], wr=[ta])
        for ti in range(nt):
            b0 = 0 if ti % 2 == 0 else 2
            tsl = slice(ti * 128, (ti + 1) * 128)
            if reload_x:
                dma("sp", xg[:, ti, :], Dr["xin"][off + ti * 128: off + (ti + 1) * 128, :], [], xg.r(ti), "xg%d" % ti)
            for hb in range(2):
                bk = banks[b0 + hb]
                for kc in range(KC):
                    mm(bk[:, :], out_fm[:, kc, tsl], h_fm[:, kc, hb * 512:(hb + 1) * 512], out_fm.r(kc) + toks(h_fm), bk,
                       start=(kc == 0), stop=(kc == KC - 1))
                X("act", "activation", tb[:, hb * 512:(hb + 1) * 512], bk[:, :], AF.Square, accum_out=ss3[:, hb:hb + 1], rd=[bk], wr=[tb, ss3])
            X("dve", "tensor_tensor", rs3[:, :], ss3[:, 0:1], ss3[:, 1:2], ALU.add, rd=[ss3], wr=[rs3])
            X("act", "activation", rs3[:, :], rs3[:, :], AF.Ln, bias=eps_cols["rms"][:], scale=1.0 / D, rd=[rs3, eps_cols["rms"]], wr=[rs3])
            X("act", "activation", rs3[:, :], rs3[:, :], AF.Exp, scale=-0.5, rd=[rs3], wr=[rs3])
            for hb in range(2):
                bk = banks[b0 + hb]
                sl = slice(hb * 512, (hb + 1) * 512)
                X("dve", "scalar_tensor_tensor", tb[:, sl], bk[:, :], rs3[:, 0:1], ta[:, sl], ALU.mult, ALU.mult, rd=[bk, rs3, ta], wr=[tb])
            X("pool", "tensor_tensor", xg[:, ti, :], xg[:, ti, :], tb[:, 0:D], ALU.add, rd=xg.r(ti) + [tb], wr=xg.r(ti))
            if final:
                o = dma("sp", y_out[off + ti * 128: off + (ti + 1) * 128, :], xg[:, ti, :], xg.r(ti), [], "yo%d" % ti)
                out_ops.append(o)

    def layer0(g):
        phase1(0, g, True)
        lora_stage(g)
        load_chunk_weights(0)
        ci = load_chunk_consts(0)
        proj_stage(g, 0)
        for e_idx in range(KC):
            if e_idx + 1 < KC:
                load_chunk_weights(e_idx + 1)
                ci_next = load_chunk_consts(e_idx + 1)
                wkv_chunk(g, e_idx, ci, (lambda: project_mm(g, 0, 1)))
                proj_stage(g, e_idx + 1, pre_mm0=True)
            else:
                wkv_chunk(g, e_idx, ci)
            ci = ci_next
        phase3(0, g, Dr["w_out"], True, False)


    def load_w_plain(slot, src_ap):
        wt = wring[slot]
        dma("pool", wt[:], src_ap.rearrange("(kc p) e -> p kc e", p=128), [], wt, "w%d" % slot)

    def project_plain(g, slot, bank_base):
        off, lens, cond, Tg, nt, nblk = ginfo(g)
        wt = wring[slot]
        for b in range(nblk):
            bk = banks[bank_base + b]
            for kc in range(KC):
                mm(bk[:, :], wt[:, kc, :], h_fm[:, kc, b * 512:(b + 1) * 512], [wt] + h_fm.r(kc), bk, start=(kc == 0), stop=(kc == KC - 1))

    def layer1(g, from_dram=False):
        off, lens, cond, Tg, nt, nblk = ginfo(g)
        ta, tb, ksum = cf["ta"], cf["tb"], cf["ksum"]
        ident, ones = C["ident"], C["ones"]
        phase1(1, g, from_dram)
        seglen = 64 if g == "S" else 256
        nseg = Tg // seglen
        segw = seglen + CK - 1
        spb = 512 // seglen
        zpad_w = Vw(zpad_t[:, 0:nseg * segw], zpad_t.toks)
        zpad3 = zpad_w[:, :].rearrange("p (s w) -> p s w", w=segw)
        X("dve", "tensor_scalar", zpad_w[:, :], ones[:, 0:1].to_broadcast([128, nseg * segw]), 0.0, None, ALU.mult, rd=[ones], wr=[zpad_w])
        dgflat = wpall[:, :, :, :].rearrange("p a b c -> p (a b c)")[:, 0:CK * 128]
        dg = Vw(dgflat.rearrange("p (k c) -> p k c", c=128), wpall.toks)
        zc = Vw(out_fm[:, :, :].bitcast(F32), out_fm.toks)
        cw = Dr["cv_w_in"]

        def load_conv_chunk(e_idx):
            sp_ = (e_idx % 2) * 2
            load_w_plain(sp_, cw[:, e_idx * 128:(e_idx + 1) * 128])
            load_w_plain(sp_ + 1, cw[:, D + e_idx * 128:D + (e_idx + 1) * 128])

        load_conv_chunk(0)
        for e_idx in range(KC):
            sp_ = (e_idx % 2) * 2
            project_plain(g, sp_, 0)
            project_plain(g, sp_ + 1, 2)
            if e_idx + 1 < KC:
                load_conv_chunk(e_idx + 1)
            X("dve", "tensor_tensor", dg[:, :, :].bitcast(F32R), ident[:, :].unsqueeze(1).to_broadcast([128, CK, 128]),
              cvdw[:, e_idx, :].unsqueeze(2).to_broadcast([128, CK, 128]), ALU.mult, rd=[ident, cvdw], wr=[dg])
            for b in range(nblk):
                sl = slice(b * 512, (b + 1) * 512)
                bv, bg = banks[b], banks[2 + b]
                X("act", "activation", ta[:, sl], bg[:, :], AF.Exp, scale=-1.0, rd=[bg], wr=[ta])
                X("act", "activation", ta[:, sl], ta[:, sl], AF.Ln, bias=one_col[:], rd=[ta, one_col], wr=[ta])
                X("act", "activation", ta[:, sl], ta[:, sl], AF.Exp, scale=-1.0, rd=[ta], wr=[ta])
                X("dve", "tensor_tensor", zpad3[:, b * spb:(b + 1) * spb, CK // 2:CK // 2 + seglen],
                  bv[:, :].rearrange("p (s l) -> p s l", l=seglen), ta[:, sl].rearrange("p (s l) -> p s l", l=seglen), ALU.mult,
                  rd=[bv, ta], wr=[zpad_w])
            for b in range(nblk):
                sl = slice(b * 512, (b + 1) * 512)
                bk = banks[4 + b]
                for k in range(CK):
                    mm(bk[:, :].rearrange("p (s l) -> p s l", l=seglen), dg[:, k, :].bitcast(F32R), zpad3[:, b * spb:(b + 1) * spb, k:k + seglen],
                       [dg, zpad_w], bk, start=(k == 0), stop=(k == CK - 1))
                X("act", "activation", out_fm[:, e_idx, sl], bk[:, :], AF.Identity, bias=cvcols[:, e_idx, 0:1], rd=[bk, cvcols], wr=out_fm.r(e_idx))
        for b in range(nblk):
            sl = slice(b * 512, (b + 1) * 512)
            for kc in range(KC):
                mm(banks[0][:, :], ones[:, :], zc[:, kc, sl], [ones] + out_fm.r(kc), banks[0], start=(kc == 0), stop=(kc == KC - 1))
            for kc in range(KC):
                hs = slice((kc % 2) * 512, (kc % 2 + 1) * 512)
                X("act", "activation", ta[:, hs], zc[:, kc, sl], AF.Square, rd=out_fm.r(kc), wr=[ta])
                mm(banks[1][:, :], ones[:, :], ta[:, hs], [ones, ta], banks[1], start=(kc == 0), stop=(kc == KC - 1))
            X("act", "activation", lw_fm[:, sl], banks[0][:, :], AF.Identity, scale=1.0 / D, rd=[banks[0]], wr=[lw_fm])
            X("dve", "tensor_tensor", tb[:, sl], lw_fm[:, sl], lw_fm[:, sl], ALU.mult, rd=[lw_fm], wr=[tb])
            X("dve", "scalar_tensor_tensor", tb[:, sl], banks[1][:, :], 1.0 / D, tb[:, sl], ALU.mult, ALU.subtract, rd=[banks[1], tb], wr=[tb])
            X("act", "activation", tb[:, sl], tb[:, sl], AF.Ln, bias=eps_cols["ln"][:], rd=[tb, eps_cols["ln"]], wr=[tb])
            X("act", "activation", la_fm[:, sl], tb[:, sl], AF.Exp, scale=-0.5, rd=[tb], wr=[la_fm])
        load_w_plain(0, cw[:, 2 * D:2 * D + 128])
        for e_idx in range(KC):
            slot = e_idx % 4
            if e_idx + 1 < KC:
                load_w_plain((e_idx + 1) % 4, cw[:, 2 * D + (e_idx + 1) * 128:2 * D + (e_idx + 2) * 128])
            project_plain(g, slot, 2)
            for b in range(nblk):
                sl = slice(b * 512, (b + 1) * 512)
                bk = banks[2 + b]
                X("act", "activation", ksum[:, sl], bk[:, :], AF.Exp, scale=-1.0, rd=[bk], wr=[ksum])
                X("act", "activation", ksum[:, sl], ksum[:, sl], AF.Ln, bias=one_col[:], rd=[ksum, one_col], wr=[ksum])
                X("act", "activation", ksum[:, sl], ksum[:, sl], AF.Exp, scale=-1.0, rd=[ksum], wr=[ksum])
                X("dve", "tensor_tensor", ksum[:, sl], ksum[:, sl], bk[:, :], ALU.mult, rd=[ksum, bk], wr=[ksum])
            X("dve", "tensor_tensor", ta[:, 0:Tg], zc[:, e_idx, 0:Tg], lw_fm[:, 0:Tg], ALU.subtract, rd=out_fm.r(e_idx) + [lw_fm], wr=[ta])
            X("dve", "tensor_tensor", ta[:, 0:Tg], ta[:, 0:Tg], la_fm[:, 0:Tg], ALU.mult, rd=[ta, la_fm], wr=[ta])
            X("act", "activation", tb[:, 0:Tg], ta[:, 0:Tg], AF.Identity, bias=cvcols[:, e_idx, 2:3], scale=cvcols[:, e_idx, 1:2], rd=[ta, cvcols], wr=[tb])
            X("act", "activation", ta[:, 0:Tg], tb[:, 0:Tg], AF.Exp, scale=-1.0, rd=[tb], wr=[ta])
            X("act", "activation", ta[:, 0:Tg], ta[:, 0:Tg], AF.Ln, bias=one_col[:], rd=[ta, one_col], wr=[ta])
            X("act", "activation", ta[:, 0:Tg], ta[:, 0:Tg], AF.Exp, scale=-1.0, rd=[ta], wr=[ta])
            X("dve", "tensor_tensor", tb[:, 0:Tg], tb[:, 0:Tg], ta[:, 0:Tg], ALU.mult, rd=[ta, tb], wr=[tb])
            X("dve", "tensor_tensor", out_fm[:, e_idx, 0:Tg], tb[:, 0:Tg], ksum[:, 0:Tg], ALU.mult, rd=[tb, ksum], wr=out_fm.r(e_idx))
        phase3(1, g, Dr["cv_w_out"], False, True)

    if stage == "full":
        for g in ("S", "P"):
            layer0(g)
            layer1(g)
        P.emit(out_ops)
        print("ops", len(P.ops), "sems", P.n_sems, "eng_cnt", P.eng_cnt, "sb_bytes", P.sb_bytes, flush=True)
        P.close()
        return nc
    if stage == "l1":
        import os
        g = os.environ.get("WKV_G", "P")
        layer1(g, True)
        P.emit(out_ops)
        P.close()
        return nc
    if stage == "l0":
        import os
        g = os.environ.get("WKV_G", "P")
        layer0(g)
        nt = ginfo(g)[4]
        for ti in range(nt):
            dbg_dump(xg[:, ti, :], xg.r(ti), ti * D, D)
        P.emit(out_ops)
        print("ops", len(P.ops), "sems", P.n_sems, "eng_cnt", P.eng_cnt, "sb_bytes", P.sb_bytes)
        P.close()
        return nc
    if stage == "wkv":
        import os
        g = os.environ.get("WKV_G", "P")
        e_idx = 3
        phase1(0, g, True)
        lora_stage(g)
        load_chunk_weights(e_idx)
        ci = load_chunk_consts(e_idx)
        proj_stage(g, e_idx)
        wkv_chunk(g, e_idx, ci)
        Tg = ginfo(g)[3]
        dbg_dump(out_fm[:, e_idx, 0:Tg].bitcast(F32), out_fm, 0, Tg)
        dbg_dump(o_acc[:, 0:Tg // 128, :].rearrange("p t c -> p (t c)"), o_acc, Tg, Tg)
        P.emit(out_ops)
        print("ops", len(P.ops), "sems", P.n_sems, "eng_cnt", P.eng_cnt, "sb_bytes", P.sb_bytes)
        P.close()
        return nc
    if stage == "proj":
        import os
        cut = int(os.environ.get("PROJ_CUT", "9"))
        g = "P"
        phase1(0, g, True)
        if cut >= 2:
            load_w(0, Dr["w1cat"], 4)
        if cut >= 3:
            load_w(1, Dr["a1cat"], 5)
            project(g, 0, lw_fm)
        if cut >= 4:
            lora_stage(g)
        if cut >= 5:
            load_chunk_weights(3)
            proj_stage(g, 3)
        Tg = ginfo(g)[3]
        col = 0
        for t in (lw_fm, la_fm, cf["r"], cf["k"], cf["v"], cf["sg"]):
            dbg_dump(t[:, 0:Tg], t, col, Tg)
            col += Tg
        dbg_dump(h_fm[:, 5, 0:Tg].bitcast(F32), h_fm, col, Tg)
        P.emit(out_ops)
        P.close()
        return nc
    return nc


_NC_CACHE = {}


def kernel(**inputs):
    maps = prep_inputs(inputs)
    if "nc" not in _NC_CACHE:
        _NC_CACHE["nc"] = build(stage="full")
    nc = _NC_CACHE["nc"]
    res = run_bass_kernel_spmd(nc, maps, core_ids=list(range(NCORES)))
    y_prompt = np.zeros((16, 256, D), np.float32)
    y_sample = np.zeros((8, 1024, D), np.float32)
    new_state = np.zeros((16, 1, 2, NH, HD, HD), np.float32)
    for i in range(NCORES):
        r = res.results[i]
        yo = np.asarray(r["y_out"])
        y_prompt[2 * i] = yo[0:256]
        y_prompt[2 * i + 1] = yo[256:512]
        y_sample[i] = yo[512:1536]
        st = np.asarray(r["st_out"])
        for si in range(2):
            new_state[2 * i + si, 0] = st[si].transpose(0, 1, 3, 2)
    return (y_prompt, y_sample, new_state)
```

```python
import numpy as np
from contextlib import ExitStack
import concourse.bass as bass
import concourse.mybir as mybir
from concourse.bass_utils import run_bass_kernel_spmd

F32 = mybir.dt.float32
F32R = mybir.dt.float32r
AF = mybir.ActivationFunctionType
ALU = mybir.AluOpType
AX = mybir.AxisListType

ENGS = ("pe", "act", "dve", "pool", "sp")
NCORES = 8
D = 1024
KC = 8
NH = 16
HD = 64
CK = 31
RMS_EPS = 1e-6
GN_EPS = 64e-5
LN_EPS = 1e-5
DEC_C = float(np.exp(-0.5))
import os as _os
WDT = F32 if _os.environ.get('WKV_F32') == '1' else F32R
YDT = F32R if _os.environ.get('WKV_YR') == '1' else F32


class Tok:
    __slots__ = ("name", "lw", "rd", "excl")

    def __init__(self, name="", excl=False):
        self.name = name
        self.lw = None
        self.rd = []
        self.excl = excl


class Op:
    __slots__ = ("eng", "fn", "deps", "signal", "semkey", "semval", "idx", "isdma", "waits")

    def __init__(self, eng, fn, isdma=False, semkey=None):
        self.eng = eng
        self.fn = fn
        self.deps = []
        self.signal = False
        self.semkey = semkey
        self.semval = None
        self.isdma = isdma
        self.waits = []


class Tl:
    def __init__(self, t, name, nreg=1, excl=False):
        self.t = t
        self.toks = [Tok(f"{name}.{i}", excl) for i in range(nreg)]

    def __getitem__(self, k):
        return self.t[k]

    def all(self):
        return list(self.toks)

    def r(self, i):
        return [self.toks[i]]


class Vw(Tl):
    def __init__(self, ap, tk):
        self.t = ap
        self.toks = list(tk)


class Prog:
    def __init__(self, nc):
        self.nc = nc
        self.ops = []
        self.stack = ExitStack()
        self.nt = 0
        self.sb_bytes = 0
        self.zcol = None

    def sb(self, shape, dtype=F32, name=None, nreg=1, zero=True):
        self.nt += 1
        name = (name or "t") + f"_{self.nt}"
        t = self.stack.enter_context(self.nc.sbuf_tensor(name, list(shape), dtype))
        self.sb_bytes += int(np.prod(shape[1:])) * 4
        tl = Tl(t, name, nreg)
        if not zero:
            return tl
        if self.zcol is None:
            self.zcol = tl
            self.op("dve", lambda e: e.memset(t[:], 0.0), writes=tl.toks)
        else:
            eng = ("dve", "pool")[self.nt % 2]
            if dtype == F32:
                self.op(eng, lambda e: e.memset(t[:], 0.0), writes=tl.toks)
            else:
                z = self.zcol.t[0:shape[0], 0:1]
                for _ in range(len(shape) - 2):
                    z = z.unsqueeze(2)
                zb = z.to_broadcast(list(shape))
                self.op(eng, lambda e: e.tensor_scalar(t[:], zb, 0.0, None, ALU.mult), reads=self.zcol.toks, writes=tl.toks)
        return tl

    def ps(self, shape, dtype=F32, name=None, nreg=1):
        self.nt += 1
        name = (name or "p") + f"_{self.nt}"
        t = self.stack.enter_context(self.nc.psum_tensor(name, list(shape), dtype))
        return Tl(t, name, nreg, excl=True)

    def op(self, eng, fn, reads=(), writes=(), isdma=False, semkey=None):
        o = Op(eng, fn, isdma, semkey)
        o.idx = len(self.ops)
        deps = set()
        for r in reads:
            if r.lw is not None:
                deps.add(r.lw)
            if r.excl:
                for x in r.rd:
                    if x.eng != eng:
                        deps.add(x)
        for w in writes:
            if w.lw is not None:
                deps.add(w.lw)
            for x in w.rd:
                deps.add(x)
        for r in reads:
            r.rd.append(o)
        for w in writes:
            w.lw = o
            w.rd = []
        deps.discard(o)
        for d in deps:
            if d.eng == "pe" and eng == "pe" and not d.isdma and not isdma:
                continue
            if d.isdma and isdma and d.semkey == semkey and semkey.startswith("all:"):
                continue
            o.deps.append(d)
            d.signal = True
        self.ops.append(o)
        return o

    def emit(self, final_wait_ops=()):
        nc = self.nc
        eng_cnt = {e: 0 for e in ENGS}
        dma_cnt = {}
        for o in final_wait_ops:
            o.signal = True
        for o in self.ops:
            if o.isdma:
                o.signal = True
            if not o.signal:
                continue
            if o.isdma:
                k = o.semkey
                dma_cnt[k] = dma_cnt.get(k, 0) + 16
                o.semval = dma_cnt[k]
            else:
                eng_cnt[o.eng] += 1
                o.semval = eng_cnt[o.eng]
        for o in self.ops:
            if o.isdma and o.semkey.startswith("all:"):
                o.semval = dma_cnt[o.semkey]
        eng_sem = {}
        dma_sem = {}
        for e in ENGS:
            eng_sem[e] = self.stack.enter_context(nc.semaphore("sem_" + e))
        for i, k in enumerate(dma_cnt):
            dma_sem[k] = self.stack.enter_context(nc.semaphore("dsem_%d" % i))
        self.n_sems = len(eng_sem) + len(dma_sem)
        self.eng_cnt = eng_cnt

        def semof(o):
            return dma_sem[o.semkey] if o.isdma else eng_sem[o.eng]

        per_eng = {e: [] for e in ENGS}
        for o in self.ops:
            per_eng[o.eng].append(o)
        waited = {e: {} for e in ENGS}
        for o in self.ops:
            w = waited[o.eng]
            need = {}
            for d in o.deps:
                s = semof(d)
                key = id(s)
                v = d.semval
                if w.get(key, 0) >= v:
                    continue
                if key not in need or need[key][1] < v:
                    need[key] = (s, v)
            for key, (s, v) in need.items():
                w[key] = v
                o.waits.append((s, v))
        finals = [(semof(o), o.semval) for o in final_wait_ops]
        block = self.stack.enter_context(nc.Block())

        def run(engobj, lst, is_sp=False):
            for o in lst:
                for (s, v) in o.waits:
                    engobj.wait_ge(s, v)
                ins = o.fn(engobj)
                if o.signal:
                    ins.then_inc(semof(o), 16 if o.isdma else 1)
            if is_sp:
                for (s, v) in finals:
                    engobj.wait_ge(s, v)

        @block.tensor
        def _(e):
            run(e, per_eng["pe"])

        @block.scalar
        def _(e):
            run(e, per_eng["act"])

        @block.vector
        def _(e):
            run(e, per_eng["dve"])

        @block.gpsimd
        def _(e):
            run(e, per_eng["pool"])

        @block.sync
        def _(e):
            run(e, per_eng["sp"], True)

    def close(self):
        self.stack.close()


def make_consts():
    c = {}
    c["ident"] = np.eye(128, dtype=np.float32)
    ob = np.zeros((128, 128), np.float32)
    ob[:64, :64] = 1
    ob[64:, 64:] = 1
    c["ones_blk"] = ob
    p = np.arange(128)[:, None]
    f = np.arange(128)[None, :]
    lt = (p < f).astype(np.float32)
    le = (p <= f).astype(np.float32)
    gt = (p > f).astype(np.float32)
    ge = (p >= f).astype(np.float32)
    c["cmF"] = np.concatenate([le, lt], 1)
    c["cmB"] = np.concatenate([ge, gt], 1)
    c["maskF"] = np.concatenate([-lt, lt, le, -gt, -le], 1)
    c["maskB"] = np.concatenate([-gt, gt, ge, -lt, -ge], 1)
    c["ones"] = np.ones((128, 128), np.float32)
    c["i64x2"] = np.concatenate([np.eye(64), np.eye(64)], 0).astype(np.float32)
    return c


GROUPS = {
    "S": (512, [1024], 1),
    "P": (0, [256, 256], 0),
}


def dram_specs():
    s = {}
    s["xin"] = [1536, D]
    s["condT"] = [128, KC, 2]
    s["st0"] = [2, NH, HD, HD]
    s["ada_w"] = [2, D, 3 * D]
    s["adab_fm"] = [2, 128, 24]
    s["npre_fm"] = [2, 128, KC]
    s["npost_fm"] = [2, 128, KC]
    s["mu_fm"] = [128, 6, KC]
    s["w_in"] = [4, D, D]
    s["w1cat"] = [D, 128]
    s["a1cat"] = [D, 128]
    s["w2cat"] = [128, D]
    s["a2cat"] = [128, D]
    s["colpack"] = [KC, 128, 9]
    s["w_out"] = [D, D]
    s["cv_w_in"] = [D, 3 * D]
    s["cv_dw"] = [KC, 128, CK]
    s["cv_cols"] = [KC, 128, 3]
    s["cv_w_out"] = [D, D]
    for k, v in make_consts().items():
        s[k] = list(v.shape)
    return s


def prep_inputs(inp):
    f = lambda a: np.ascontiguousarray(np.asarray(a, dtype=np.float32))
    sh = {}
    ada_b = np.asarray(inp["ada_b"], np.float32)
    sh["ada_w"] = f(inp["ada_w"])
    sh["adab_fm"] = f(ada_b.reshape(2, 24, 128).transpose(0, 2, 1))
    sh["npre_fm"] = f(np.asarray(inp["norm_pre"]).reshape(2, KC, 128).transpose(0, 2, 1))
    sh["npost_fm"] = f(np.asarray(inp["norm_post"]).reshape(2, KC, 128).transpose(0, 2, 1))
    sh["mu_fm"] = f(np.asarray(inp["rw_mu"])[0].reshape(6, KC, 128).transpose(2, 0, 1))
    sh["w_in"] = f(np.asarray(inp["rw_w_in"])[0])
    w1 = np.asarray(inp["rw_w1"])[0]
    a1 = np.asarray(inp["rw_a1"])[0]
    sh["w1cat"] = f(np.concatenate([w1[0], w1[1]], 1))
    sh["a1cat"] = f(np.concatenate([a1[0], a1[1]], 1))
    sh["w2cat"] = f(np.asarray(inp["rw_w2"])[0].reshape(128, D))
    sh["a2cat"] = f(np.asarray(inp["rw_a2"])[0].reshape(128, D))
    cols = [np.asarray(inp["rw_w0"])[0, 0], np.asarray(inp["rw_w0"])[0, 1],
            np.asarray(inp["rw_a0"])[0, 0], np.asarray(inp["rw_a0"])[0, 1],
            np.asarray(inp["rw_k_k"])[0], np.asarray(inp["rw_k_a"])[0],
            np.asarray(inp["rw_r_k"])[0].reshape(D), np.asarray(inp["rw_lnx_g"])[0],
            np.asarray(inp["rw_lnx_b"])[0]]
    sh["colpack"] = f(np.stack(cols, 1).reshape(KC, 128, 9))
    sh["w_out"] = f(np.asarray(inp["rw_w_out"])[0])
    sh["cv_w_in"] = f(np.asarray(inp["cv_w_in"])[0])
    sh["cv_dw"] = f(np.asarray(inp["cv_dw_w"])[0].T.reshape(KC, 128, CK))
    cc = [np.asarray(inp["cv_dw_b"])[0], np.asarray(inp["cv_ln_g"])[0], np.asarray(inp["cv_ln_b"])[0]]
    sh["cv_cols"] = f(np.stack(cc, 1).reshape(KC, 128, 3))
    sh["cv_w_out"] = f(np.asarray(inp["cv_w_out"])[0])
    sh.update(make_consts())
    xp = np.asarray(inp["x_prompt"], np.float32)
    xs = np.asarray(inp["x_sample"], np.float32)
    st = np.asarray(inp["state_rwkv"], np.float32)
    c = np.asarray(inp["c"], np.float32)
    cctx = np.asarray(inp["c_ctx"], np.float32)
    maps = []
    for i in range(NCORES):
        m = dict(sh)
        m["xin"] = f(np.concatenate([xp[2 * i].reshape(256, D), xp[2 * i + 1].reshape(256, D), xs[i]], 0))
        cond = np.stack([cctx, c[i]], 1)
        m["condT"] = f(cond.reshape(KC, 128, 2).transpose(1, 0, 2))
        m["st0"] = f(st[i, 0].transpose(0, 1, 3, 2))
        maps.append(m)
    return maps


def toks(*xs):
    out = []
    for x in xs:
        if isinstance(x, Tl):
            out.extend(x.toks)
        elif isinstance(x, Tok):
            out.append(x)
        else:
            out.extend(toks(*x))
    return out


def build(stage="full", dbg_shape=None):
    nc = bass.Bass("TRN2", target_bir_lowering=False)
    specs = dram_specs()
    Dr = {k: nc.dram_tensor(k, shp, F32, kind="ExternalInput").ap() for k, shp in specs.items()}
    y_out = nc.dram_tensor("y_out", [1536, D], F32, kind="ExternalOutput").ap()
    st_out = nc.dram_tensor("st_out", [2, 2, NH, HD, HD], F32, kind="ExternalOutput").ap()
    dbg = None
    if dbg_shape is not None:
        dbg = nc.dram_tensor("dbg", list(dbg_shape), F32, kind="ExternalOutput").ap()
    P = Prog(nc)
    out_ops = []
    zcol_t = P.sb([128, 1], F32, name="zcol")

    def dma(eng, out_ap, in_ap, reads, writes, key):
        return P.op(eng, lambda e: e.dma_start(out=out_ap, in_=in_ap), reads=toks(reads), writes=toks(writes),
                    isdma=True, semkey=key)

    def dbg_dump(tile_ap, src, col0, ncols, rows=128):
        o = dma("sp", dbg[0:rows, col0:col0 + ncols], tile_ap, src, [], "dbg%d" % col0)
        out_ops.append(o)

    C = {}
    for k in make_consts():
        C[k] = P.sb(specs[k], F32, name=k, zero=False)
        dma("sp", C[k][:], Dr[k], [], C[k], "all:const")
    one_col = P.sb([128, 1], name="one")
    tiny_col = P.sb([128, 1], name="tiny")
    P.op("dve", lambda e: e.memset(one_col[:], 1.0), writes=toks(one_col))
    P.op("dve", lambda e: e.memset(tiny_col[:], 1e-24), writes=toks(tiny_col))
    eps_cols = {}
    for nm, val in (("rms", RMS_EPS), ("gn", GN_EPS), ("ln", LN_EPS)):
        eps_cols[nm] = P.sb([128, 1], name="eps" + nm)
        P.op("dve", (lambda t, v: lambda e: e.memset(t[:], v))(eps_cols[nm], val), writes=toks(eps_cols[nm]))

    mu_fm = P.sb([128, 6, KC], name="mu_fm", zero=False)
    dma("sp", mu_fm[:], Dr["mu_fm"], [], mu_fm, "all:const")

    cvcols = P.sb([128, KC, 3], name="cvcols", zero=False)
    cvdw = P.sb([128, KC, CK], name="cvdw", zero=False)
    for e_ in range(KC):
        dma("sp", cvcols[:, e_, :], Dr["cv_cols"][e_], [], cvcols, "all:const")
        dma("sp", cvdw[:, e_, :], Dr["cv_dw"][e_], [], cvdw, "all:const")
    muh = P.sb([128, 6, KC], name="muh")
    P.op("dve", lambda e: e.tensor_scalar(muh[:], mu_fm[:], 0.5, None, ALU.mult), reads=toks(mu_fm), writes=toks(muh))

    banks = [P.ps([128, 512], name="bank%d" % i, nreg=1) for i in range(8)]
    for bk_ in banks:
        P.op("dve", (lambda bk_: lambda e: e.memset(bk_[:, :], 0.0))(bk_), writes=toks(bk_))

    def bq(b, c0, c1):
        return list(banks[b].toks)

    def bh(b, half):
        return list(banks[b].toks)

    TG = 1024
    NT = 8
    xg = P.sb([128, NT, D], F32, name="xg", nreg=NT, zero=False)
    h_fm = P.sb([128, KC, TG], F32R, name="h_fm", nreg=KC, zero=False)
    out_fm = P.sb([128, KC, TG], F32R, name="out_fm", nreg=KC, zero=False)

    condT = P.sb([128, KC, 2], name="condT", zero=False)
    dma("sp", condT[:], Dr["condT"], [], condT, "all:const")
    scond = P.sb([128, KC, 2], name="scond")
    stmp = P.sb([128, KC, 2], name="stmp")
    P.op("act", lambda e: e.activation(stmp[:], condT[:], AF.Exp, scale=-1.0), reads=toks(condT), writes=toks(stmp))
    P.op("dve", lambda e: e.tensor_scalar_add(stmp[:], stmp[:], 1.0), reads=toks(stmp), writes=toks(stmp))
    P.op("dve", lambda e: e.reciprocal(stmp[:], stmp[:]), reads=toks(stmp), writes=toks(stmp))
    P.op("dve", lambda e: e.tensor_tensor(scond[:], condT[:], stmp[:], ALU.mult), reads=toks(condT, stmp), writes=toks(scond))
    adab_fm = P.sb([128, 2, 24], name="adab_fm", zero=False)
    npre_fm = P.sb([128, 2, KC], name="npre_fm", zero=False)
    npost_fm = P.sb([128, 2, KC], name="npost_fm", zero=False)
    for l in range(2):
        dma("sp", adab_fm[:, l, :], Dr["adab_fm"][l], [], adab_fm, "all:const")
        dma("sp", npre_fm[:, l, :], Dr["npre_fm"][l], [], npre_fm, "all:const")
        dma("sp", npost_fm[:, l, :], Dr["npost_fm"][l], [], npost_fm, "all:const")
    modfm = P.sb([128, 2, 24, 2], name="modfm")
    scale_col = P.sb([128, 2, KC, 2], name="scale_col")
    gate_col = P.sb([128, 2, KC, 2], name="gate_col")
    stg = [Vw(xg[:, 4 * i:4 * i + 4, :].rearrange("p a (b c) -> p (a b) c", c=512), xg.toks[4 * i:4 * i + 4]) for i in range(2)]
    nstg = 0
    for l in range(2):
        for q in range(6):
            s_ = stg[nstg % 2]
            nstg += 1
            dma("sp", s_[:], Dr["ada_w"][l].rearrange("(kc p) c -> p kc c", p=128)[:, :, q * 512:(q + 1) * 512],
                [], s_, "adastg%d" % (nstg % 2))
            for b4 in range(4):
                cb = q * 4 + b4
                for kc in range(KC):
                    P.op("pe", (lambda s_, cb, b4, kc: lambda e: e.matmul(
                        banks[0][:, cb * 2:(cb + 1) * 2], s_[:, kc, b4 * 128:(b4 + 1) * 128], scond[:, kc, :],
                        start=(kc == 0), stop=(kc == KC - 1)))(s_, cb, b4, kc),
                         reads=toks(s_, scond), writes=toks(banks[0]))
        P.op("dve", (lambda l: lambda e: e.tensor_tensor(
            modfm[:, l, :, :], banks[0][:, 0:48].rearrange("p (c j) -> p c j", j=2),
            adab_fm[:, l, :].unsqueeze(2).to_broadcast([128, 24, 2]), ALU.add))(l),
             reads=toks(adab_fm, banks[0]), writes=toks(modfm))
        P.op("dve", (lambda l: lambda e: e.scalar_tensor_tensor(
            scale_col[:, l, :, :], modfm[:, l, 8:16, :], 1.0,
            npre_fm[:, l, :].unsqueeze(2).to_broadcast([128, KC, 2]), ALU.add, ALU.mult))(l),
             reads=toks(modfm, npre_fm), writes=toks(scale_col))
        P.op("dve", (lambda l: lambda e: e.tensor_tensor(
            gate_col[:, l, :, :], modfm[:, l, 16:24, :],
            npost_fm[:, l, :].unsqueeze(2).to_broadcast([128, KC, 2]), ALU.mult))(l),
             reads=toks(modfm, npost_fm), writes=toks(gate_col))
    if stage == "mod":
        dbg_dump(modfm[:].rearrange("p l c j -> p (l c j)"), modfm, 0, 96)
        dbg_dump(scale_col[:].rearrange("p l c j -> p (l c j)"), scale_col, 96, 32)
        dbg_dump(gate_col[:].rearrange("p l c j -> p (l c j)"), gate_col, 128, 32)
        P.emit(out_ops)
        P.close()
        return nc

    ss = P.sb([128, NT], name="ss")
    rstd = P.sb([128, NT], name="rstd")
    wring = [P.sb([128, KC, 128], F32R, name="wr%d" % i, zero=False) for i in range(4)]
    wpall = P.sb([128, 4, KC, 128], F32R, name="wpall", nreg=4, zero=False)
    wpring = [Vw(wpall[:, i], wpall.r(i)) for i in range(4)]
    lwla = P.sb([128, 2, TG], name="lwla", nreg=2, zero=False)
    lw_fm = Vw(lwla[:, 0, :], lwla.r(0))
    la_fm = Vw(lwla[:, 1, :], lwla.r(1))
    cf = {}
    for i, nm in enumerate(("r", "k", "v", "sg", "kk", "b", "kd", "sw")):
        cf[nm] = Vw(xg[:, i, :], xg.r(i))
    for nm in ("ksum", "ta", "tb"):
        cf[nm] = P.sb([128, TG + 4], name=nm)
    Bpad, tmpd = cf["ta"], cf["tb"]
    P.op("pool", lambda e: e.memset(Bpad[:], 0.0), writes=toks(Bpad))
    P.op("pool", lambda e: e.memset(tmpd[:], 0.0), writes=toks(tmpd))
    xn_buf = [cf["tb"], cf["ksum"]]
    junk = cf["ta"]
    xcnt = [0]

    def ginfo(g):
        off, lens, cond = GROUPS[g]
        Tg = sum(lens)
        return off, lens, cond, Tg, Tg // 128, Tg // 512

    def seq_pad_offsets(lens):
        offs = []
        o = 1
        for L in lens:
            offs.append(o)
            o += L + 2
        return offs, o - 1

    def phase1(l, g, from_dram):
        off, lens, cond, Tg, nt, nblk = ginfo(g)
        P.op("dve", lambda e: e.memset(ss[:], 0.0), writes=toks(ss))
        for ti in range(nt):
            if from_dram:
                dma("sp", xg[:, ti, :], Dr["xin"][off + ti * 128: off + (ti + 1) * 128, :], [], xg.r(ti), "xg%d" % ti)
            P.op("act", (lambda ti: lambda e: e.activation(junk[:, 0:D], xg[:, ti, :], AF.Square, accum_out=ss[:, ti:ti + 1]))(ti),
                 reads=xg.r(ti), writes=toks(junk, ss))
        P.op("act", lambda e: e.activation(rstd[:, 0:nt], ss[:, 0:nt], AF.Ln, bias=eps_cols["rms"][:], scale=1.0 / D),
             reads=toks(ss, eps_cols["rms"]), writes=toks(rstd))
        P.op("act", lambda e: e.activation(rstd[:, 0:nt], rstd[:, 0:nt], AF.Exp, scale=-0.5), reads=toks(rstd), writes=toks(rstd))
        for ti in range(nt):
            xn = xn_buf[ti % 2]
            P.op("dve", (lambda ti, xn: lambda e: e.tensor_scalar(xn[:, 0:D], xg[:, ti, :], rstd[:, ti:ti + 1], None, ALU.mult))(ti, xn),
                 reads=xg.r(ti) + toks(rstd), writes=toks(xn))
            b0 = 0 if ti % 2 == 0 else 2
            for kc in range(KC):
                bk = banks[b0 + kc // 4]
                c0 = (kc % 4) * 128
                P.op("pe", (lambda bk, c0, xn, kc: lambda e: e.transpose(bk[:, c0:c0 + 128], xn[:, kc * 128:(kc + 1) * 128], C["ident"][:]))(bk, c0, xn, kc),
                     reads=toks(xn, C["ident"]), writes=toks(bk))
            for kc in range(KC):
                bk = banks[b0 + kc // 4]
                c0 = (kc % 4) * 128
                if kc % 2 == 0:
                    P.op("act", (lambda bk, c0, kc, ti: lambda e: e.activation(
                        h_fm[:, kc, ti * 128:(ti + 1) * 128], bk[:, c0:c0 + 128], AF.Identity,
                        bias=modfm[:, l, kc, cond:cond + 1], scale=scale_col[:, l, kc, cond:cond + 1]))(bk, c0, kc, ti),
                         reads=toks(bk, modfm, scale_col), writes=h_fm.r(kc))
                else:
                    P.op("dve", (lambda bk, c0, kc, ti: lambda e: e.tensor_scalar(
                        h_fm[:, kc, ti * 128:(ti + 1) * 128], bk[:, c0:c0 + 128],
                        scale_col[:, l, kc, cond:cond + 1], modfm[:, l, kc, cond:cond + 1], ALU.mult, ALU.add))(bk, c0, kc, ti),
                         reads=toks(bk, modfm, scale_col), writes=h_fm.r(kc))

    def load_w(slot, src_ap, mu_idx):
        wt, wp = wring[slot], wpring[slot]
        dma("pool", wt[:], src_ap.rearrange("(kc p) e -> p kc e", p=128), [], wt, "w%d" % slot)
        P.op("pool", lambda e: e.tensor_tensor(wp[:], wt[:], muh[:, mu_idx, :].unsqueeze(2).to_broadcast([128, KC, 128]), ALU.mult),
             reads=toks(wt, muh), writes=toks(wp))

    def project_mm(g, slot, bset=0):
        off, lens, cond, Tg, nt, nblk = ginfo(g)
        wt, wp = wring[slot], wpring[slot]
        for b in range(nblk):
            bA, bB = banks[4 * bset + b], banks[4 * bset + 2 + b]
            for (bk, w) in ((bA, wt), (bB, wp)):
                for kc in range(KC):
                    P.op("pe", (lambda bk, w, kc, b: lambda e: e.matmul(
                        bk[:, :], w[:, kc, :], h_fm[:, kc, b * 512:(b + 1) * 512], start=(kc == 0), stop=(kc == KC - 1)))(bk, w, kc, b),
                         reads=toks(w) + h_fm.r(kc), writes=toks(bk))

    def project_post(g, slot, out_t, bset=0):
        off, lens, cond, Tg, nt, nblk = ginfo(g)
        offs, width = seq_pad_offsets(lens)
        for b in range(nblk):
            bA, bB = banks[4 * bset + b], banks[4 * bset + 2 + b]
            P.op("act", (lambda bA, b: lambda e: e.copy(out_t[:, b * 512:(b + 1) * 512], bA[:, :]))(bA, b),
                 reads=toks(bA), writes=toks(out_t))
            t0 = b * 512
            pos = 0
            for si, L in enumerate(lens):
                lo, hi = max(t0, pos), min(t0 + 512, pos + L)
                if lo < hi:
                    P.op("act", (lambda bB, lo, hi, t0, po: lambda e: e.copy(Bpad[:, po:po + (hi - lo)], bB[:, lo - t0:hi - t0]))(
                        bB, lo, hi, t0, offs[si] + lo - pos), reads=toks(bB), writes=toks(Bpad))
                pos += L
        W = width + 1
        P.op("dve", lambda e: e.tensor_tensor(tmpd[:, 1:W - 1], Bpad[:, 0:W - 2], Bpad[:, 2:W], ALU.add),
             reads=toks(Bpad), writes=toks(tmpd))
        P.op("dve", lambda e: e.scalar_tensor_tensor(tmpd[:, 1:W - 1], Bpad[:, 1:W - 1], -2.0, tmpd[:, 1:W - 1], ALU.mult, ALU.add),
             reads=toks(Bpad, tmpd), writes=toks(tmpd))
        pos = 0
        for si, L in enumerate(lens):
            P.op("dve", (lambda pos, L, po: lambda e: e.tensor_tensor(out_t[:, pos:pos + L], out_t[:, pos:pos + L], tmpd[:, po:po + L], ALU.add))(pos, L, offs[si]),
                 reads=toks(tmpd, out_t), writes=toks(out_t))
            pos += L

    def project(g, slot, out_t, bset=0):
        project_mm(g, slot, bset)
        project_post(g, slot, out_t, bset)

    def sigmoid(out_ap, in_ap, rd, wr, tmp_t, Tg, nbias_ap=None, scale=1.0):
        kw = dict(scale=-scale)
        rdx = toks(rd)
        if nbias_ap is not None:
            kw["bias"] = nbias_ap[0]
            rdx = rdx + toks(nbias_ap[1])
        P.op("act", lambda e: e.activation(tmp_t[:, 0:Tg], in_ap, AF.Exp, **kw), reads=rdx, writes=toks(tmp_t))
        P.op("act", lambda e: e.activation(tmp_t[:, 0:Tg], tmp_t[:, 0:Tg], AF.Ln, bias=one_col[:]), reads=toks(tmp_t, one_col), writes=toks(tmp_t))
        P.op("act", lambda e: e.activation(out_ap, tmp_t[:, 0:Tg], AF.Exp, scale=-1.0), reads=toks(tmp_t), writes=toks(wr))

    def zero_pads(g):
        off, lens, cond, Tg, nt, nblk = ginfo(g)
        offs_, width_ = seq_pad_offsets(lens)
        for si_, L_ in enumerate(lens):
            for col in (offs_[si_] - 1, offs_[si_] + L_):
                X("pool", "memset", Bpad[:, col:col + 1], 0.0, wr=[Bpad])

    def lora_stage(g):
        off, lens, cond, Tg, nt, nblk = ginfo(g)
        zero_pads(g)
        load_w(0, Dr["w1cat"], 4)
        load_w(1, Dr["a1cat"], 5)
        project(g, 0, lw_fm, 0)
        project(g, 1, la_fm, 1)
        sigmoid(cf["ta"][:, 0:Tg], lw_fm[:, 0:Tg], lw_fm, cf["ta"], cf["tb"], Tg, scale=2.0)
        P.op("dve", lambda e: e.tensor_scalar(lw_fm[:, 0:Tg], cf["ta"][:, 0:Tg], 2.0, -1.0, ALU.mult, ALU.add),
             reads=toks(cf["ta"]), writes=toks(lw_fm))

    def proj_stage(g, e_idx, pre_mm0=False):
        off, lens, cond, Tg, nt, nblk = ginfo(g)
        zero_pads(g)
        for n, nm in enumerate(("r", "k", "v", "sg")):
            bset = (n + 1) % 2
            if not (n == 0 and pre_mm0):
                project_mm(g, n, bset)
            project_post(g, n, cf[nm], bset)
        sigmoid(cf["ta"][:, 0:Tg], cf["sg"][:, 0:Tg], cf["sg"], cf["ta"], cf["tb"], Tg)
        P.op("dve", lambda e: e.tensor_tensor(cf["sg"][:, 0:Tg], cf["sg"][:, 0:Tg], cf["ta"][:, 0:Tg], ALU.mult),
             reads=toks(cf["sg"], cf["ta"]), writes=toks(cf["sg"]))

    def load_chunk_weights(e_idx):
        for n in range(4):
            load_w(n, Dr["w_in"][n][:, e_idx * 128:(e_idx + 1) * 128], n)


    ccb = [P.sb([128, 9], name="cc%d" % i) for i in range(2)]
    w2c = [P.sb([128, 128], name="w2c%d" % i) for i in range(2)]
    a2c = [P.sb([128, 128], name="a2c%d" % i) for i in range(2)]
    dcol = P.sb([128, 8], name="dcol")
    TM_2 = [P.sb([128, 4, 128], WDT, name="TM%d" % i) for i in range(3)]
    LwT = P.sb([128, 128], name="LwT")
    EF = P.sb([128, 384], name="EF")
    ET = P.sb([128, 256], name="ET")
    BK_2 = [P.sb([128, 256], WDT, name="BK%d" % i) for i in range(2)]
    QRP_2 = [P.sb([128, 320], WDT, name="QRP%d" % i) for i in range(2)]
    Zt_2 = [P.sb([128, 2, 128], WDT, name="Zt%d" % i) for i in range(2)]
    AM_2 = [P.sb([128, 2, 448], WDT, name="AM%d" % i, nreg=2) for i in range(2)]
    Kd_2 = [P.sb([128, 128], WDT, name="Kd%d" % i) for i in range(2)]
    YPTall = P.sb([128, 2, 2, 384], YDT, name="YPT", nreg=4)
    YPT = [Vw(YPTall[:, i], YPTall.toks[2 * i:2 * i + 2]) for i in range(2)]
    WU = P.sb([128, 2, 128], WDT, name="WU", nreg=2)
    TinvR = P.sb([128, 2, 128], WDT, name="TinvR", nreg=2)
    zpad_t = P.sb([128, 16 * (64 + CK - 1)], F32R, name="zpad")
    QMs = P.sb([64, 2, 192], WDT, name="QMs", nreg=2)
    ST = [P.sb([64, 2, 64], WDT, name="ST%d" % d) for d in range(2)]
    identR = P.sb([128, 128], WDT, name="identR")
    P.op("dve", lambda e: e.tensor_copy(identR[:], C["ident"][:]), reads=toks(C["ident"]), writes=toks(identR))
    o_acc = P.sb([128, NT, 128], name="o_acc", nreg=NT)
    gstat = P.sb([128, 4, NT * 2], name="gstat")
    ccnt = [0]

    def load_chunk_consts(e_idx):
        i = ccnt[0] % 2
        ccnt[0] += 1
        dma("sp", ccb[i][:], Dr["colpack"][e_idx], [], ccb[i], "cc%d" % i)
        dma("sp", w2c[i][:], Dr["w2cat"][:, e_idx * 128:(e_idx + 1) * 128], [], w2c[i], "w2c%d" % i)
        dma("sp", a2c[i][:], Dr["a2cat"][:, e_idx * 128:(e_idx + 1) * 128], [], a2c[i], "a2c%d" % i)
        return i

    def mm(out_ap, lhsT, rhs, rd, wr, start=True, stop=True):
        P.op("pe", lambda e: e.matmul(out_ap, lhsT, rhs, start=start, stop=stop), reads=toks(rd), writes=toks(wr))

    def X(eng, meth, *args, rd=(), wr=(), **kw):
        P.op(eng, lambda e: getattr(e, meth)(*args, **kw), reads=toks(rd), writes=toks(wr))

    def wkv_chunk(g, e_idx, ci, next_mm=None):
        off, lens, cond, Tg, nt, nblk = ginfo(g)
        cc, w2, a2 = ccb[ci], w2c[ci], a2c[ci]
        r_, k_, v_, sg_, kk_, b_, kd_, sw_ = (cf[n] for n in ("r", "k", "v", "sg", "kk", "b", "kd", "sw"))
        ksum, ta, tb = cf["ksum"], cf["ta"], cf["tb"]
        ident, ones_blk, i64x2, ones_c = C["ident"], C["ones_blk"], C["i64x2"], C["ones"]
        X("dve", "tensor_scalar", dcol[:, 0:4], cc[:, 0:4], -1.0, None, ALU.mult, rd=[cc], wr=[dcol])
        X("dve", "tensor_scalar", dcol[:, 4:5], cc[:, 5:6], -1.0, 1.0, ALU.mult, ALU.add, rd=[cc], wr=[dcol])
        X("dve", "tensor_scalar", dcol[:, 5:6], cc[:, 6:7], 0.5, None, ALU.mult, rd=[cc], wr=[dcol])
        X("dve", "tensor_scalar", kk_[:, 0:Tg], k_[:, 0:Tg], cc[:, 4:5], None, ALU.mult, rd=[k_, cc], wr=[kk_])
        X("dve", "tensor_tensor", ta[:, 0:Tg], kk_[:, 0:Tg], kk_[:, 0:Tg], ALU.mult, rd=[kk_], wr=[ta])
        for b in range(nblk):
            sl = slice(b * 512, (b + 1) * 512)
            mm(banks[b][:, :], ones_blk[:], ta[:, sl], [ones_blk, ta], banks[b])
            X("act", "activation", tb[:, sl], banks[b][:, :], AF.Ln, bias=tiny_col[:], rd=[banks[b], tiny_col], wr=[tb])
        X("act", "activation", tb[:, 0:Tg], tb[:, 0:Tg], AF.Exp, scale=-0.5, rd=[tb], wr=[tb])
        X("dve", "tensor_tensor", kk_[:, 0:Tg], kk_[:, 0:Tg], tb[:, 0:Tg], ALU.mult, rd=[kk_, tb], wr=[kk_])

        seq_of_tile = []
        for si, L in enumerate(lens):
            seq_of_tile += [si] * (L // 128)
        first_tile, last_tile = {}, {}
        for ti, si in enumerate(seq_of_tile):
            first_tile.setdefault(si, ti)
            last_tile[si] = ti

        def h3(ap):
            return ap.rearrange("p (h j) -> p h j", h=2)

        def sig_fm(d, wt_, src, ncol_idx, dst):
            rows = slice(d * 64, (d + 1) * 64)
            bb = 0 if d == 0 else 2
            for b in range(nblk):
                sl = slice(b * 512, (b + 1) * 512)
                bk = banks[bb + b]
                mm(bk[:, :], wt_[rows, :], src[rows, sl], [wt_, src], bk)
                X("act", "activation", dst[:, sl], bk[:, :], AF.Exp, bias=dcol[:, ncol_idx:ncol_idx + 1], scale=-1.0, rd=[bk, dcol], wr=[dst])
            X("act", "activation", dst[:, 0:Tg], dst[:, 0:Tg], AF.Ln, bias=one_col[:], rd=[dst, one_col], wr=[dst])
            X("act", "activation", dst[:, 0:Tg], dst[:, 0:Tg], AF.Exp, scale=-1.0, rd=[dst], wr=[dst])

        def bkd_from_a(d):
            X("dve", "tensor_tensor", b_[:, 0:Tg], kk_[:, 0:Tg], ta[:, 0:Tg], ALU.mult, rd=[kk_, ta], wr=[b_])
            X("dve", "tensor_scalar", ta[:, 0:Tg], ta[:, 0:Tg], cc[:, 5:6], dcol[:, 4:5], ALU.mult, ALU.add, rd=[ta, cc, dcol], wr=[ta])
            X("dve", "tensor_tensor", kd_[:, 0:Tg], k_[:, 0:Tg], ta[:, 0:Tg], ALU.mult, rd=[k_, ta], wr=[kd_])
            if d == 0:
                X("pool", "tensor_copy", ksum[:, 0:Tg], kd_[:, 0:Tg], rd=[kd_], wr=[ksum])
            else:
                X("pool", "tensor_tensor", ksum[:, 0:Tg], ksum[:, 0:Tg], kd_[:, 0:Tg], ALU.add, rd=[kd_, ksum], wr=[ksum])

        for d in range(2):
            if d == 0:
                sig_fm(0, a2, la_fm, 2, ta)
                bkd_from_a(0)
                sig_fm(0, w2, lw_fm, 0, sw_)
                sig_fm(1, a2, la_fm, 3, ta)
                sig_fm(1, w2, lw_fm, 1, tb)
                sw_src = sw_
            else:
                bkd_from_a(1)
                sw_src = tb
            cm = C["cmF"] if d == 0 else C["cmB"]
            mEx = Vw(cm[:, 128:256], cm.toks)
            cmo = C["cmB"] if d == 0 else C["cmF"]
            mEd = Vw(cmo[:, 128:256], cmo.toks)
            mask = C["maskF"] if d == 0 else C["maskB"]
            pcc = 127 if d == 0 else 0
            tiles = list(range(nt)) if d == 0 else list(range(nt - 1, -1, -1))

            def head_pieces(ti, ui_):
                pb = ui_ % 2
                TM, BK, QRP, Zt, AM, Kd = TM_2[ui_ % 3], BK_2[pb], QRP_2[pb], Zt_2[pb], AM_2[pb], Kd_2[pb]
                tsl = slice(ti * 128, (ti + 1) * 128)
                cur = YPT[0]
                pcs = []

                def p0():
                    for j, src in enumerate((v_, kk_, b_, kd_)):
                        X("pe", "transpose", banks[4][:, j * 128:(j + 1) * 128], src[:, tsl], ident[:], rd=[src, ident], wr=[banks[4]])
                    X("pe", "transpose", banks[5][:, 0:128], sw_src[:, tsl], ident[:], rd=[sw_src, ident], wr=[banks[5]])
                    X("act", "copy", TM[:, 0:4, :].rearrange("p a b -> p (a b)"), banks[4][:, :], rd=[banks[4]], wr=[TM])
                    X("act", "activation", LwT[:, :], banks[5][:, 0:128], AF.Identity, scale=-DEC_C, rd=[banks[5]], wr=[LwT])
                pcs.append(p0)

                def p1():
                    mm(banks[5][:, 128:384], LwT[:, :], cm[:, :], [LwT, cm], banks[5])
                    mm(banks[4][:, 0:128], mEx[:, :], LwT[:, :], [LwT, mEx], banks[4])
                    mm(banks[4][:, 128:256], mEd[:, :], LwT[:, :], [LwT, mEd], banks[4])
                    X("act", "activation", EF[:, 0:256], banks[5][:, 128:384], AF.Exp, rd=[banks[5]], wr=[EF])
                    X("act", "activation", EF[:, 256:384], banks[5][:, 128:256], AF.Exp, scale=-1.0, rd=[banks[5]], wr=[EF])
                    X("act", "activation", ET[:, :], banks[4][:, 0:256], AF.Exp, rd=[banks[4]], wr=[ET])
                pcs.append(p1)

                def p2():
                    X("dve", "tensor_tensor", BK[:, 0:128], b_[:, tsl], EF[:, 256:384], ALU.mult, rd=[b_, EF], wr=[BK])
                    X("dve", "tensor_tensor", BK[:, 128:256], kd_[:, tsl], EF[:, 256:384], ALU.mult, rd=[kd_, EF], wr=[BK])
                    X("dve", "tensor_tensor", QRP[:, 0:128], kk_[:, tsl], EF[:, 128:256], ALU.mult, rd=[kk_, EF], wr=[QRP])
                    X("dve", "tensor_tensor", QRP[:, 128:256], r_[:, tsl], EF[:, 0:128], ALU.mult, rd=[r_, EF], wr=[QRP])
                pcs.append(p2)

                def p3():
                    X("dve", "tensor_scalar", QRP[:, 256:320], i64x2[:, :], EF[:, pcc:pcc + 1], None, ALU.mult, rd=[i64x2, EF], wr=[QRP])
                    X("dve", "tensor_tensor", Zt[:, :, 0:64], h3(TM[:, 1, :].bitcast(F32)), h3(ET[:, 0:128]), ALU.mult, rd=[TM, ET], wr=[Zt])
                    X("dve", "scalar_tensor_tensor", AM[:, :, 384:448], h3(TM[:, 2, :].bitcast(F32)), -1.0, h3(ET[:, 128:256]), ALU.mult, ALU.mult, rd=[TM, ET], wr=[AM])
                    X("dve", "tensor_tensor", Kd[:, :], TM[:, 3, :].bitcast(F32), ET[:, 128:256], ALU.mult, rd=[TM, ET], wr=[Kd])
                pcs.append(p3)

                def pa(hh):
                    def f():
                        hr = slice(hh * 64, (hh + 1) * 64)
                        bk = banks[1 + 2 * hh]
                        bk2 = banks[0 + 2 * hh]
                        mm(bk[:, 0:128], BK[hr, 0:128], QRP[hr, 0:128], [BK, QRP], bk)
                        mm(bk[:, 384:512], QRP[hr, 0:128], BK[hr, 0:128], [BK, QRP], bk)
                        mm(bk[:, 128:256], BK[hr, 128:256], QRP[hr, 0:128], [BK, QRP], bk)
                        mm(bk[:, 256:384], BK[hr, 128:256], QRP[hr, 128:256], [BK, QRP], bk)
                        mm(bk2[:, 0:128], BK[hr, 0:128], QRP[hr, 128:256], [BK, QRP], bk2)
                    return f

                def pm(hh):
                    def f():
                        bk = banks[1 + 2 * hh]
                        bk2 = banks[0 + 2 * hh]
                        ct = [cur.toks[hh]]
                        X("dve", "tensor_tensor", cur[:, hh, 0:128], bk[:, 0:128], mask[:, 0:128], ALU.mult, rd=[bk, mask], wr=ct)
                        X("dve", "tensor_tensor", cur[:, hh, 256:384], bk[:, 384:512], mask[:, 384:512], ALU.mult, rd=[bk, mask], wr=ct)
                        X("pool", "tensor_tensor", cur[:, hh, 128:256], cur[:, hh, 0:128].bitcast(F32), ident[:, :], ALU.add, rd=ct + [ident], wr=ct)
                        X("dve", "tensor_tensor", AM[:, hh, 0:256], bk[:, 128:384], mask[:, 128:384], ALU.mult, rd=[bk, mask], wr=AM.r(hh))
                        X("dve", "tensor_tensor", AM[:, hh, 256:384], bk2[:, 0:128], mask[:, 512:640], ALU.mult, rd=[bk2, mask], wr=AM.r(hh))
                    return f
                pcs += [pa(0), pa(1), pm(0), pm(1)]
                return pcs

            NLEV = 7

            def doubling_level(lev):
                cur, nxt = (YPT[0], YPT[1]) if lev % 2 == 0 else (YPT[1], YPT[0])
                for hh in range(2):
                    bk = banks[6 + hh]
                    ct = [cur.toks[hh]]
                    Y, Pm, YT = cur[:, hh, 0:128], cur[:, hh, 128:256], cur[:, hh, 256:384]
                    if lev == 0:
                        mm(bk[:, 256:384], Y, YT, ct, bk)
                        mm(bk[:, 0:128], YT, Y, ct, bk)
                    elif lev <= NLEV - 3:
                        mm(bk[:, 256:384], Y, YT, ct, bk)
                        mm(bk[:, 0:256], YT, cur[:, hh, 0:256], ct, bk)
                    elif lev == NLEV - 2:
                        mm(bk[:, 256:384], Y, YT, ct, bk)
                        mm(bk[:, 128:256], YT, Pm, ct, bk)
                    else:
                        mm(bk[:, 128:256], YT, Pm, ct, bk)
                for hh in range(2):
                    bk = banks[6 + hh]
                    ct = [cur.toks[hh]]
                    nt_ = [nxt.toks[hh]]
                    if lev == 0:
                        X("act", "copy", nxt[:, hh, 0:128], bk[:, 0:128], rd=[bk], wr=nt_)
                        X("act", "copy", nxt[:, hh, 256:384], bk[:, 256:384], rd=[bk], wr=nt_)
                        X("pool", "tensor_copy", nxt[:, hh, 128:256], cur[:, hh, 128:256].bitcast(F32), rd=ct, wr=nt_)
                    elif lev <= NLEV - 3:
                        X("act", "copy", nxt[:, hh, :], bk[:, 0:384], rd=[bk], wr=nt_)
                        X("dve", "tensor_tensor", nxt[:, hh, 128:256], nxt[:, hh, 128:256].bitcast(F32), cur[:, hh, 128:256].bitcast(F32), ALU.add, rd=ct + nt_, wr=nt_)
                    elif lev == NLEV - 2:
                        X("act", "copy", nxt[:, hh, 128:384], bk[:, 128:384], rd=[bk], wr=nt_)
                        X("dve", "tensor_tensor", nxt[:, hh, 128:256], nxt[:, hh, 128:256].bitcast(F32), cur[:, hh, 128:256].bitcast(F32), ALU.add, rd=ct + nt_, wr=nt_)
                    else:
                        X("dve", "tensor_tensor", TinvR[:, hh, :], bk[:, 128:256], cur[:, hh, 128:256].bitcast(F32), ALU.add, rd=[bk] + ct, wr=TinvR.r(hh))

            def tail_pieces(ti, ui_):
                pb = ui_ % 2
                TM, BK, QRP, Zt, AM, Kd = TM_2[ui_ % 3], BK_2[pb], QRP_2[pb], Zt_2[pb], AM_2[pb], Kd_2[pb]
                si = seq_of_tile[ti]
                seq_start = (ti == first_tile[si]) if d == 0 else (ti == last_tile[si])
                seq_end = (ti == last_tile[si]) if d == 0 else (ti == first_tile[si])
                tb_ = [banks[1], banks[3]]

                def t0():
                    if seq_start:
                        if g == "S":
                            dma("pool", ST[d][:], Dr["st0"][d, 2 * e_idx:2 * e_idx + 2].rearrange("h j i -> j h i"), [], ST[d], "st%d" % d)
                        else:
                            X("dve", "tensor_scalar", ST[d][:], ones_c[0:64, 0:1].unsqueeze(2).to_broadcast([64, 2, 64]), 0.0, None, ALU.mult, rd=[ones_c], wr=[ST[d]])
                    for hh in range(2):
                        bk = tb_[hh]
                        mm(bk[:, 0:64], AM[:, hh, 0:128], TM[:, 0, hh * 64:(hh + 1) * 64], AM.r(hh) + [TM], bk)
                        X("act", "copy", Zt[:, hh, 64:128], bk[:, 0:64], rd=[bk], wr=[Zt])

                def t0b():
                    for hh in range(2):
                        bk = tb_[hh]
                        mm(bk[:, 64:192], TinvR[:, hh, :], Zt[:, hh, :], TinvR.r(hh) + [Zt], bk)
                        X("act", "copy", WU[:, hh, :], bk[:, 64:192], rd=[bk], wr=WU.r(hh))

                def t1():
                    for hh in range(2):
                        hr = slice(hh * 64, (hh + 1) * 64)
                        bk = tb_[hh]
                        mm(bk[0:64, 192:384], WU[:, hh, 0:64], AM[:, hh, 256:448], AM.r(hh) + WU.r(hh), bk, start=True, stop=False)
                        mm(bk[0:64, 192:384], identR[hr, hr], QRP[hr, 128:320], [identR, QRP], bk, start=False, stop=True)
                        X("act", "copy", QMs[:, hh, :], bk[0:64, 192:384], rd=[bk], wr=QMs.r(hh))

                def t2():
                    for hh in range(2):
                        bk = tb_[hh]
                        vv = TM[:, 0, hh * 64:(hh + 1) * 64]
                        nu0 = WU[:, hh, 64:128]
                        mm(bk[:, 384:448], AM[:, hh, 256:384], nu0, AM.r(hh) + WU.r(hh), bk, start=True, stop=False)
                        mm(bk[:, 384:448], AM[:, hh, 128:256], vv, AM.r(hh) + [TM], bk, start=False, stop=False)
                        mm(bk[:, 384:448], QMs[:, hh, 0:128], ST[d][:, hh, :], QMs.r(hh) + [ST[d]], bk, start=False, stop=True)
                        mm(bk[0:64, 448:512], AM[:, hh, 384:448], nu0, AM.r(hh) + WU.r(hh), bk, start=True, stop=False)
                        mm(bk[0:64, 448:512], Kd[:, hh * 64:(hh + 1) * 64], vv, [Kd, TM], bk, start=False, stop=False)
                        mm(bk[0:64, 448:512], QMs[:, hh, 128:192], ST[d][:, hh, :], QMs.r(hh) + [ST[d]], bk, start=False, stop=True)
                        if d == 0:
                            X("act", "copy", o_acc[:, ti, hh * 64:(hh + 1) * 64], bk[:, 384:448], rd=[bk], wr=o_acc.r(ti))
                        else:
                            X("dve", "tensor_tensor", o_acc[:, ti, hh * 64:(hh + 1) * 64], o_acc[:, ti, hh * 64:(hh + 1) * 64], bk[:, 384:448], ALU.add,
                              rd=[bk] + o_acc.r(ti), wr=o_acc.r(ti))
                        X("act", "copy", ST[d][:, hh, :], bk[0:64, 448:512], rd=[bk], wr=[ST[d]])
                    if seq_end and g == "P":
                        o = dma("sp", st_out[si, d, 2 * e_idx:2 * e_idx + 2].rearrange("h j i -> j h i"), ST[d][:].bitcast(F32), ST[d], [], "sto%d" % d)
                        out_ops.append(o)
                return [t0, t0b, t1, t2]

            for pc in head_pieces(tiles[0], 0):
                pc()
            prev_tail = []
            for ui, ti in enumerate(tiles):
                nxt_pcs = head_pieces(tiles[ui + 1], ui + 1) if ui + 1 < len(tiles) else []
                for lev in range(NLEV):
                    doubling_level(lev)
                    if lev < 4 and prev_tail:
                        prev_tail[lev]()
                    if nxt_pcs:
                        if lev == 1:
                            nxt_pcs[0]()
                        elif lev == 2:
                            nxt_pcs[1]()
                        elif lev == 3:
                            nxt_pcs[2]()
                        elif lev == 4:
                            nxt_pcs[3]()
                        elif lev == 5:
                            nxt_pcs[4]()
                            nxt_pcs[5]()
                if nxt_pcs:
                    nxt_pcs[6]()
                    nxt_pcs[7]()
                prev_tail = tail_pieces(ti, ui)
            for pc in prev_tail:
                pc()

        n2 = nt * 2
        o3 = o_acc[:, 0:nt, :].rearrange("p t (h i) -> p (t h) i", h=2)
        X("dve", "tensor_reduce", gstat[:, 0, 0:n2], o3, AX.X, ALU.add, rd=[o_acc], wr=[gstat])
        sq3 = ta[:, 0:nt * 128].rearrange("p (a i) -> p a i", i=64)
        X("dve", "tensor_tensor", sq3, o3, o3, ALU.mult, rd=[o_acc], wr=[ta])
        X("dve", "tensor_reduce", gstat[:, 1, 0:n2], sq3, AX.X, ALU.add, rd=[ta], wr=[gstat])
        X("dve", "tensor_scalar", gstat[:, 0, 0:n2], gstat[:, 0, 0:n2], 1.0 / 64, None, ALU.mult, rd=[gstat], wr=[gstat])
        X("dve", "tensor_tensor", gstat[:, 2, 0:n2], gstat[:, 0, 0:n2], gstat[:, 0, 0:n2], ALU.mult, rd=[gstat], wr=[gstat])
        X("dve", "scalar_tensor_tensor", gstat[:, 1, 0:n2], gstat[:, 1, 0:n2], 1.0 / 64, gstat[:, 2, 0:n2], ALU.mult, ALU.subtract, rd=[gstat], wr=[gstat])
        X("act", "activation", gstat[:, 3, 0:n2], gstat[:, 1, 0:n2], AF.Ln, bias=eps_cols["gn"][:], rd=[gstat, eps_cols["gn"]], wr=[gstat])
        X("act", "activation", gstat[:, 3, 0:n2], gstat[:, 3, 0:n2], AF.Exp, scale=-0.5, rd=[gstat], wr=[gstat])
        X("dve", "tensor_tensor", o3, o3, gstat[:, 0, 0:n2].unsqueeze(2).to_broadcast([128, n2, 64]), ALU.subtract, rd=[o_acc, gstat], wr=[o_acc])
        X("dve", "tensor_tensor", o3, o3, gstat[:, 3, 0:n2].unsqueeze(2).to_broadcast([128, n2, 64]), ALU.mult, rd=[o_acc, gstat], wr=[o_acc])
        if next_mm is not None:
            next_mm()
        X("dve", "tensor_tensor", ta[:, 0:Tg], r_[:, 0:Tg], ksum[:, 0:Tg], ALU.mult, rd=[r_, ksum], wr=[ta])
        X("dve", "tensor_scalar", ta[:, 0:Tg], ta[:, 0:Tg], dcol[:, 5:6], None, ALU.mult, rd=[ta, dcol], wr=[ta])
        for b in range(nblk):
            sl = slice(b * 512, (b + 1) * 512)
            mm(banks[b][:, :], ones_blk[:], ta[:, sl], [ones_blk, ta], banks[b])
            X("dve", "tensor_tensor", tb[:, sl], banks[b][:, :], v_[:, sl], ALU.mult, rd=[banks[b], v_], wr=[tb])
        X("dve", "tensor_scalar", tb[:, 0:Tg], tb[:, 0:Tg], cc[:, 8:9], None, ALU.add, rd=[tb, cc], wr=[tb])
        for b in range(nblk):
            bk = banks[2 + b]
            for q in range(4):
                ti = b * 4 + q
                X("pe", "transpose", bk[:, q * 128:(q + 1) * 128], o_acc[:, ti, :], ident[:], rd=o_acc.r(ti) + [ident], wr=[bk])
            sl = slice(b * 512, (b + 1) * 512)
            X("dve", "scalar_tensor_tensor", ta[:, sl], bk[:, :], cc[:, 7:8], tb[:, sl], ALU.mult, ALU.add, rd=[bk, cc, tb], wr=[ta])
            X("dve", "tensor_tensor", out_fm[:, e_idx, sl], ta[:, sl], sg_[:, sl], ALU.mult, rd=[ta, sg_], wr=out_fm.r(e_idx))


    ss3 = P.sb([128, 2], name="ss3")
    rs3 = P.sb([128, 1], name="rs3")
    dgt = P.sb([128, 128], name="dgt")

    wout_prefetched = [False]

    def prefetch_wout(wout_ap):
        dma("pool", h_fm[:, :, :], wout_ap.rearrange("(kc p) e -> p kc e", p=128), [], h_fm, "wout")
        wout_prefetched[0] = True

    def phase3(l, g, wout_ap, reload_x, final):
        off, lens, cond, Tg, nt, nblk = ginfo(g)
        ta, tb = cf["ta"], cf["tb"]
        ident, ones = C["ident"], C["ones"]
        if not wout_prefetched[0]:
            prefetch_wout(wout_ap)
        wout_prefetched[0] = False
        for kc in range(KC):
            X("dve", "tensor_scalar", dgt[:, :], ident[:, :], gate_col[:, l, kc, cond:cond + 1], None, ALU.mult, rd=[ident, gate_col], wr=[dgt])
            bk = banks[4 + kc // 4]
            mm(bk[:, (kc % 4) * 128:(kc % 4 + 1) * 128], ones[:, :], dgt[:, :], [ones, dgt], bk)
            if kc % 4 == 3:
                X("act", "copy", ta[:, (kc // 4) * 512:(kc // 4 + 1) * 512], bk[:, :], rd=[bk], wr=[ta])
        for ti in range(nt):
            b0 = 0 if ti % 2 == 0 else 2
            tsl = slice(ti * 128, (ti + 1) * 128)
            if reload_x:
                dma("sp", xg[:, ti, :], Dr["xin"][off + ti * 128: off + (ti + 1) * 128, :], [], xg.r(ti), "xg%d" % ti)
            for hb in range(2):
                bk = banks[b0 + hb]
                for kc in range(KC):
                    mm(bk[:, :], out_fm[:, kc, tsl], h_fm[:, kc, hb * 512:(hb + 1) * 512], out_fm.r(kc) + toks(h_fm), bk,
                       start=(kc == 0), stop=(kc == KC - 1))
                X("act", "activation", tb[:, hb * 512:(hb + 1) * 512], bk[:, :], AF.Square, accum_out=ss3[:, hb:hb + 1], rd=[bk], wr=[tb, ss3])
            X("dve", "tensor_tensor", rs3[:, :], ss3[:, 0:1], ss3[:, 1:2], ALU.add, rd=[ss3], wr=[rs3])
            X("act", "activation", rs3[:, :], rs3[:, :], AF.Ln, bias=eps_cols["rms"][:], scale=1.0 / D, rd=[rs3, eps_cols["rms"]], wr=[rs3])
            X("act", "activation", rs3[:, :], rs3[:, :], AF.Exp, scale=-0.5, rd=[rs3], wr=[rs3])
            for hb in range(2):
                bk = banks[b0 + hb]
                sl = slice(hb * 512, (hb + 1) * 512)
                X("dve", "scalar_tensor_tensor", tb[:, sl], bk[:, :], rs3[:, 0:1], ta[:, sl], ALU.mult, ALU.mult, rd=[bk, rs3, ta], wr=[tb])
            X("pool", "tensor_tensor", xg[:, ti, :], xg[:, ti, :], tb[:, 0:D], ALU.add, rd=xg.r(ti) + [tb], wr=xg.r(ti))
            if final:
                o = dma("sp", y_out[off + ti * 128: off + (ti + 1) * 128, :], xg[:, ti, :], xg.r(ti), [], "yo%d" % ti)
                out_ops.append(o)

    def layer0(g):
        phase1(0, g, True)
        lora_stage(g)
        load_chunk_weights(0)
        ci = load_chunk_consts(0)
        proj_stage(g, 0)
        for e_idx in range(KC):
            if e_idx + 1 < KC:
                load_chunk_weights(e_idx + 1)
                ci_next = load_chunk_consts(e_idx + 1)
                wkv_chunk(g, e_idx, ci, (lambda: project_mm(g, 0, 1)))
                proj_stage(g, e_idx + 1, pre_mm0=True)
            else:
                prefetch_wout(Dr["w_out"])
                wkv_chunk(g, e_idx, ci)
            ci = ci_next
        phase3(0, g, Dr["w_out"], True, False)


    def load_w_plain(slot, src_ap):
        wt = wring[slot]
        dma("pool", wt[:], src_ap.rearrange("(kc p) e -> p kc e", p=128), [], wt, "w%d" % slot)

    def project_plain(g, slot, bank_base):
        off, lens, cond, Tg, nt, nblk = ginfo(g)
        wt = wring[slot]
        for b in range(nblk):
            bk = banks[bank_base + b]
            for kc in range(KC):
                mm(bk[:, :], wt[:, kc, :], h_fm[:, kc, b * 512:(b + 1) * 512], [wt] + h_fm.r(kc), bk, start=(kc == 0), stop=(kc == KC - 1))

    def layer1(g, from_dram=False):
        off, lens, cond, Tg, nt, nblk = ginfo(g)
        ta, tb, ksum = cf["ta"], cf["tb"], cf["ksum"]
        ident, ones = C["ident"], C["ones"]
        phase1(1, g, from_dram)
        seglen = 64 if g == "S" else 256
        nseg = Tg // seglen
        segw = seglen + CK - 1
        spb = 512 // seglen
        zpad_w = Vw(zpad_t[:, 0:nseg * segw], zpad_t.toks)
        zpad3 = zpad_w[:, :].rearrange("p (s w) -> p s w", w=segw)
        X("dve", "tensor_scalar", zpad_w[:, :], ones[:, 0:1].to_broadcast([128, nseg * segw]), 0.0, None, ALU.mult, rd=[ones], wr=[zpad_w])
        dgflat = wpall[:, :, :, :].rearrange("p a b c -> p (a b c)")[:, 0:CK * 128]
        dg = Vw(dgflat.rearrange("p (k c) -> p k c", c=128), wpall.toks)
        zc = Vw(out_fm[:, :, :].bitcast(F32), out_fm.toks)
        cw = Dr["cv_w_in"]

        def load_conv_chunk(e_idx):
            sp_ = (e_idx % 2) * 2
            load_w_plain(sp_, cw[:, e_idx * 128:(e_idx + 1) * 128])
            load_w_plain(sp_ + 1, cw[:, D + e_idx * 128:D + (e_idx + 1) * 128])

        load_conv_chunk(0)
        for e_idx in range(KC):
            sp_ = (e_idx % 2) * 2
            project_plain(g, sp_, 0)
            project_plain(g, sp_ + 1, 2)
            if e_idx + 1 < KC:
                load_conv_chunk(e_idx + 1)
            X("dve", "tensor_tensor", dg[:, :, :].bitcast(F32R), ident[:, :].unsqueeze(1).to_broadcast([128, CK, 128]),
              cvdw[:, e_idx, :].unsqueeze(2).to_broadcast([128, CK, 128]), ALU.mult, rd=[ident, cvdw], wr=[dg])
            for b in range(nblk):
                sl = slice(b * 512, (b + 1) * 512)
                bv, bg = banks[b], banks[2 + b]
                X("act", "activation", ta[:, sl], bg[:, :], AF.Exp, scale=-1.0, rd=[bg], wr=[ta])
                X("act", "activation", ta[:, sl], ta[:, sl], AF.Ln, bias=one_col[:], rd=[ta, one_col], wr=[ta])
                X("act", "activation", ta[:, sl], ta[:, sl], AF.Exp, scale=-1.0, rd=[ta], wr=[ta])
                X("dve", "tensor_tensor", zpad3[:, b * spb:(b + 1) * spb, CK // 2:CK // 2 + seglen],
                  bv[:, :].rearrange("p (s l) -> p s l", l=seglen), ta[:, sl].rearrange("p (s l) -> p s l", l=seglen), ALU.mult,
                  rd=[bv, ta], wr=[zpad_w])
            for b in range(nblk):
                sl = slice(b * 512, (b + 1) * 512)
                bk = banks[4 + b]
                for k in range(CK):
                    mm(bk[:, :].rearrange("p (s l) -> p s l", l=seglen), dg[:, k, :].bitcast(F32R), zpad3[:, b * spb:(b + 1) * spb, k:k + seglen],
                       [dg, zpad_w], bk, start=(k == 0), stop=(k == CK - 1))
                X("act", "activation", out_fm[:, e_idx, sl], bk[:, :], AF.Identity, bias=cvcols[:, e_idx, 0:1], rd=[bk, cvcols], wr=out_fm.r(e_idx))
        for b in range(nblk):
            sl = slice(b * 512, (b + 1) * 512)
            for kc in range(KC):
                mm(banks[0][:, :], ones[:, :], zc[:, kc, sl], [ones] + out_fm.r(kc), banks[0], start=(kc == 0), stop=(kc == KC - 1))
            for kc in range(KC):
                hs = slice((kc % 2) * 512, (kc % 2 + 1) * 512)
                X("act", "activation", ta[:, hs], zc[:, kc, sl], AF.Square, rd=out_fm.r(kc), wr=[ta])
                mm(banks[1][:, :], ones[:, :], ta[:, hs], [ones, ta], banks[1], start=(kc == 0), stop=(kc == KC - 1))
            X("act", "activation", lw_fm[:, sl], banks[0][:, :], AF.Identity, scale=1.0 / D, rd=[banks[0]], wr=[lw_fm])
            X("dve", "tensor_tensor", tb[:, sl], lw_fm[:, sl], lw_fm[:, sl], ALU.mult, rd=[lw_fm], wr=[tb])
            X("dve", "scalar_tensor_tensor", tb[:, sl], banks[1][:, :], 1.0 / D, tb[:, sl], ALU.mult, ALU.subtract, rd=[banks[1], tb], wr=[tb])
            X("act", "activation", tb[:, sl], tb[:, sl], AF.Ln, bias=eps_cols["ln"][:], rd=[tb, eps_cols["ln"]], wr=[tb])
            X("act", "activation", la_fm[:, sl], tb[:, sl], AF.Exp, scale=-0.5, rd=[tb], wr=[la_fm])
        load_w_plain(0, cw[:, 2 * D:2 * D + 128])
        for e_idx in range(KC):
            slot = e_idx % 4
            if e_idx + 1 < KC:
                load_w_plain((e_idx + 1) % 4, cw[:, 2 * D + (e_idx + 1) * 128:2 * D + (e_idx + 2) * 128])
            project_plain(g, slot, 2)
            if e_idx == KC - 1:
                prefetch_wout(Dr["cv_w_out"])
            for b in range(nblk):
                sl = slice(b * 512, (b + 1) * 512)
                bk = banks[2 + b]
                X("act", "activation", ksum[:, sl], bk[:, :], AF.Exp, scale=-1.0, rd=[bk], wr=[ksum])
                X("act", "activation", ksum[:, sl], ksum[:, sl], AF.Ln, bias=one_col[:], rd=[ksum, one_col], wr=[ksum])
                X("act", "activation", ksum[:, sl], ksum[:, sl], AF.Exp, scale=-1.0, rd=[ksum], wr=[ksum])
                X("dve", "tensor_tensor", ksum[:, sl], ksum[:, sl], bk[:, :], ALU.mult, rd=[ksum, bk], wr=[ksum])
            X("dve", "tensor_tensor", ta[:, 0:Tg], zc[:, e_idx, 0:Tg], lw_fm[:, 0:Tg], ALU.subtract, rd=out_fm.r(e_idx) + [lw_fm], wr=[ta])
            X("dve", "tensor_tensor", ta[:, 0:Tg], ta[:, 0:Tg], la_fm[:, 0:Tg], ALU.mult, rd=[ta, la_fm], wr=[ta])
            X("act", "activation", tb[:, 0:Tg], ta[:, 0:Tg], AF.Identity, bias=cvcols[:, e_idx, 2:3], scale=cvcols[:, e_idx, 1:2], rd=[ta, cvcols], wr=[tb])
            X("act", "activation", ta[:, 0:Tg], tb[:, 0:Tg], AF.Exp, scale=-1.0, rd=[tb], wr=[ta])
            X("act", "activation", ta[:, 0:Tg], ta[:, 0:Tg], AF.Ln, bias=one_col[:], rd=[ta, one_col], wr=[ta])
            X("act", "activation", ta[:, 0:Tg], ta[:, 0:Tg], AF.Exp, scale=-1.0, rd=[ta], wr=[ta])
            X("dve", "tensor_tensor", tb[:, 0:Tg], tb[:, 0:Tg], ta[:, 0:Tg], ALU.mult, rd=[ta, tb], wr=[tb])
            X("dve", "tensor_tensor", out_fm[:, e_idx, 0:Tg], tb[:, 0:Tg], ksum[:, 0:Tg], ALU.mult, rd=[tb, ksum], wr=out_fm.r(e_idx))
        phase3(1, g, Dr["cv_w_out"], False, True)

    if stage == "full":
        for g in ("S", "P"):
            layer0(g)
            layer1(g)
        P.emit(out_ops)
        print("ops", len(P.ops), "sems", P.n_sems, "eng_cnt", P.eng_cnt, "sb_bytes", P.sb_bytes, flush=True)
        P.close()
        return nc
    if stage == "l1":
        import os
        g = os.environ.get("WKV_G", "P")
        layer1(g, True)
        P.emit(out_ops)
        P.close()
        return nc
    if stage == "l0":
        import os
        g = os.environ.get("WKV_G", "P")
        layer0(g)
        nt = ginfo(g)[4]
        for ti in range(nt):
            dbg_dump(xg[:, ti, :], xg.r(ti), ti * D, D)
        P.emit(out_ops)
        print("ops", len(P.ops), "sems", P.n_sems, "eng_cnt", P.eng_cnt, "sb_bytes", P.sb_bytes)
        P.close()
        return nc
    if stage == "wkv":
        import os
        g = os.environ.get("WKV_G", "P")
        e_idx = 3
        phase1(0, g, True)
        lora_stage(g)
        load_chunk_weights(e_idx)
        ci = load_chunk_consts(e_idx)
        proj_stage(g, e_idx)
        wkv_chunk(g, e_idx, ci)
        Tg = ginfo(g)[3]
        dbg_dump(out_fm[:, e_idx, 0:Tg].bitcast(F32), out_fm, 0, Tg)
        dbg_dump(o_acc[:, 0:Tg // 128, :].rearrange("p t c -> p (t c)"), o_acc, Tg, Tg)
        P.emit(out_ops)
        print("ops", len(P.ops), "sems", P.n_sems, "eng_cnt", P.eng_cnt, "sb_bytes", P.sb_bytes)
        P.close()
        return nc
    if stage == "proj":
        import os
        cut = int(os.environ.get("PROJ_CUT", "9"))
        g = "P"
        phase1(0, g, True)
        if cut >= 2:
            load_w(0, Dr["w1cat"], 4)
        if cut >= 3:
            load_w(1, Dr["a1cat"], 5)
            project(g, 0, lw_fm)
        if cut >= 4:
            lora_stage(g)
        if cut >= 5:
            load_chunk_weights(3)
            proj_stage(g, 3)
        Tg = ginfo(g)[3]
        col = 0
        for t in (lw_fm, la_fm, cf["r"], cf["k"], cf["v"], cf["sg"]):
            dbg_dump(t[:, 0:Tg], t, col, Tg)
            col += Tg
        dbg_dump(h_fm[:, 5, 0:Tg].bitcast(F32), h_fm, col, Tg)
        P.emit(out_ops)
        P.close()
        return nc
    return nc


_NC_CACHE = {}


def kernel(**inputs):
    maps = prep_inputs(inputs)
    if "nc" not in _NC_CACHE:
        _NC_CACHE["nc"] = build(stage="full")
    nc = _NC_CACHE["nc"]
    res = run_bass_kernel_spmd(nc, maps, core_ids=list(range(NCORES)))
    y_prompt = np.zeros((16, 256, D), np.float32)
    y_sample = np.zeros((8, 1024, D), np.float32)
    new_state = np.zeros((16, 1, 2, NH, HD, HD), np.float32)
    for i in range(NCORES):
        r = res.results[i]
        yo = np.asarray(r["y_out"])
        y_prompt[2 * i] = yo[0:256]
        y_prompt[2 * i + 1] = yo[256:512]
        y_sample[i] = yo[512:1536]
        st = np.asarray(r["st_out"])
        for si in range(2):
            new_state[2 * i + si, 0] = st[si].transpose(0, 1, 3, 2)
    return (y_prompt, y_sample, new_state)
```
